# Optimizing a Trainium2 kernel written in Bass

```python
import jax, jax.numpy as jnp
from jax import lax
import numpy as np


D_MODEL = 1024
BATCH = 4
SEQ = 8192
DEPTH = 4

N_MIXERS = 4
N_HEADS = 16
HEAD_DIM = D_MODEL // N_HEADS
HD = N_HEADS * HEAD_DIM
FFN_DIM = 2816
ROPE_THETA = 10000.0
Q_BLOCK = 128
RMS_EPS = 1e-6
NEG_INF = -1e30
NSA_KV_HEADS = 4
NSA_CMP_LEN = 32
NSA_CMP_STRIDE = 16
NSA_SEL_LEN = 64
NSA_N_SEL = 16
NSA_WINDOW = 512
NSA_FORCED_SCORE = 1e9
GLA_HEADS = 4
GLA_KEY_DIM = D_MODEL // 2
GLA_VALUE_DIM = D_MODEL
GLA_GATE_RANK = 16
GLA_TAU = 16.0
GLA_CHUNK = 64

kernel_name = 'hybrid_fox_nsa_gla_stickbreak_macaron'


def rms_norm(x, g):
    xf = x.astype(jnp.float32)
    y = xf * lax.rsqrt(jnp.mean(xf * xf, axis=-1, keepdims=True) + RMS_EPS)
    return (y * g.astype(jnp.float32)).astype(x.dtype)


def rope_tables(seq, dim):
    inv = ROPE_THETA ** (-jnp.arange(0, dim, 2, dtype=jnp.float32) / dim)
    ang = jnp.arange(seq, dtype=jnp.float32)[:, None] * inv[None, :]
    return jnp.cos(ang), jnp.sin(ang)


def apply_rope(x, cos, sin):
    x1, x2 = jnp.split(x, 2, axis=-1)
    c, s = cos[:, None, :], sin[:, None, :]
    return jnp.concatenate([x1 * c - x2 * s, x2 * c + x1 * s], axis=-1).astype(x.dtype)


def swiglu(x, w_gate, w_up, w_down):
    return (jax.nn.silu(x @ w_gate) * (x @ w_up)) @ w_down


def fox_mixer(h, w_in, b_f, g_q, g_k, w_out):
    B, S, _ = h.shape
    H, dh = N_HEADS, HEAD_DIM
    q, k, v, f, og = jnp.split(h @ w_in, np.cumsum([HD, HD, HD, H]).tolist(), axis=-1)
    q = rms_norm(q.reshape(B, S, H, dh), g_q).transpose(0, 2, 1, 3)
    k = rms_norm(k.reshape(B, S, H, dh), g_k).transpose(0, 2, 1, 3)
    v = v.reshape(B, S, H, dh).transpose(0, 2, 1, 3)
    c = jnp.cumsum(jax.nn.log_sigmoid((f + b_f).astype(jnp.float32)), axis=1).transpose(0, 2, 1)
    nb = S // Q_BLOCK
    qb = q.reshape(B, H, nb, Q_BLOCK, dh).transpose(2, 0, 1, 3, 4)
    cb = c.reshape(B, H, nb, Q_BLOCK).transpose(2, 0, 1, 3)
    k_pos = jnp.arange(S)
    scale = dh ** -0.5

    def block(args):
        i, q_i, c_i = args
        q_pos = i * Q_BLOCK + jnp.arange(Q_BLOCK)
        s = jnp.einsum('bhqd,bhkd->bhqk', q_i, k).astype(jnp.float32) * scale + (c_i[..., None] - c[:, :, None, :])
        p = jax.nn.softmax(jnp.where(k_pos[None, :] <= q_pos[:, None], s, NEG_INF), axis=-1)
        return jnp.einsum('bhqk,bhkd->bhqd', p.astype(v.dtype), v)

    o = lax.map(block, (jnp.arange(nb), qb, cb))
    o = o.transpose(1, 0, 3, 2, 4).reshape(B, S, HD)
    return (o * jax.nn.sigmoid(og)) @ w_out


def nsa_mixer(h, w_in, cmp_pos_k, cmp_pos_v, w_cmp_k, w_cmp_v, g_q, g_k, w_out, cos, sin):
    B, S, _ = h.shape
    H, G, dh = N_HEADS, NSA_KV_HEADS, HEAD_DIM
    P = H // G
    kvw = G * dh
    q, kc, vc, ks, vs, kw, vw, gate = jnp.split(h @ w_in, np.cumsum([HD] + [kvw] * 6).tolist(), axis=-1)
    q = apply_rope(rms_norm(q.reshape(B, S, H, dh), g_q), cos, sin)
    q = q.reshape(B, S, G, P, dh).transpose(0, 2, 3, 1, 4)

    def keys(t):
        return apply_rope(rms_norm(t.reshape(B, S, G, dh), g_k), cos, sin).transpose(0, 2, 1, 3)

    def vals(t):
        return t.reshape(B, S, G, dh).transpose(0, 2, 1, 3)

    kc, ks, kw = keys(kc), keys(ks), keys(kw)
    vc, vs, vw = vals(vc), vals(vs), vals(vw)
    n_cmp = (S - NSA_CMP_LEN) // NSA_CMP_STRIDE + 1
    cmp_start = np.arange(n_cmp) * NSA_CMP_STRIDE
    cmp_idx = cmp_start[:, None] + np.arange(NSA_CMP_LEN)[None, :]
    kc = rms_norm(jnp.einsum('bgnld,lde->bgne', kc[:, :, cmp_idx] + cmp_pos_k, w_cmp_k), g_k)
    vc = jnp.einsum('bgnld,lde->bgne', vc[:, :, cmp_idx] + cmp_pos_v, w_cmp_v)
    cmp_end = jnp.asarray(cmp_start + NSA_CMP_LEN - 1)
    n_sel = S // NSA_SEL_LEN
    sel_start = np.arange(n_sel) * NSA_SEL_LEN
    cover = jnp.asarray(((cmp_start[:, None] < sel_start[None, :] + NSA_SEL_LEN)
                         & (cmp_start[:, None] + NSA_CMP_LEN > sel_start[None, :])).astype(np.float32))
    k_top = min(NSA_N_SEL, n_sel)
    ks_blk = ks.reshape(B, G, n_sel, NSA_SEL_LEN, dh)
    vs_blk = vs.reshape(B, G, n_sel, NSA_SEL_LEN, dh)
    pad = ((0, 0), (0, 0), (NSA_WINDOW, 0), (0, 0))
    kw_pad, vw_pad = jnp.pad(kw, pad), jnp.pad(vw, pad)
    nb = S // Q_BLOCK
    qb = q.reshape(B, G, P, nb, Q_BLOCK, dh).transpose(3, 0, 1, 2, 4, 5)
    b_idx = jnp.arange(B)[:, None, None, None]
    g_idx = jnp.arange(G)[None, :, None, None]
    blk = jnp.arange(n_sel)
    scale = dh ** -0.5

    def block(args):
        i, q_i = args
        q_pos = i * Q_BLOCK + jnp.arange(Q_BLOCK)
        s = jnp.einsum('bgpqd,bgnd->bgpqn', q_i, kc).astype(jnp.float32) * scale
        valid = cmp_end[None, :] <= q_pos[:, None]
        p_c = jax.nn.softmax(jnp.where(valid, s, NEG_INF), axis=-1) * valid
        o_c = jnp.einsum('bgpqn,bgnd->bgpqd', p_c.astype(vc.dtype), vc)
        imp = jnp.einsum('bgpqn,nj->bgqj', p_c, cover)
        cur = q_pos[:, None] // NSA_SEL_LEN
        forced = (blk[None, :] == 0) | (blk[None, :] == cur) | (blk[None, :] == cur - 1)
        imp = jnp.where(forced, NSA_FORCED_SCORE, jnp.where(blk[None, :] <= cur, imp, -1.0))
        _, sel = lax.top_k(imp, k_top)
        k_g = ks_blk[b_idx, g_idx, sel]
        v_g = vs_blk[b_idx, g_idx, sel]
        s = jnp.einsum('bgpqd,bgqkld->bgpqkl', q_i, k_g).astype(jnp.float32) * scale
        k_pos = sel[..., None] * NSA_SEL_LEN + jnp.arange(NSA_SEL_LEN)
        ok = (k_pos <= q_pos[:, None, None])[:, :, None]
        s = jnp.where(ok, s, NEG_INF).reshape(B, G, P, Q_BLOCK, k_top * NSA_SEL_LEN)
        p_s = jax.nn.softmax(s, axis=-1).reshape(B, G, P, Q_BLOCK, k_top, NSA_SEL_LEN)
        o_s = jnp.einsum('bgpqkl,bgqkld->bgpqd', p_s.astype(v_g.dtype), v_g)
        k_w = lax.dynamic_slice_in_dim(kw_pad, i * Q_BLOCK, NSA_WINDOW + Q_BLOCK, axis=2)
        v_w = lax.dynamic_slice_in_dim(vw_pad, i * Q_BLOCK, NSA_WINDOW + Q_BLOCK, axis=2)
        w_pos = i * Q_BLOCK - NSA_WINDOW + jnp.arange(NSA_WINDOW + Q_BLOCK)
        diff = q_pos[:, None] - w_pos[None, :]
        ok_w = (diff >= 0) & (diff < NSA_WINDOW) & (w_pos[None, :] >= 0)
        s = jnp.einsum('bgpqd,bgkd->bgpqk', q_i, k_w).astype(jnp.float32) * scale
        p_w = jax.nn.softmax(jnp.where(ok_w, s, NEG_INF), axis=-1)
        o_w = jnp.einsum('bgpqk,bgkd->bgpqd', p_w.astype(v_w.dtype), v_w)
        return jnp.stack([o_c, o_s, o_w], axis=-2)

    o = lax.map(block, (jnp.arange(nb), qb))
    o = o.transpose(1, 0, 4, 2, 3, 5, 6).reshape(B, S, H, 3, dh)
    g = jax.nn.sigmoid(gate.reshape(B, S, H, 3))
    o = jnp.einsum('bshc,bshcd->bshd', g.astype(o.dtype), o)
    return o.reshape(B, S, HD) @ w_out


def gla_mixer(h, w_in, w_gate_up, b_gate, g_out, w_out):
    B, S, _ = h.shape
    Hg = GLA_HEADS
    dk, dv = GLA_KEY_DIM // Hg, GLA_VALUE_DIM // Hg
    C = GLA_CHUNK
    nc = S // C
    q, k, v, g_low, r = jnp.split(h @ w_in, np.cumsum([GLA_KEY_DIM, GLA_KEY_DIM, GLA_VALUE_DIM, GLA_GATE_RANK]).tolist(), axis=-1)
    log_a = jax.nn.log_sigmoid((g_low @ w_gate_up + b_gate).astype(jnp.float32)) / GLA_TAU

    def chunked(t, d):
        return t.reshape(B, nc, C, Hg, d).transpose(0, 3, 1, 2, 4).astype(jnp.float32)

    q = chunked(q, dk) * (dk ** -0.5)
    k = chunked(k, dk)
    v = chunked(v, dv)
    b = jnp.cumsum(chunked(log_a, dk), axis=3)
    b_last = b[:, :, :, -1:, :]
    q_dec = q * jnp.exp(b)
    causal = jnp.tril(jnp.ones((C, C), dtype=bool))
    attn = jnp.where(causal, jnp.einsum('bhncd,bhnsd->bhncs', q_dec, k * jnp.exp(-b)), 0.0)
    o_intra = jnp.einsum('bhncs,bhnse->bhnce', attn, v)
    u = jnp.einsum('bhncd,bhnce->bhnde', k * jnp.exp(b_last - b), v)
    decay = jnp.exp(b_last[:, :, :, 0, :])

    def step(state, inp):
        q_c, dec_c, u_c = inp
        o_c = jnp.einsum('bhcd,bhde->bhce', q_c, state)
        return dec_c[..., None] * state + u_c, o_c

    state0 = jnp.zeros((B, Hg, dk, dv), jnp.float32)
    _, o_inter = lax.scan(step, state0, (jnp.moveaxis(q_dec, 2, 0), jnp.moveaxis(decay, 2, 0), jnp.moveaxis(u, 2, 0)))
    o = o_intra + jnp.moveaxis(o_inter, 0, 2)
    o = o.transpose(0, 2, 3, 1, 4).reshape(B, S, Hg, dv)
    o = rms_norm(o, g_out) * jax.nn.silu(r.reshape(B, S, Hg, dv).astype(jnp.float32))
    return o.reshape(B, S, GLA_VALUE_DIM).astype(h.dtype) @ w_out


def sb_mixer(h, w_in, w_out):
    B, S, _ = h.shape
    H, dh = N_HEADS, HEAD_DIM
    q, k, v = [t.reshape(B, S, H, dh).transpose(0, 2, 1, 3) for t in jnp.split(h @ w_in, 3, axis=-1)]
    nb = S // Q_BLOCK
    qb = q.reshape(B, H, nb, Q_BLOCK, dh).transpose(2, 0, 1, 3, 4)
    k_pos = jnp.arange(S)
    scale = dh ** -0.5

    def block(args):
        i, q_i = args
        q_pos = i * Q_BLOCK + jnp.arange(Q_BLOCK)
        z = jnp.einsum('bhqd,bhkd->bhqk', q_i, k).astype(jnp.float32) * scale
        strict = k_pos[None, :] < q_pos[:, None]
        log_1m = jnp.where(strict, jax.nn.log_sigmoid(-z), 0.0)
        later = lax.cumsum(log_1m, axis=3, reverse=True) - log_1m
        a = jnp.where(strict, jnp.exp(jax.nn.log_sigmoid(z) + later), 0.0)
        return jnp.einsum('bhqk,bhkd->bhqd', a.astype(v.dtype), v)

    o = lax.map(block, (jnp.arange(nb), qb))
    return o.transpose(1, 0, 3, 2, 4).reshape(B, S, HD) @ w_out


def _layers_with_mixer(m):
    return len(range(m, DEPTH, N_MIXERS))


def setup_inputs(seed: int = 0) -> dict:
    keys = iter(jax.random.split(jax.random.key(seed), 40))

    def normal(shape, scale):
        return jax.random.normal(next(keys), shape, jnp.float32) * scale

    def dense(shape):
        return normal(shape, shape[-2] ** -0.5)

    def gain(shape):
        return 1.0 + normal(shape, 0.02)

    nA, nB, nC, nD = (_layers_with_mixer(m) for m in range(N_MIXERS))
    D, F, dh = D_MODEL, FFN_DIM, HEAD_DIM
    fox_cols = 3 * HD + N_HEADS + D
    nsa_cols = HD + 6 * NSA_KV_HEADS * dh + 3 * N_HEADS
    gla_cols = 2 * GLA_KEY_DIM + GLA_VALUE_DIM + GLA_GATE_RANK + GLA_VALUE_DIM
    cmp_scale = (NSA_CMP_LEN * dh) ** -0.5
    return {
        'x': normal((BATCH, SEQ, D), 1.0),
        'norm_g': gain((DEPTH, 3, D)),
        'ffn1_w_gate': dense((DEPTH, D, F)),
        'ffn1_w_up': dense((DEPTH, D, F)),
        'ffn1_w_down': dense((DEPTH, F, D)),
        'ffn2_w_gate': dense((DEPTH, D, F)),
        'ffn2_w_up': dense((DEPTH, D, F)),
        'ffn2_w_down': dense((DEPTH, F, D)),
        'fox_w_in': dense((nA, D, fox_cols)),
        'fox_b_f': normal((nA, N_HEADS), 0.1),
        'fox_g_q': gain((nA, dh)),
        'fox_g_k': gain((nA, dh)),
        'fox_w_out': dense((nA, HD, D)),
        'nsa_w_in': dense((nB, D, nsa_cols)),
        'nsa_cmp_pos_k': normal((nB, NSA_CMP_LEN, dh), 0.02),
        'nsa_cmp_pos_v': normal((nB, NSA_CMP_LEN, dh), 0.02),
        'nsa_w_cmp_k': normal((nB, NSA_CMP_LEN, dh, dh), cmp_scale),
        'nsa_w_cmp_v': normal((nB, NSA_CMP_LEN, dh, dh), cmp_scale),
        'nsa_g_q': gain((nB, dh)),
        'nsa_g_k': gain((nB, dh)),
        'nsa_w_out': dense((nB, HD, D)),
        'gla_w_in': dense((nC, D, gla_cols)),
        'gla_w_gate_up': dense((nC, GLA_GATE_RANK, GLA_KEY_DIM)),
        'gla_b_gate': normal((nC, GLA_KEY_DIM), 0.1),
        'gla_g_out': gain((nC, GLA_VALUE_DIM // GLA_HEADS)),
        'gla_w_out': dense((nC, GLA_VALUE_DIM, D)),
        'sb_w_in': dense((nD, D, 3 * HD)),
        'sb_w_out': dense((nD, HD, D)),
    }


def reference(x, norm_g, ffn1_w_gate, ffn1_w_up, ffn1_w_down, ffn2_w_gate, ffn2_w_up, ffn2_w_down,
              fox_w_in, fox_b_f, fox_g_q, fox_g_k, fox_w_out,
              nsa_w_in, nsa_cmp_pos_k, nsa_cmp_pos_v, nsa_w_cmp_k, nsa_w_cmp_v, nsa_g_q, nsa_g_k, nsa_w_out,
              gla_w_in, gla_w_gate_up, gla_b_gate, gla_g_out, gla_w_out,
              sb_w_in, sb_w_out):
    cos, sin = rope_tables(x.shape[1], HEAD_DIM)
    for i in range(DEPTH):
        m, j = i % N_MIXERS, i // N_MIXERS
        x = x + 0.5 * swiglu(rms_norm(x, norm_g[i, 0]), ffn1_w_gate[i], ffn1_w_up[i], ffn1_w_down[i])
        h = rms_norm(x, norm_g[i, 1])
        if m == 0:
            y = fox_mixer(h, fox_w_in[j], fox_b_f[j], fox_g_q[j], fox_g_k[j], fox_w_out[j])
        elif m == 1:
            y = nsa_mixer(h, nsa_w_in[j], nsa_cmp_pos_k[j], nsa_cmp_pos_v[j], nsa_w_cmp_k[j], nsa_w_cmp_v[j],
                          nsa_g_q[j], nsa_g_k[j], nsa_w_out[j], cos, sin)
        elif m == 2:
            y = gla_mixer(h, gla_w_in[j], gla_w_gate_up[j], gla_b_gate[j], gla_g_out[j], gla_w_out[j])
        else:
            y = sb_mixer(h, sb_w_in[j], sb_w_out[j])
        x = x + y
        x = x + 0.5 * swiglu(rms_norm(x, norm_g[i, 2]), ffn2_w_gate[i], ffn2_w_up[i], ffn2_w_down[i])
    return x
```

```python
from contextlib import ExitStack
import ml_dtypes
from concourse.bass_utils import run_bass_kernel_spmd


import numpy as np
import concourse.bass as bass
import concourse.mybir as mybir

F32 = mybir.dt.float32
BF16 = mybir.dt.bfloat16
AF = mybir.ActivationFunctionType
ALU = mybir.AluOpType
AX = mybir.AxisListType

ENGS = ["pe", "act", "dve", "pool", "sp"]
NDQ = 6


class Buf:
    __slots__ = ("name", "w", "r")

    def __init__(self, name):
        self.name = name
        self.w = None
        self.r = {}


class Prog:
    def __init__(self, nc, stack):
        self.nc = nc
        self.stack = stack
        self.ops = {e: [] for e in ENGS}
        self.cnt = {}
        self.seen = {e: {} for e in ENGS}
        self.sems = {}
        self.dma_n = {"sp": 0, "pool": 0, "act": 0}
        for e in ENGS:
            self._mksem(e)
        for q in ("sp", "pool", "act"):
            for i in range(NDQ):
                self._mksem(f"dq_{q}_{i}")

    def _mksem(self, key):
        self.sems[key] = self.stack.enter_context(self.nc.semaphore("s_" + key))
        self.cnt[key] = 0

    def _deps(self, reads, writes):
        deps = {}
        def add(k, v):
            if v > deps.get(k, 0):
                deps[k] = v
        for b in reads:
            if b.w is not None:
                add(*b.w)
        for b in writes:
            if b.w is not None:
                add(*b.w)
            for k, v in b.r.items():
                add(k, v)
        return deps

    def _emit_waits(self, eng, deps, skip_key=None):
        seen = self.seen[eng]
        waits = []
        for k, v in deps.items():
            if k == skip_key:
                continue
            if v > seen.get(k, 0):
                seen[k] = v
                waits.append((self.sems[k], v))
        return waits

    def op(self, eng, fn, reads=(), writes=(), inc=True, strict=False):
        deps = self._deps(reads, writes)
        waits = self._emit_waits(eng, deps, skip_key=eng if (eng == 'pe' and not strict) else None)
        sem = self.sems[eng]
        if inc:
            self.cnt[eng] += 1
            val = self.cnt[eng]
        else:
            val = self.cnt[eng] + 1
        for b in reads:
            if val > b.r.get(eng, 0):
                b.r[eng] = val
        for b in writes:
            b.w = (eng, val)
            b.r = {}
        self.ops[eng].append((waits, fn, sem if inc else None, 1))

    def dma(self, q, fn, reads=(), writes=()):
        j = self.dma_n[q]
        self.dma_n[q] += 1
        key = f"dq_{q}_{j % NDQ}"
        deps = self._deps(reads, writes)
        if self.cnt[key] > 0:
            deps[key] = max(deps.get(key, 0), self.cnt[key])
        waits = self._emit_waits(q, deps)
        self.cnt[key] += 16
        val = self.cnt[key]
        for b in reads:
            if val > b.r.get(key, 0):
                b.r[key] = val
        for b in writes:
            b.w = (key, val)
            b.r = {}
        self.ops[q].append((waits, fn, self.sems[key], 16))

    def finish(self):
        deps = {k: v for k, v in self.cnt.items() if v > 0 and k != "sp"}
        waits = self._emit_waits("sp", deps)
        self.ops["sp"].append((waits, None, None, 0))

    def emit(self):
        nc = self.nc
        ops_all = self.ops
        self.ops = {e: [] for e in ENGS}
        self.n_phase = getattr(self, "n_phase", 0) + 1
        import contextlib
        scope = nc.named_scope("ph%02d" % self.n_phase) if getattr(self, "scopes", False) else contextlib.nullcontext()
        with scope, nc.Block() as block:
            def run(engname):
                def body(eng):
                    for waits, fn, sem, amt in ops_all[engname]:
                        for s, v in waits:
                            eng.wait_ge(s, v)
                        if fn is not None:
                            inst = fn(eng)
                            if sem is not None:
                                inst.then_inc(sem, amt)
                return body
            block.tensor(run("pe"))
            block.scalar(run("act"))
            block.vector(run("dve"))
            block.gpsimd(run("pool"))
            block.sync(run("sp"))


def _barrier(self):
    snap = {k: v for k, v in self.cnt.items() if v > 0}
    for e in ENGS:
        deps = {k: v for k, v in snap.items() if k != e}
        waits = self._emit_waits(e, deps)
        if waits:
            self.ops[e].append((waits, None, None, 0))
Prog.barrier = _barrier


def _phase_end(self):
    self.barrier()
    self.emit()
Prog.phase_end = _phase_end


D = 1024
F = 2816
NFC = F // 128
NKC = D // 128
FG = 256
NFG = F // FG


class Common:
    def __init__(self, P, consts_dram):
        nc, st = P.nc, P.stack
        self.P = P
        self.psum = []
        self.psb = []
        for i in range(8):
            t = st.enter_context(nc.psum_tensor(f"ps{i}", [128, 512], F32))
            self.psum.append(t)
            self.psb.append(Buf(f"ps{i}"))
        self.ones_f = st.enter_context(nc.sbuf_tensor("ones_f", [128, 128], F32))
        self.b_ones_f = Buf("ones_f")
        P.op("pool", lambda e: e.memset(self.ones_f[:], 1.0), writes=[self.b_ones_f])


def ffn_phase(P, C, xT_in, xT_out, wg, wu, wd, gvec, Tc, tag, wq="pool"):
    from contextlib import ExitStack
    nc = P.nc
    st = ExitStack()
    TT = 512
    NT = Tc // TT
    sb = lambda n, s, d: st.enter_context(nc.sbuf_tensor(f"{tag}_{n}", s, d))
    x_t = [sb(f"x{i}", [128, NKC, TT], F32) for i in range(2)]
    b_x = [Buf(f"x{i}") for i in range(2)]
    sq_t = sb("sq", [128, NKC, TT], F32); b_sq = Buf("sq")
    rstd = sb("rstd", [128, TT], F32); b_rstd = Buf("rstd")
    h_t = sb("h", [128, NKC, TT], BF16); b_h = Buf("h")
    a_t = sb("a", [128, NFC, TT], BF16); b_a = [Buf(f"a{i}") for i in range(NFC)]
    sil = [sb(f"sil{i}", [128, TT], F32) for i in range(2)]; b_sil = [Buf("sil0"), Buf("sil1")]
    wg_t = [sb(f"wg{i}", [128, NKC, FG], BF16) for i in range(2)]
    wu_t = [sb(f"wu{i}", [128, NKC, FG], BF16) for i in range(2)]
    b_wg = [Buf("wg0"), Buf("wg1")]; b_wu = [Buf("wu0"), Buf("wu1")]
    wd_t = [sb(f"wd{i}", [128, NFC, 128], BF16) for i in range(2)]
    b_wd = [Buf("wd0"), Buf("wd1")]
    g_t = sb("g", [128, NKC], F32); b_g = Buf("g")

    P.dma("sp", lambda e: e.dma_start(out=g_t[:], in_=gvec.rearrange("(c p) -> p c", p=128), allow_slow_non_contiguous=True),
          writes=[b_g])
    xin = xT_in.rearrange("(c p) t -> p c t", p=128)
    xout = xT_out.rearrange("(c p) t -> p c t", p=128)
    wgr = wg.rearrange("(c p) f -> p c f", p=128)
    wur = wu.rearrange("(c p) f -> p c f", p=128)
    wdr = wd.rearrange("(c p) d -> p c d", p=128)

    wcount = [0]
    for it in range(NT):
        xs = it % 2
        t0 = it * TT
        P.dma("sp", lambda e, xs=xs, t0=t0: e.dma_start(out=x_t[xs][:], in_=xin[:, :, t0:t0 + TT]),
              writes=[b_x[xs]])
        P.op("act", lambda e, xs=xs: e.activation(out=sq_t[:], in_=x_t[xs][:], func=AF.Square),
             reads=[b_x[xs]], writes=[b_sq])
        pst = 6
        for kc in range(NKC):
            P.op("pe", lambda e, kc=kc: e.matmul(C.psum[pst][:], lhsT=C.ones_f[:], rhs=sq_t[:, kc, :],
                                                 start=(kc == 0), stop=(kc == NKC - 1)),
                 reads=[b_sq, C.b_ones_f], writes=[C.psb[pst]], inc=(kc == NKC - 1))
        P.op("dve", lambda e: e.tensor_scalar(out=rstd[:], in0=C.psum[pst][:], scalar1=1.0 / D, scalar2=1e-6,
                                              op0=ALU.mult, op1=ALU.add),
             reads=[C.psb[pst]], writes=[b_rstd])
        P.op("act", lambda e: e.activation(out=rstd[:], in_=rstd[:], func=AF.Sqrt),
             reads=[b_rstd], writes=[b_rstd])
        P.op("dve", lambda e: e.reciprocal(out=rstd[:], in_=rstd[:]),
             reads=[b_rstd], writes=[b_rstd])
        for kc in range(NKC):
            P.op("dve", lambda e, kc=kc, xs=xs: e.scalar_tensor_tensor(
                out=h_t[:, kc, :], in0=x_t[xs][:, kc, :], scalar=g_t[:, kc:kc + 1], in1=rstd[:],
                op0=ALU.mult, op1=ALU.mult),
                reads=[b_x[xs], b_g, b_rstd], writes=[b_h])
        for fg in range(NFG):
            ws = wcount[0] % 2
            wcount[0] += 1
            f0 = fg * FG
            P.dma(wq, lambda e, ws=ws, f0=f0: e.dma_start(out=wg_t[ws][:], in_=wgr[:, :, f0:f0 + FG]),
                  writes=[b_wg[ws]])
            P.dma(wq, lambda e, ws=ws, f0=f0: e.dma_start(out=wu_t[ws][:], in_=wur[:, :, f0:f0 + FG]),
                  writes=[b_wu[ws]])
            for j in range(FG // 128):
                fc = fg * (FG // 128) + j
                pg = fc % 2
                pu = 2 + fc % 2
                for kc in range(NKC):
                    P.op("pe", lambda e, kc=kc, ws=ws, j=j, pg=pg: e.matmul(
                        C.psum[pg][:], lhsT=wg_t[ws][:, kc, j * 128:(j + 1) * 128], rhs=h_t[:, kc, :],
                        start=(kc == 0), stop=(kc == NKC - 1)),
                        reads=[b_wg[ws], b_h], writes=[C.psb[pg]], inc=(kc == NKC - 1))
                for kc in range(NKC):
                    P.op("pe", lambda e, kc=kc, ws=ws, j=j, pu=pu: e.matmul(
                        C.psum[pu][:], lhsT=wu_t[ws][:, kc, j * 128:(j + 1) * 128], rhs=h_t[:, kc, :],
                        start=(kc == 0), stop=(kc == NKC - 1)),
                        reads=[b_wu[ws], b_h], writes=[C.psb[pu]], inc=(kc == NKC - 1))
                ss = fc % 2
                P.op("act", lambda e, ss=ss, pg=pg: e.activation(out=sil[ss][:], in_=C.psum[pg][:], func=AF.Silu),
                     reads=[C.psb[pg]], writes=[b_sil[ss]])
                P.op("dve", lambda e, ss=ss, pu=pu, fc=fc: e.tensor_tensor(
                    out=a_t[:, fc, :], in0=sil[ss][:], in1=C.psum[pu][:], op=ALU.mult),
                    reads=[b_sil[ss], C.psb[pu]], writes=[b_a[fc]])
        for dc in range(NKC):
            ws = dc % 2
            P.dma(wq, lambda e, ws=ws, dc=dc: e.dma_start(out=wd_t[ws][:], in_=wdr[:, :, dc * 128:(dc + 1) * 128]),
                  writes=[b_wd[ws]])
            py = 4 + dc % 2
            for fc in range(NFC):
                P.op("pe", lambda e, fc=fc, ws=ws, py=py: e.matmul(
                    C.psum[py][:], lhsT=wd_t[ws][:, fc, :], rhs=a_t[:, fc, :],
                    start=(fc == 0), stop=(fc == NFC - 1)),
                    reads=[b_wd[ws], b_a[fc]], writes=[C.psb[py]], inc=(fc == NFC - 1))
            P.op("dve", lambda e, dc=dc, xs=xs, py=py: e.scalar_tensor_tensor(
                out=x_t[xs][:, dc, :], in0=C.psum[py][:], scalar=0.5, in1=x_t[xs][:, dc, :],
                op0=ALU.mult, op1=ALU.add),
                reads=[C.psb[py], b_x[xs]], writes=[b_x[xs]])
        P.dma("sp", lambda e, xs=xs, t0=t0: e.dma_start(out=xout[:, :, t0:t0 + TT], in_=x_t[xs][:]),
              reads=[b_x[xs]])
    P.phase_end()
    st.close()


from contextlib import ExitStack

H = 16
DH = 64
TT = 512


def ps_next(C):
    i = C.ps_i = (getattr(C, "ps_i", -1) + 1) % 8
    return C.psum[i], C.psb[i]


def load_x_norm(P, C, sb, xin, t0, g_t, b_g, tag):
    x_t = sb("x", [128, NKC, TT], F32); b_x = Buf("x")
    sq_t = sb("sq", [128, NKC, TT], F32); b_sq = Buf("sq")
    rstd = sb("rstd", [128, TT], F32); b_rstd = Buf("rstd")
    h_t = sb("h", [128, NKC, TT], BF16); b_h = Buf("h")
    return x_t, b_x, sq_t, b_sq, rstd, b_rstd, h_t, b_h


def emit_norm(P, C, x_t, b_x, sq_t, b_sq, rstd, b_rstd, h_t, b_h, g_t, b_g):
    P.op("act", lambda e: e.activation(out=sq_t[:], in_=x_t[:], func=AF.Square), reads=[b_x], writes=[b_sq])
    ps, pb = ps_next(C)
    for kc in range(NKC):
        P.op("pe", lambda e, kc=kc: e.matmul(ps[:], lhsT=C.ones_f[:], rhs=sq_t[:, kc, :], start=(kc == 0), stop=(kc == NKC - 1)),
             reads=[b_sq, C.b_ones_f], writes=[pb], inc=(kc == NKC - 1))
    P.op("dve", lambda e: e.tensor_scalar(out=rstd[:], in0=ps[:], scalar1=1.0 / D, scalar2=1e-6, op0=ALU.mult, op1=ALU.add),
         reads=[pb], writes=[b_rstd])
    P.op("act", lambda e: e.activation(out=rstd[:], in_=rstd[:], func=AF.Sqrt), reads=[b_rstd], writes=[b_rstd])
    P.op("dve", lambda e: e.reciprocal(out=rstd[:], in_=rstd[:]), reads=[b_rstd], writes=[b_rstd])
    for kc in range(NKC):
        P.op("dve", lambda e, kc=kc: e.scalar_tensor_tensor(out=h_t[:, kc, :], in0=x_t[:, kc, :], scalar=g_t[:, kc:kc + 1],
                                                            in1=rstd[:], op0=ALU.mult, op1=ALU.mult),
             reads=[b_x, b_g, b_rstd], writes=[b_h])


def lin_fm(P, C, sb, wr, col0, ncols, src_t, b_src, consume, tag, G=256, nkc=NKC):
    wt = [sb(f"{tag}w{i}", [128, nkc, G], BF16) for i in range(2)]
    bw = [Buf(f"{tag}w0"), Buf(f"{tag}w1")]
    ng = (ncols + G - 1) // G
    def run():
        for g in range(ng):
            ws = g % 2
            c0 = col0 + g * G
            gw = min(G, col0 + ncols - c0)
            P.dma("pool", lambda e, ws=ws, c0=c0, gw=gw: e.dma_start(out=wt[ws][:, :, 0:gw], in_=wr[:, :, c0:c0 + gw]),
                  writes=[bw[ws]])
            for j in range((gw + 127) // 128):
                cw = min(128, gw - j * 128)
                ps, pb = ps_next(C)
                for kc in range(nkc):
                    P.op("pe", lambda e, kc=kc, ws=ws, j=j, cw=cw, ps=ps: e.matmul(
                        ps[0:cw, :], lhsT=wt[ws][:, kc, j * 128:j * 128 + cw], rhs=src_t[:, kc, :],
                        start=(kc == 0), stop=(kc == nkc - 1)),
                        reads=[bw[ws], b_src], writes=[pb], inc=(kc == nkc - 1))
                consume(g * (G // 128) + j, ps, pb)
    return run


def fox_A(P, C, xT, S, w_in, gnorm, b_f, g_q, g_k, scr):
    nc = P.nc
    with ExitStack() as st:
        sb = lambda n, s, d: st.enter_context(nc.sbuf_tensor(f"fa_{n}", s, d))
        g_t = sb("g", [128, NKC], F32); b_g = Buf("g")
        P.dma("sp", lambda e: e.dma_start(out=g_t[:], in_=gnorm.rearrange("(c p) -> p c", p=128), allow_slow_non_contiguous=True), writes=[b_g])
        gq = sb("gq", [128, 1], F32); gk = sb("gk", [128, 1], F32); b_gqk = Buf("gqk")
        for half in range(2):
            P.dma("sp", lambda e, half=half: e.dma_start(out=gq[half * 64:(half + 1) * 64, :], in_=g_q.rearrange("(p o) -> p o", o=1), allow_slow_non_contiguous=True), writes=[b_gqk])
            P.dma("sp", lambda e, half=half: e.dma_start(out=gk[half * 64:(half + 1) * 64, :], in_=g_k.rearrange("(p o) -> p o", o=1), allow_slow_non_contiguous=True), writes=[b_gqk])
        nbf = sb("nbf", [16, 1], F32); b_nbf = Buf("nbf")
        P.dma("sp", lambda e: e.dma_start(out=nbf[:], in_=b_f.rearrange("(p o) -> p o", o=1), allow_slow_non_contiguous=True), writes=[b_nbf])
        P.op("dve", lambda e: e.tensor_scalar(out=nbf[:], in0=nbf[:], scalar1=-1.0, scalar2=None, op0=ALU.mult), reads=[b_nbf], writes=[b_nbf])
        bones = sb("bones", [128, 128], F32); b_bones = Buf("bones")
        P.op("pool", lambda e: e.memset(bones[:], 0.0), writes=[b_bones])
        P.op("pool", lambda e: e.memset(bones[0:64, 0:64], 1.0), writes=[b_bones])
        P.op("pool", lambda e: e.memset(bones[64:128, 64:128], 1.0), writes=[b_bones])
        x_t, b_x, sq_t, b_sq, rstd, b_rstd, h_t, b_h = load_x_norm(P, C, sb, None, 0, g_t, b_g, "fa")
        sqq = [sb(f"sqq{i}", [128, TT], F32) for i in range(2)]; b_sqq = [Buf("sqq0"), Buf("sqq1")]
        rr = [sb(f"rr{i}", [128, TT], F32) for i in range(2)]; b_rr = [Buf("rr0"), Buf("rr1")]
        stg = [sb(f"stg{i}", [128, TT], BF16) for i in range(3)]; b_stg = [Buf(f"stg{i}") for i in range(3)]
        vst = [sb(f"vst{i}", [128, 4, 512], BF16) for i in range(2)]; b_vst = [Buf("vst0"), Buf("vst1")]
        wv_t = [sb(f"wv{i}", [128, NKC, 512], BF16) for i in range(2)]; b_wv = [Buf("wv0"), Buf("wv1")]
        fe = sb("fe", [16, TT], F32); b_fe = Buf("fe")
        xin = xT.rearrange("(c p) t -> p c t", p=128)
        wr = w_in.rearrange("(c p) f -> p c f", p=128)
        cnt = {"s": 0, "q": 0}

        def mk_qk(dst, gcol, t0):
            def consume(ci, ps, pb):
                i = cnt["q"] % 2; cnt["q"] += 1
                s3 = cnt["s"] % 3; cnt["s"] += 1
                P.op("act", lambda e: e.activation(out=sqq[i][:], in_=ps[:], func=AF.Square), reads=[pb], writes=[b_sqq[i]])
                ps2, pb2 = ps_next(C)
                P.op("pe", lambda e: e.matmul(ps2[:], lhsT=bones[:], rhs=sqq[i][:], start=True, stop=True),
                     reads=[b_bones, b_sqq[i]], writes=[pb2])
                P.op("dve", lambda e: e.tensor_scalar(out=rr[i][:], in0=ps2[:], scalar1=1.0 / DH, scalar2=1e-6, op0=ALU.mult, op1=ALU.add),
                     reads=[pb2], writes=[b_rr[i]])
                P.op("act", lambda e: e.activation(out=rr[i][:], in_=rr[i][:], func=AF.Sqrt), reads=[b_rr[i]], writes=[b_rr[i]])
                P.op("dve", lambda e: e.reciprocal(out=rr[i][:], in_=rr[i][:]), reads=[b_rr[i]], writes=[b_rr[i]])
                P.op("dve", lambda e: e.scalar_tensor_tensor(out=stg[s3][:], in0=ps[:], scalar=gcol[:, 0:1], in1=rr[i][:],
                                                             op0=ALU.mult, op1=ALU.mult),
                     reads=[pb, b_gqk, b_rr[i]], writes=[b_stg[s3]])
                P.dma("sp", lambda e: e.dma_start(out=dst[ci * 128:(ci + 1) * 128, t0:t0 + TT], in_=stg[s3][:]), reads=[b_stg[s3]])
            return consume

        def mk_og(t0):
            def consume(ci, ps, pb):
                s3 = cnt["s"] % 3; cnt["s"] += 1
                P.op("act", lambda e: e.activation(out=stg[s3][:], in_=ps[:], func=AF.Sigmoid), reads=[pb], writes=[b_stg[s3]])
                P.dma("sp", lambda e: e.dma_start(out=scr["sg"][ci * 128:(ci + 1) * 128, t0:t0 + TT], in_=stg[s3][:]), reads=[b_stg[s3]])
            return consume

        def mk_f(t0):
            def consume(ci, ps, pb):
                P.op("act", lambda e: e.activation(out=fe[:], in_=ps[0:16, :], func=AF.Exp, bias=nbf[:, 0:1], scale=-1.0),
                     reads=[pb, b_nbf], writes=[b_fe])
                P.op("act", lambda e: e.activation(out=fe[:], in_=fe[:], func=AF.Ln, bias=1.0, scale=1.0), reads=[b_fe], writes=[b_fe])
                P.op("dve", lambda e: e.tensor_scalar(out=fe[:], in0=fe[:], scalar1=-1.0, scalar2=None, op0=ALU.mult), reads=[b_fe], writes=[b_fe])
                P.dma("sp", lambda e: e.dma_start(out=scr["ls"][:, t0:t0 + TT], in_=fe[:]), reads=[b_fe])
            return consume

        for it in range(S // TT):
            t0 = it * TT
            P.dma("sp", lambda e, t0=t0: e.dma_start(out=x_t[:], in_=xin[:, :, t0:t0 + TT]), writes=[b_x])
            emit_norm(P, C, x_t, b_x, sq_t, b_sq, rstd, b_rstd, h_t, b_h, g_t, b_g)
            for name, col0, ncols, mk in (("q", 0, 1024, lambda: mk_qk(scr["qT"], gq, t0)),
                                          ("k", 1024, 1024, lambda: mk_qk(scr["kT"], gk, t0)),
                                          ("og", 3088, 1024, lambda: mk_og(t0)),
                                          ("f", 3072, 16, lambda: mk_f(t0))):
                key = "wbuf_" + name
                if key not in cnt:
                    cnt[key] = ([sb(f"{name}w{i}", [128, NKC, 256], BF16) for i in range(2)], [Buf(name + "w0"), Buf(name + "w1")])
                wt, bw = cnt[key]
                consume = mk()
                G = 256
                ng = (ncols + G - 1) // G
                for g in range(ng):
                    ws = g % 2
                    c0 = col0 + g * G
                    gw = min(G, col0 + ncols - c0)
                    P.dma("pool", lambda e, ws=ws, c0=c0, gw=gw, wt=wt: e.dma_start(out=wt[ws][:, :, 0:gw], in_=wr[:, :, c0:c0 + gw]),
                          writes=[bw[ws]])
                    for j in range((gw + 127) // 128):
                        cw = min(128, gw - j * 128)
                        ps, pb = ps_next(C)
                        for kc in range(NKC):
                            P.op("pe", lambda e, kc=kc, ws=ws, j=j, cw=cw, ps=ps, wt=wt: e.matmul(
                                ps[0:cw, :], lhsT=wt[ws][:, kc, j * 128:j * 128 + cw], rhs=h_t[:, kc, :],
                                start=(kc == 0), stop=(kc == NKC - 1)),
                                reads=[bw[ws], b_h], writes=[pb], inc=(kc == NKC - 1))
                        consume(g * (G // 128) + j, ps, pb)
            for cg in range(2):
                P.dma("pool", lambda e, cg=cg: e.dma_start(out=wv_t[cg][:], in_=wr[:, :, 2048 + cg * 512:2048 + (cg + 1) * 512]),
                      writes=[b_wv[cg]])
                vs = cg
                for tb in range(4):
                    ps, pb = ps_next(C)
                    for kc in range(NKC):
                        P.op("pe", lambda e, kc=kc, tb=tb, cg=cg, ps=ps: e.matmul(
                            ps[:], lhsT=h_t[:, kc, tb * 128:(tb + 1) * 128], rhs=wv_t[cg][:, kc, :],
                            start=(kc == 0), stop=(kc == NKC - 1)),
                            reads=[b_wv[cg], b_h], writes=[pb], inc=(kc == NKC - 1))
                    P.op("act", lambda e, tb=tb, vs=vs, ps=ps: e.activation(out=vst[vs][:, tb, :], in_=ps[:], func=AF.Copy),
                         reads=[pb], writes=[b_vst[vs]])
                P.dma("sp", lambda e, vs=vs, cg=cg, t0=t0: e.dma_start(
                    out=scr["V"][t0:t0 + TT, cg * 512:(cg + 1) * 512].rearrange("(tb p) c -> p tb c", p=128), in_=vst[vs][:]),
                    reads=[b_vst[vs]])
        P.phase_end()


def fox_B(P, C, S, scr, consts):
    nc = P.nc
    NB = S // 128
    NQC = S // 512
    with ExitStack() as st:
        sb = lambda n, s, d: st.enter_context(nc.sbuf_tensor(f"fb_{n}", s, d))
        tri = sb("tri", [128, 128], BF16); b_tri = Buf("tri")
        P.dma("sp", lambda e: e.dma_start(out=tri[:], in_=consts["tri"]), writes=[b_tri])
        identb = sb("identb", [128, 128], BF16)
        P.dma("sp", lambda e: e.dma_start(out=identb[:], in_=consts["identb"]), writes=[b_tri])
        ident = sb("ident", [128, 128], F32); b_ident = Buf("ident")
        P.dma("sp", lambda e: e.dma_start(out=ident[:], in_=consts["ident"]), writes=[b_ident])
        c_t = sb("c", [16, S], F32); b_c = Buf("c")
        zc = sb("zc", [16, 1], F32); b_zc = Buf("zc")
        P.op("pool", lambda e: e.memset(zc[:], 0.0), writes=[b_zc])
        P.dma("sp", lambda e: e.dma_start(out=c_t[:], in_=scr["ls"]), writes=[b_c])
        P.op("dve", lambda e: e.tensor_tensor_scan(out=c_t[:], data0=c_t[:], data1=zc[:, 0:1].to_broadcast([16, S]), initial=0.0,
                                                   op0=ALU.add, op1=ALU.add), reads=[b_c, b_zc], writes=[b_c])
        negc = sb("negc", [128, NB, 16], F32); b_negc = Buf("negc")
        GB = 32
        for g0 in range(0, NB, GB):
            ps, pb = ps_next(C)
            nb_ = min(GB, NB - g0)
            for j in range(nb_):
                kb = g0 + j
                P.op("pe", lambda e, kb=kb, j=j, ps=ps: e.transpose(ps[:, j * 16:(j + 1) * 16], c_t[:, kb * 128:(kb + 1) * 128], ident[0:16, 0:16]),
                     reads=[b_c, b_ident], writes=[pb], inc=(j == nb_ - 1))
            P.op("dve", lambda e, g0=g0, nb_=nb_, ps=ps: e.tensor_scalar(
                out=negc[:, g0:g0 + nb_, :].rearrange("p a b -> p (a b)"), in0=ps[:, 0:nb_ * 16], scalar1=-1.0, scalar2=None, op0=ALU.mult),
                reads=[pb], writes=[b_negc])
        b_crow = Buf("crow")
        r_t = sb("r", [16, S], F32); b_r = Buf("r")
        hi = sb("hi", [16, S], BF16); b_hi = Buf("hi")
        P.op("dve", lambda e: e.tensor_scalar(out=r_t[:], in0=c_t[:], scalar1=8.0, scalar2=None, op0=ALU.mult), reads=[b_c], writes=[b_r])
        for j in range(3):
            P.op("dve", lambda e: e.tensor_copy(out=hi[:], in_=r_t[:]), reads=[b_r], writes=[b_hi])
            P.dma("sp", lambda e, j=j: e.dma_start(out=scr["crow"][j], in_=hi[:]), reads=[b_hi], writes=[b_crow])
            if j < 2:
                P.op("dve", lambda e: e.tensor_tensor(out=r_t[:], in0=r_t[:], in1=hi[:], op=ALU.subtract), reads=[b_r, b_hi], writes=[b_r])
        qa = [sb(f"qa{i}", [67, S], BF16) for i in range(2)]; b_qa = [Buf("qa0"), Buf("qa1")]
        ka = [sb(f"ka{i}", [67, S], BF16) for i in range(2)]; b_ka = [Buf("ka0"), Buf("ka1")]
        va = [sb(f"va{i}", [128, NB, 65], BF16) for i in range(2)]; b_va = [Buf("va0"), Buf("va1")]
        for i in range(2):
            P.op("pool", lambda e, i=i: e.memset(ka[i][64:67, :], 1.0), writes=[b_ka[i]])
            P.op("pool", lambda e, i=i: e.memset(va[i][:, :, 64:65], 1.0), writes=[b_va[i]])
        pt = [sb(f"pt{i}", [128, 512], BF16) for i in range(3)]; b_pt = [Buf(f"pt{i}") for i in range(3)]
        osb = sb("osb", [65, 512], F32); b_osb = Buf("osb")
        rec = sb("rec", [65, 512], F32); b_rec = Buf("rec")
        ost = [sb(f"ost{i}", [64, 512], BF16) for i in range(2)]; b_ost = [Buf("ost0"), Buf("ost1")]
        npt = 0
        for h in range(H):
            hs = h % 2
            P.dma("sp", lambda e, h=h, hs=hs: e.dma_start(out=qa[hs][0:64, :], in_=scr["qT"][h * 64:(h + 1) * 64, :]), writes=[b_qa[hs]])
            P.dma("sp", lambda e, h=h, hs=hs: e.dma_start(out=qa[hs][64:67, :], in_=scr["crow"][:, h, :]), reads=[b_crow], writes=[b_qa[hs]])
            P.dma("sp", lambda e, h=h, hs=hs: e.dma_start(out=ka[hs][0:64, :], in_=scr["kT"][h * 64:(h + 1) * 64, :]), writes=[b_ka[hs]])
            P.dma("sp", lambda e, h=h, hs=hs: e.dma_start(out=va[hs][:, :, 0:64], in_=scr["V"][:, h * 64:(h + 1) * 64].rearrange("(kb p) d -> p kb d", p=128)),
                  writes=[b_va[hs]])
            for qc in range(NQC):
                po, pob = ps_next(C)
                nkb = 4 * qc + 4
                pend = None
                for kb in range(nkb):
                    off = max(0, kb - 4 * qc) * 128
                    ps, pb = ps_next(C)
                    if ps is po:
                        ps, pb = ps_next(C)
                    diag = kb >= 4 * qc
                    P.op("pe", lambda e, kb=kb, off=off, ps=ps, hs=hs, qc=qc, diag=diag: e.matmul(
                        ps[:, off:512], lhsT=ka[hs][0:67, kb * 128:(kb + 1) * 128], rhs=qa[hs][0:67, qc * 512 + off:(qc + 1) * 512],
                        start=True, stop=not diag), reads=[b_ka[hs], b_qa[hs]], writes=[pb], inc=not diag)
                    if diag:
                        P.op("pe", lambda e, off=off, ps=ps: e.matmul(
                            ps[:, off:off + 128], lhsT=identb[:], rhs=tri[:], start=False, stop=True),
                            reads=[b_tri], writes=[pb])
                    pi = npt % 3; npt += 1
                    P.op("act", lambda e, kb=kb, off=off, ps=ps, pi=pi, h=h: e.activation(
                        out=pt[pi][:, off:512], in_=ps[:, off:512], func=AF.Exp, bias=negc[:, kb, h:h + 1], scale=0.125),
                        reads=[pb, b_negc], writes=[b_pt[pi]])
                    def pv(kb=kb, off=off, pi=pi, hs=hs, po=po, nkb=nkb, pob=pob):
                        P.op("pe", lambda e: e.matmul(
                            po[0:65, off:512], lhsT=va[hs][:, kb, :], rhs=pt[pi][:, off:512], start=(kb == 0), stop=(kb == nkb - 1)),
                            reads=[b_va[hs], b_pt[pi]], writes=[pob])
                    if pend is not None:
                        pend()
                    pend = pv
                pend()
                pend = None
                P.op("act", lambda e, po=po: e.activation(out=osb[:], in_=po[0:65, :], func=AF.Copy), reads=[pob], writes=[b_osb])
                P.op("dve", lambda e: e.reciprocal(out=rec[64:65, :], in_=osb[64:65, :]), reads=[b_osb], writes=[b_rec])
                pbc, pbcb = ps_next(C)
                P.op("pe", lambda e, pbc=pbc: e.matmul(pbc[0:64, :], lhsT=C.ones_f[64:65, 0:64], rhs=rec[64:65, :], start=True, stop=True),
                     reads=[b_rec, C.b_ones_f], writes=[pbcb])
                oi = qc % 2
                P.op("dve", lambda e, pbc=pbc, oi=oi: e.tensor_tensor(out=ost[oi][:], in0=osb[0:64, :], in1=pbc[0:64, :], op=ALU.mult),
                     reads=[b_osb, pbcb], writes=[b_ost[oi]])
                P.dma("sp", lambda e, h=h, qc=qc, oi=oi: e.dma_start(out=scr["oT"][h * 64:(h + 1) * 64, qc * 512:(qc + 1) * 512], in_=ost[oi][:]),
                      reads=[b_ost[oi]])
        P.phase_end()


def mix_C(P, C, xT, S, w_out, scr, gate_key):
    nc = P.nc
    with ExitStack() as st:
        mix_C.n = getattr(mix_C, "n", 0) + 1
        tagc = mix_C.n
        sb = lambda n, s, d: st.enter_context(nc.sbuf_tensor(f"mc{tagc}_{n}", s, d))
        x_t = [sb(f"x{i}", [128, NKC, TT], F32) for i in range(2)]; b_x = [Buf("x0"), Buf("x1")]
        o_t = [sb(f"o{i}", [128, NKC, TT], BF16) for i in range(2)]; b_o = [Buf("o0"), Buf("o1")]
        g_t = [sb(f"gt{i}", [128, NKC, TT], BF16) for i in range(2)]; b_gt = [Buf("gt0"), Buf("gt1")]
        wo = [sb(f"wo{i}", [128, NKC, 256], BF16) for i in range(2)]; b_wo = [Buf("wo0"), Buf("wo1")]
        xin = xT.rearrange("(c p) t -> p c t", p=128)
        oin = scr["oT"].rearrange("(c p) t -> p c t", p=128)
        wr = w_out.rearrange("(c p) f -> p c f", p=128)
        if gate_key:
            gin = scr[gate_key].rearrange("(c p) t -> p c t", p=128)
        for it in range(S // TT):
            t0 = it * TT
            xs = it % 2
            P.dma("sp", lambda e, t0=t0, xs=xs: e.dma_start(out=x_t[xs][:], in_=xin[:, :, t0:t0 + TT]), writes=[b_x[xs]])
            P.dma("sp", lambda e, t0=t0, xs=xs: e.dma_start(out=o_t[xs][:], in_=oin[:, :, t0:t0 + TT]), writes=[b_o[xs]])
            if gate_key:
                P.dma("sp", lambda e, t0=t0, xs=xs: e.dma_start(out=g_t[xs][:], in_=gin[:, :, t0:t0 + TT]), writes=[b_gt[xs]])
                P.op("pool", lambda e, xs=xs: e.tensor_tensor(out=o_t[xs][:], in0=o_t[xs][:], in1=g_t[xs][:], op=ALU.mult),
                     reads=[b_o[xs], b_gt[xs]], writes=[b_o[xs]])
            for g in range(4):
                ws = g % 2
                P.dma("pool", lambda e, ws=ws, g=g: e.dma_start(out=wo[ws][:], in_=wr[:, :, g * 256:(g + 1) * 256]), writes=[b_wo[ws]])
                for j in range(2):
                    dc = g * 2 + j
                    ps, pb = ps_next(C)
                    for kc in range(NKC):
                        P.op("pe", lambda e, kc=kc, ws=ws, j=j, ps=ps, xs=xs: e.matmul(
                            ps[:], lhsT=wo[ws][:, kc, j * 128:(j + 1) * 128], rhs=o_t[xs][:, kc, :], start=(kc == 0), stop=(kc == NKC - 1)),
                            reads=[b_wo[ws], b_o[xs]], writes=[pb], inc=(kc == NKC - 1))
                    P.op("dve", lambda e, dc=dc, ps=ps, xs=xs: e.tensor_tensor(out=x_t[xs][:, dc, :], in0=x_t[xs][:, dc, :], in1=ps[:], op=ALU.add),
                         reads=[pb, b_x[xs]], writes=[b_x[xs]])
            P.dma("sp", lambda e, t0=t0, xs=xs: e.dma_start(out=xin[:, :, t0:t0 + TT], in_=x_t[xs][:]), reads=[b_x[xs]])
        P.phase_end()


def qkv_A(P, C, xT, S, w_in, gnorm, scr):
    nc = P.nc
    with ExitStack() as st:
        sb = lambda n, s, d: st.enter_context(nc.sbuf_tensor(f"sa_{n}", s, d))
        g_t = sb("g", [128, NKC], F32); b_g = Buf("g")
        P.dma("sp", lambda e: e.dma_start(out=g_t[:], in_=gnorm.rearrange("(c p) -> p c", p=128), allow_slow_non_contiguous=True), writes=[b_g])
        x_t, b_x, sq_t, b_sq, rstd, b_rstd, h_t, b_h = load_x_norm(P, C, sb, None, 0, g_t, b_g, "sa")
        stg = [sb(f"stg{i}", [128, TT], BF16) for i in range(3)]; b_stg = [Buf(f"stg{i}") for i in range(3)]
        vst = [sb(f"vst{i}", [128, 4, 512], BF16) for i in range(2)]; b_vst = [Buf("vst0"), Buf("vst1")]
        wv_t = [sb(f"wv{i}", [128, NKC, 512], BF16) for i in range(2)]; b_wv = [Buf("wv0"), Buf("wv1")]
        wt = [sb(f"w{i}", [128, NKC, 256], BF16) for i in range(2)]; bw = [Buf("w0"), Buf("w1")]
        xin = xT.rearrange("(c p) t -> p c t", p=128)
        wr = w_in.rearrange("(c p) f -> p c f", p=128)
        ns = 0
        for it in range(S // TT):
            t0 = it * TT
            P.dma("sp", lambda e, t0=t0: e.dma_start(out=x_t[:], in_=xin[:, :, t0:t0 + TT]), writes=[b_x])
            emit_norm(P, C, x_t, b_x, sq_t, b_sq, rstd, b_rstd, h_t, b_h, g_t, b_g)
            for dst, col0 in ((scr["qT"], 0), (scr["kT"], 1024)):
                for g in range(4):
                    ws = g % 2
                    c0 = col0 + g * 256
                    P.dma("pool", lambda e, ws=ws, c0=c0: e.dma_start(out=wt[ws][:], in_=wr[:, :, c0:c0 + 256]), writes=[bw[ws]])
                    for j in range(2):
                        ci = g * 2 + j
                        ps, pb = ps_next(C)
                        for kc in range(NKC):
                            P.op("pe", lambda e, kc=kc, ws=ws, j=j, ps=ps: e.matmul(
                                ps[:], lhsT=wt[ws][:, kc, j * 128:(j + 1) * 128], rhs=h_t[:, kc, :], start=(kc == 0), stop=(kc == NKC - 1)),
                                reads=[bw[ws], b_h], writes=[pb], inc=(kc == NKC - 1))
                        s3 = ns % 3; ns += 1
                        P.op("act", lambda e, s3=s3, ps=ps: e.activation(out=stg[s3][:], in_=ps[:], func=AF.Copy), reads=[pb], writes=[b_stg[s3]])
                        P.dma("sp", lambda e, s3=s3, ci=ci, dst=dst, t0=t0: e.dma_start(out=dst[ci * 128:(ci + 1) * 128, t0:t0 + TT], in_=stg[s3][:]), reads=[b_stg[s3]])
            for cg in range(2):
                P.dma("pool", lambda e, cg=cg: e.dma_start(out=wv_t[cg][:], in_=wr[:, :, 2048 + cg * 512:2048 + (cg + 1) * 512]), writes=[b_wv[cg]])
                for tb in range(4):
                    ps, pb = ps_next(C)
                    for kc in range(NKC):
                        P.op("pe", lambda e, kc=kc, tb=tb, cg=cg, ps=ps: e.matmul(
                            ps[:], lhsT=h_t[:, kc, tb * 128:(tb + 1) * 128], rhs=wv_t[cg][:, kc, :], start=(kc == 0), stop=(kc == NKC - 1)),
                            reads=[b_wv[cg], b_h], writes=[pb], inc=(kc == NKC - 1))
                    P.op("act", lambda e, tb=tb, cg=cg, ps=ps: e.activation(out=vst[cg][:, tb, :], in_=ps[:], func=AF.Copy), reads=[pb], writes=[b_vst[cg]])
                P.dma("sp", lambda e, cg=cg, t0=t0: e.dma_start(
                    out=scr["V"][t0:t0 + TT, cg * 512:(cg + 1) * 512].rearrange("(tb p) c -> p tb c", p=128), in_=vst[cg][:]), reads=[b_vst[cg]])
        P.phase_end()


def sb_B(P, C, S, scr, consts):
    nc = P.nc
    NB = S // 128
    NQC = S // 512
    with ExitStack() as st:
        sb = lambda n, s, d: st.enter_context(nc.sbuf_tensor(f"sbb_{n}", s, d))
        cb = Buf("consts")
        def ld(name, dt=BF16):
            t = sb(name, [128, 128], dt)
            P.dma("sp", lambda e: e.dma_start(out=t[:], in_=consts[name]), writes=[cb])
            return t
        trii = ld("trii"); nones = ld("nones"); strict = ld("strict"); negti = ld("negtri_incl"); identb = ld("identb")
        zer = sb("zer", [128, 512], BF16)
        P.op("pool", lambda e: e.memset(zer[:], 0.0), writes=[cb])
        qa = [sb(f"qa{i}", [64, S], BF16) for i in range(2)]; b_qa = [Buf("qa0"), Buf("qa1")]
        ka = [sb(f"ka{i}", [64, S], BF16) for i in range(2)]; b_ka = [Buf("ka0"), Buf("ka1")]
        va = [sb(f"va{i}", [128, NB, 64], BF16) for i in range(2)]; b_va = [Buf("va0"), Buf("va1")]
        e1 = [sb(f"e1{i}", [128, 512], F32) for i in range(3)]; b_e1 = [Buf(f"e1{i}") for i in range(3)]
        ew = [sb(f"ew{i}", [128, 512], F32) for i in range(2)]; b_ew = [Buf("ew0"), Buf("ew1")]
        strictf = sb("strictf", [128, 128], F32)
        P.op("dve", lambda e: e.tensor_copy(out=strictf[:], in_=strict[:]), reads=[cb], writes=[cb])
        lb = [sb(f"lb{i}", [128, 512], BF16) for i in range(3)]; b_lb = [Buf(f"lb{i}") for i in range(3)]
        acc = sb("acc", [128, 512], F32); b_acc = Buf("acc")
        accb = [sb(f"accb{i}", [128, 512], BF16) for i in range(3)]; b_accb = [Buf(f"accb{i}") for i in range(3)]
        pt = [sb(f"pt{i}", [128, 512], BF16) for i in range(3)]; b_pt = [Buf(f"pt{i}") for i in range(3)]
        ost = [sb(f"ost{i}", [64, 512], BF16) for i in range(2)]; b_ost = [Buf("ost0"), Buf("ost1")]
        n = 0
        for h in range(H):
            hs = h % 2
            P.dma("sp", lambda e, h=h, hs=hs: e.dma_start(out=qa[hs][:], in_=scr["qT"][h * 64:(h + 1) * 64, :]), writes=[b_qa[hs]])
            P.dma("sp", lambda e, h=h, hs=hs: e.dma_start(out=ka[hs][:], in_=scr["kT"][h * 64:(h + 1) * 64, :]), writes=[b_ka[hs]])
            P.dma("sp", lambda e, h=h, hs=hs: e.dma_start(out=va[hs][:], in_=scr["V"][:, h * 64:(h + 1) * 64].rearrange("(kb p) d -> p kb d", p=128)),
                  writes=[b_va[hs]])
            for qc in range(NQC):
                po, pob = ps_next(C)
                P.op("pe", lambda e, po=po, hs=hs: e.matmul(po[0:64, :], lhsT=zer[:, 0:64], rhs=zer[:], start=True, stop=False),
                     reads=[cb], writes=[pob], inc=False)
                nkb = 4 * qc + 4
                first = True
                pend = None
                for kb in range(nkb - 1, -1, -1):
                    off = max(0, kb - 4 * qc) * 128
                    diag = kb >= 4 * qc
                    i2 = n % 2; i3 = n % 3; ip = (n - 1) % 3; n += 1
                    ps, pb = ps_next(C)
                    if ps is po:
                        ps, pb = ps_next(C)
                    P.op("pe", lambda e, kb=kb, off=off, ps=ps, hs=hs, qc=qc: e.matmul(
                        ps[:, off:512], lhsT=ka[hs][:, kb * 128:(kb + 1) * 128], rhs=qa[hs][:, qc * 512 + off:(qc + 1) * 512],
                        start=True, stop=True), reads=[b_ka[hs], b_qa[hs]], writes=[pb])
                    P.op("act", lambda e, off=off, ps=ps, i3=i3: e.activation(out=e1[i3][:, off:512], in_=ps[:, off:512], func=AF.Exp, scale=0.125),
                         reads=[pb], writes=[b_e1[i3]])
                    if off > 0:
                        P.op("pool", lambda e, off=off, i3=i3: e.memset(lb[i3][:, 0:off], 0.0), writes=[b_lb[i3]])
                    P.op("act", lambda e, off=off, i3=i3: e.activation(out=lb[i3][:, off:512], in_=e1[i3][:, off:512], func=AF.Ln, bias=1.0, scale=1.0),
                         reads=[b_e1[i3]], writes=[b_lb[i3]])
                    if diag:
                        P.op("pool", lambda e, off=off, i3=i3: e.tensor_tensor(out=lb[i3][:, off:off + 128], in0=lb[i3][:, off:off + 128], in1=strict[:], op=ALU.mult),
                             reads=[b_lb[i3], cb], writes=[b_lb[i3]])
                        P.op("pool", lambda e, off=off, i3=i3: e.tensor_tensor(out=e1[i3][:, off:off + 128], in0=e1[i3][:, off:off + 128], in1=strictf[:], op=ALU.mult),
                             reads=[b_e1[i3], cb, b_lb[i3]], writes=[b_e1[i3]])
                    if kb > 0:
                        if first:
                            P.op("dve", lambda e, i3=i3: e.tensor_copy(out=accb[i3][:], in_=lb[i3][:]), reads=[b_lb[i3]], writes=[b_accb[i3]])
                        else:
                            P.op("dve", lambda e, i3=i3, ip=ip: e.tensor_tensor(out=accb[i3][:], in0=accb[ip][:], in1=lb[i3][:], op=ALU.add),
                                 reads=[b_lb[i3], b_accb[ip]], writes=[b_accb[i3]])
                    def stage_b(kb=kb, off=off, diag=diag, i3=i3, ip=ip, i2=i2, first=first, hs=hs, qc=qc, po=po, pob=pob):
                        pw, pwb = ps_next(C)
                        if pw is po:
                            pw, pwb = ps_next(C)
                        P.op("pe", lambda e: e.matmul(pw[:, off:512], lhsT=trii[:], rhs=lb[i3][:, off:512], start=True, stop=first),
                             reads=[b_lb[i3], cb], writes=[pwb], inc=first)
                        if not first:
                            P.op("pe", lambda e: e.matmul(pw[:, off:512], lhsT=nones[:], rhs=accb[ip][:, off:512], start=False, stop=True),
                                 reads=[b_accb[ip], cb], writes=[pwb])
                        P.op("act", lambda e: e.activation(out=ew[i2][:, off:512], in_=pw[:, off:512], func=AF.Exp, scale=0.125),
                             reads=[pwb], writes=[b_ew[i2]])
                        P.op("dve", lambda e: e.tensor_tensor(out=pt[i3][:, off:512], in0=e1[i3][:, off:512], in1=ew[i2][:, off:512], op=ALU.mult),
                             reads=[b_e1[i3], b_ew[i2]], writes=[b_pt[i3]])
                        P.op("pe", lambda e: e.matmul(
                            po[0:64, off:512], lhsT=va[hs][:, kb, :], rhs=pt[i3][:, off:512], start=False, stop=(kb == 0)),
                            reads=[b_va[hs], b_pt[i3]], writes=[pob])
                    if pend is not None:
                        pend()
                    pend = stage_b
                    first = False
                pend()
                pend = None
                oi = qc % 2
                P.op("act", lambda e, po=po, oi=oi: e.activation(out=ost[oi][:], in_=po[0:64, :], func=AF.Copy), reads=[pob], writes=[b_ost[oi]])
                P.dma("sp", lambda e, h=h, qc=qc, oi=oi: e.dma_start(out=scr["oT"][h * 64:(h + 1) * 64, qc * 512:(qc + 1) * 512], in_=ost[oi][:]),
                      reads=[b_ost[oi]])
        P.phase_end()


from contextlib import ExitStack

GH = 4
DK = 128
DV = 256


def gla_A(P, C, xT, S, w_in, gnorm, wgu, b_gate, scr, consts):
    nc = P.nc
    with ExitStack() as st:
        sb = lambda n, s, d: st.enter_context(nc.sbuf_tensor(f"ga_{n}", s, d))
        g_t = sb("g", [128, NKC], F32); b_g = Buf("g")
        P.dma("sp", lambda e: e.dma_start(out=g_t[:], in_=gnorm.rearrange("(c p) -> p c", p=128), allow_slow_non_contiguous=True), writes=[b_g])
        bg = sb("bg", [128, 4], F32); b_bg = Buf("bg")
        P.dma("sp", lambda e: e.dma_start(out=bg[:], in_=b_gate.rearrange("(c p) -> p c", p=128), allow_slow_non_contiguous=True), writes=[b_bg])
        P.op("dve", lambda e: e.tensor_scalar(out=bg[:], in0=bg[:], scalar1=-1.0, scalar2=None, op0=ALU.mult), reads=[b_bg], writes=[b_bg])
        wgu_t = sb("wgu", [16, 512], BF16); b_wgu = Buf("wgu")
        P.dma("pool", lambda e: e.dma_start(out=wgu_t[:], in_=wgu), writes=[b_wgu])
        ident = sb("ident", [128, 128], F32); b_ident = Buf("ident")
        P.dma("sp", lambda e: e.dma_start(out=ident[:], in_=consts["ident"]), writes=[b_ident])
        x_t, b_x, sq_t, b_sq, rstd, b_rstd, h_t, b_h = load_x_norm(P, C, sb, None, 0, g_t, b_g, "ga")
        stg = [sb(f"stg{i}", [128, TT], BF16) for i in range(3)]; b_stg = [Buf(f"stg{i}") for i in range(3)]
        vst = [sb(f"vst{i}", [128, 4, 512], BF16) for i in range(2)]; b_vst = [Buf("vst0"), Buf("vst1")]
        wv_t = [sb(f"wv{i}", [128, NKC, 512], BF16) for i in range(2)]; b_wv = [Buf("wv0"), Buf("wv1")]
        wt = [sb(f"w{i}", [128, NKC, 256], BF16) for i in range(2)]; bw = [Buf("w0"), Buf("w1")]
        wl = sb("wl", [128, NKC, 16], BF16); b_wl = Buf("wl")
        glT = sb("glT", [16, TT], BF16); b_glT = Buf("glT")
        la = [sb(f"la{i}", [128, TT], F32) for i in range(2)]; b_la = [Buf("la0"), Buf("la1")]
        lat = sb("lat", [128, 4, 512], F32); b_lat = Buf("lat")
        xin = xT.rearrange("(c p) t -> p c t", p=128)
        wr = w_in.rearrange("(c p) f -> p c f", p=128)
        P.dma("pool", lambda e: e.dma_start(out=wl[:], in_=wr[:, :, 2048:2064]), writes=[b_wl])
        ns = 0
        for it in range(S // TT):
            t0 = it * TT
            P.dma("sp", lambda e, t0=t0: e.dma_start(out=x_t[:], in_=xin[:, :, t0:t0 + TT]), writes=[b_x])
            emit_norm(P, C, x_t, b_x, sq_t, b_sq, rstd, b_rstd, h_t, b_h, g_t, b_g)
            for dst, col0, ng, fn in ((scr["qT"], 0, 2, AF.Copy), (scr["kT"], 512, 2, AF.Copy), (scr["sg"], 2064, 4, AF.Silu)):
                for g in range(ng):
                    ws = g % 2
                    c0 = col0 + g * 256
                    P.dma("pool", lambda e, ws=ws, c0=c0: e.dma_start(out=wt[ws][:], in_=wr[:, :, c0:c0 + 256]), writes=[bw[ws]])
                    for j in range(2):
                        ci = g * 2 + j
                        ps, pb = ps_next(C)
                        for kc in range(NKC):
                            P.op("pe", lambda e, kc=kc, ws=ws, j=j, ps=ps: e.matmul(
                                ps[:], lhsT=wt[ws][:, kc, j * 128:(j + 1) * 128], rhs=h_t[:, kc, :], start=(kc == 0), stop=(kc == NKC - 1)),
                                reads=[bw[ws], b_h], writes=[pb], inc=(kc == NKC - 1))
                        s3 = ns % 3; ns += 1
                        P.op("act", lambda e, s3=s3, ps=ps, fn=fn: e.activation(out=stg[s3][:], in_=ps[:], func=fn), reads=[pb], writes=[b_stg[s3]])
                        P.dma("sp", lambda e, s3=s3, ci=ci, dst=dst, t0=t0: e.dma_start(out=dst[ci * 128:(ci + 1) * 128, t0:t0 + TT], in_=stg[s3][:]), reads=[b_stg[s3]])
            for cg in range(3):
                ws = cg % 2
                P.dma("pool", lambda e, cg=cg, ws=ws: e.dma_start(out=wv_t[ws][:], in_=wr[:, :, 512 + cg * 512:512 + (cg + 1) * 512]), writes=[b_wv[ws]])
                for tb in range(4):
                    ps, pb = ps_next(C)
                    for kc in range(NKC):
                        P.op("pe", lambda e, kc=kc, tb=tb, ws=ws, ps=ps: e.matmul(
                            ps[:], lhsT=h_t[:, kc, tb * 128:(tb + 1) * 128], rhs=wv_t[ws][:, kc, :], start=(kc == 0), stop=(kc == NKC - 1)),
                            reads=[b_wv[ws], b_h], writes=[pb], inc=(kc == NKC - 1))
                    P.op("act", lambda e, tb=tb, ws=ws, ps=ps: e.activation(out=vst[ws][:, tb, :], in_=ps[:], func=AF.Copy), reads=[pb], writes=[b_vst[ws]])
                P.dma("sp", lambda e, ws=ws, cg=cg, t0=t0: e.dma_start(
                    out=scr["KV"][t0:t0 + TT, cg * 512:(cg + 1) * 512].rearrange("(tb p) c -> p tb c", p=128), in_=vst[ws][:]), reads=[b_vst[ws]])
            ps, pb = ps_next(C)
            for kc in range(NKC):
                P.op("pe", lambda e, kc=kc, ps=ps: e.matmul(ps[0:16, :], lhsT=wl[:, kc, :], rhs=h_t[:, kc, :], start=(kc == 0), stop=(kc == NKC - 1)),
                     reads=[b_wl, b_h], writes=[pb], inc=(kc == NKC - 1))
            P.op("act", lambda e, ps=ps: e.activation(out=glT[:], in_=ps[0:16, :], func=AF.Copy), reads=[pb], writes=[b_glT])
            for hc in range(4):
                ps, pb = ps_next(C)
                P.op("pe", lambda e, hc=hc, ps=ps: e.matmul(ps[:], lhsT=wgu_t[:, hc * 128:(hc + 1) * 128], rhs=glT[:], start=True, stop=True),
                     reads=[b_wgu, b_glT], writes=[pb])
                li = hc % 2
                P.op("act", lambda e, hc=hc, ps=ps, li=li: e.activation(out=la[li][:], in_=ps[:], func=AF.Exp, bias=bg[:, hc:hc + 1], scale=-1.0),
                     reads=[pb, b_bg], writes=[b_la[li]])
                P.op("act", lambda e, li=li: e.activation(out=la[li][:], in_=la[li][:], func=AF.Ln, bias=1.0, scale=1.0), reads=[b_la[li]], writes=[b_la[li]])
                P.op("dve", lambda e, li=li: e.tensor_scalar(out=la[li][:], in0=la[li][:], scalar1=-1.0 / 16.0, scalar2=None, op0=ALU.mult),
                     reads=[b_la[li]], writes=[b_la[li]])
                P.dma("sp", lambda e, hc=hc, li=li, t0=t0: e.dma_start(out=scr["laT"][hc * 128:(hc + 1) * 128, t0:t0 + TT], in_=la[li][:]), reads=[b_la[li]])
                pst, pstb = ps_next(C)
                for tb in range(4):
                    P.op("pe", lambda e, tb=tb, li=li, pst=pst: e.transpose(pst[:, tb * 128:(tb + 1) * 128], la[li][:, tb * 128:(tb + 1) * 128], ident[:]),
                         reads=[b_la[li], b_ident], writes=[pstb], inc=(tb == 3))
                P.op("dve", lambda e, hc=hc, pst=pst: e.tensor_copy(out=lat[:, :, hc * 128:(hc + 1) * 128], in_=pst[:].rearrange("p (a b) -> p a b", a=4)),
                     reads=[pstb], writes=[b_lat])
            P.dma("sp", lambda e, t0=t0: e.dma_start(out=scr["laK"][t0:t0 + TT, :].rearrange("(tb p) c -> p tb c", p=128), in_=lat[:]), reads=[b_lat])
        P.phase_end()


def gla_B(P, C, S, scr, consts):
    nc = P.nc
    NB = S // 128
    NCH = S // 64
    PW = min(2048, S)
    GRP = min(16, NB)
    with ExitStack() as st:
        sb = lambda n, s, d: st.enter_context(nc.sbuf_tensor(f"gb_{n}", s, d))
        cb = Buf("consts")
        m01 = sb("m01", [128, PW], BF16)
        P.dma("sp", lambda e: e.dma_start(out=m01[:], in_=consts["m01"][:, 0:PW]), writes=[cb])
        umat = sb("umat", [128, 128], F32)
        P.dma("sp", lambda e: e.dma_start(out=umat[:], in_=consts["umat"]), writes=[cb])
        bcaus = sb("bcaus", [128, 128], BF16)
        P.dma("sp", lambda e: e.dma_start(out=bcaus[:], in_=consts["bcaus"]), writes=[cb])
        qd = sb("qd", [128, S], BF16); b_qd = Buf("qd")
        kd = sb("kd", [128, S], BF16); b_kd = Buf("kd")
        bT = sb("bT", [128, S], F32); b_bT = Buf("bT")
        tmp = [sb(f"tmp{i}", [128, PW], F32) for i in range(2)]; b_tmp = [Buf("tmp0"), Buf("tmp1")]
        dcol = sb("dcol", [128, NCH], F32); b_dcol = Buf("dcol")
        ktok = sb("ktok", [128, NB, 128], BF16); b_ktok = Buf("ktok")
        vtok = sb("vtok", [128, NB, 256], BF16); b_vtok = Buf("vtok")
        latk = [sb(f"latk{i}", [128, GRP, 128], F32) for i in range(2)]; b_latk = [Buf("latk0"), Buf("latk1")]
        eu = [sb(f"eu{i}", [128, 128], F32) for i in range(2)]; b_eu = [Buf("eu0"), Buf("eu1")]
        ku = sb("ku", [128, NB, 128], BF16); b_ku = Buf("ku")
        att = [sb(f"att{i}", [128, 128], BF16) for i in range(2)]; b_att = [Buf("att0"), Buf("att1")]
        state = sb("state", [128, 256], F32); b_state = Buf("state")
        stb = [sb(f"stb{i}", [128, 256], BF16) for i in range(2)]; b_stb = [Buf("stb0"), Buf("stb1")]
        ost = [sb(f"ost{i}", [128, 2, 512], BF16) for i in range(2)]; b_ost = [Buf("ost0"), Buf("ost1")]
        for h in range(GH):
            P.dma("sp", lambda e, h=h: e.dma_start(out=qd[:], in_=scr["qT"][h * 128:(h + 1) * 128, :]), writes=[b_qd])
            P.dma("sp", lambda e, h=h: e.dma_start(out=kd[:], in_=scr["kT"][h * 128:(h + 1) * 128, :]), writes=[b_kd])
            P.dma("sp", lambda e, h=h: e.dma_start(out=bT[:], in_=scr["laT"][h * 128:(h + 1) * 128, :]), writes=[b_bT])
            P.dma("sp", lambda e, h=h: e.dma_start(out=ktok[:], in_=scr["KV"][:, h * 128:(h + 1) * 128].rearrange("(kb p) d -> p kb d", p=128)), writes=[b_ktok])
            P.dma("sp", lambda e, h=h: e.dma_start(out=vtok[:], in_=scr["KV"][:, 512 + h * 256:512 + (h + 1) * 256].rearrange("(kb p) d -> p kb d", p=128)), writes=[b_vtok])
            for pc in range(S // PW):
                sl = slice(pc * PW, (pc + 1) * PW)
                P.op("dve", lambda e, sl=sl: e.tensor_tensor_scan(out=bT[:, sl], data0=m01[:], data1=bT[:, sl], initial=0.0, op0=ALU.mult, op1=ALU.add),
                     reads=[b_bT, cb], writes=[b_bT])
                ti = pc % 2
                P.op("act", lambda e, sl=sl, ti=ti: e.activation(out=tmp[ti][:], in_=bT[:, sl], func=AF.Exp), reads=[b_bT], writes=[b_tmp[ti]])
                P.op("dve", lambda e, sl=sl, ti=ti: e.scalar_tensor_tensor(out=qd[:, sl], in0=qd[:, sl], scalar=float(DK) ** -0.5, in1=tmp[ti][:], op0=ALU.mult, op1=ALU.mult),
                     reads=[b_qd, b_tmp[ti]], writes=[b_qd])
                P.op("act", lambda e, sl=sl, ti=ti: e.activation(out=tmp[ti][:], in_=bT[:, sl], func=AF.Exp, scale=-1.0), reads=[b_bT, b_qd], writes=[b_tmp[ti]])
                P.op("dve", lambda e, sl=sl, ti=ti: e.tensor_tensor(out=kd[:, sl], in0=kd[:, sl], in1=tmp[ti][:], op=ALU.mult),
                     reads=[b_kd, b_tmp[ti]], writes=[b_kd])
            P.op("act", lambda e: e.activation(out=dcol[:], in_=bT[:].rearrange("p (n c) -> p n c", c=64)[:, :, 63], func=AF.Exp), reads=[b_bT], writes=[b_dcol])
            for g in range(NB // GRP):
                gi = g % 2
                P.dma("sp", lambda e, g=g, gi=gi, h=h: e.dma_start(
                    out=latk[gi][:], in_=scr["laK"][g * GRP * 128:(g + 1) * GRP * 128, h * 128:(h + 1) * 128].rearrange("(kb p) d -> p kb d", p=128)), writes=[b_latk[gi]])
                for j in range(GRP):
                    tb = g * GRP + j
                    ps, pb = ps_next(C)
                    P.op("pe", lambda e, j=j, gi=gi, ps=ps: e.matmul(ps[:, 0:128], lhsT=umat[:], rhs=latk[gi][:, j, :], start=True, stop=True),
                         reads=[b_latk[gi], cb], writes=[pb])
                    ei = tb % 2
                    P.op("act", lambda e, ei=ei, ps=ps: e.activation(out=eu[ei][:], in_=ps[:, 0:128], func=AF.Exp), reads=[pb], writes=[b_eu[ei]])
                    P.op("dve", lambda e, ei=ei, tb=tb: e.tensor_tensor(out=ku[:, tb, :], in0=ktok[:, tb, :], in1=eu[ei][:], op=ALU.mult),
                         reads=[b_eu[ei], b_ktok], writes=[b_ku])
            P.op("dve", lambda e: e.memset(state[:], 0.0), writes=[b_state])
            P.op("dve", lambda e: e.memset(stb[0][:], 0.0), writes=[b_stb[0]])
            P.op("dve", lambda e: e.memset(stb[1][:], 0.0), writes=[b_stb[1]])
            for tb in range(NB):
                ai = tb % 2
                ps, pb = ps_next(C)
                P.op("pe", lambda e, tb=tb, ps=ps: e.matmul(ps[:, 0:128], lhsT=kd[:, tb * 128:(tb + 1) * 128], rhs=qd[:, tb * 128:(tb + 1) * 128], start=True, stop=True),
                     reads=[b_kd, b_qd], writes=[pb])
                P.op("dve", lambda e, ai=ai, ps=ps: e.tensor_tensor(out=att[ai][:], in0=ps[:, 0:128], in1=bcaus[:], op=ALU.mult), reads=[pb, cb], writes=[b_att[ai]])
                oi = (tb // 4) % 2
                for j in range(2):
                    n = tb * 2 + j
                    si = n % 2
                    r0 = 64 * j
                    po, pob = ps_next(C)
                    for eh in range(2):
                        P.op("pe", lambda e, tb=tb, r0=r0, eh=eh, ai=ai, po=po: e.matmul(
                            po[:, eh * 64:(eh + 1) * 64], lhsT=vtok[r0:r0 + 64, tb, eh * 128:(eh + 1) * 128], rhs=att[ai][r0:r0 + 64, r0:r0 + 64],
                            start=True, stop=False), reads=[b_vtok, b_att[ai]], writes=[pob], inc=False)
                        P.op("pe", lambda e, tb=tb, r0=r0, eh=eh, si=si, po=po: e.matmul(
                            po[:, eh * 64:(eh + 1) * 64], lhsT=stb[si][:, eh * 128:(eh + 1) * 128], rhs=qd[:, tb * 128 + r0:tb * 128 + r0 + 64],
                            start=False, stop=True), reads=[b_stb[si], b_qd], writes=[pob], inc=(eh == 1))
                    c0 = (tb % 4) * 128 + r0
                    P.op("act", lambda e, po=po, oi=oi, c0=c0: e.activation(out=ost[oi][:, :, c0:c0 + 64], in_=po[:, 0:128].rearrange("p (a b) -> p a b", a=2), func=AF.Copy),
                         reads=[pob], writes=[b_ost[oi]])
                    pu, pub = ps_next(C)
                    P.op("pe", lambda e, tb=tb, r0=r0, pu=pu: e.matmul(pu[:, 0:256], lhsT=ku[r0:r0 + 64, tb, :], rhs=vtok[r0:r0 + 64, tb, :], start=True, stop=True),
                         reads=[b_ku, b_vtok], writes=[pub])
                    P.op("dve", lambda e, n=n, pu=pu: e.scalar_tensor_tensor(out=state[:], in0=state[:], scalar=dcol[:, n:n + 1], in1=pu[:, 0:256], op0=ALU.mult, op1=ALU.add),
                         reads=[b_state, b_dcol, pub], writes=[b_state])
                    P.op("dve", lambda e, si=si: e.tensor_copy(out=stb[1 - si][:], in_=state[:]), reads=[b_state], writes=[b_stb[1 - si]])
                if tb % 4 == 3:
                    t0 = (tb // 4) * 512
                    P.dma("sp", lambda e, h=h, oi=oi, t0=t0: e.dma_start(
                        out=scr["oT"][h * 256:(h + 1) * 256, t0:t0 + 512].rearrange("(a p) t -> p a t", p=128), in_=ost[oi][:]), reads=[b_ost[oi]])
        P.phase_end()


def gla_C(P, C, xT, S, w_out, g_out, scr):
    nc = P.nc
    with ExitStack() as st:
        sb = lambda n, s, d: st.enter_context(nc.sbuf_tensor(f"gc_{n}", s, d))
        go = sb("go", [128, 2], F32); b_go = Buf("go")
        P.dma("sp", lambda e: e.dma_start(out=go[:], in_=g_out.rearrange("(c p) -> p c", p=128), allow_slow_non_contiguous=True), writes=[b_go])
        x_t = [sb(f"x{i}", [128, NKC, TT], F32) for i in range(2)]; b_x = [Buf("x0"), Buf("x1")]
        o_t = [sb(f"o{i}", [128, NKC, TT], BF16) for i in range(2)]; b_o = [Buf("o0"), Buf("o1")]
        g_t = [sb(f"gt{i}", [128, NKC, TT], BF16) for i in range(2)]; b_gt = [Buf("gt0"), Buf("gt1")]
        sq = sb("sq", [128, NKC, TT], F32); b_sq = Buf("sq")
        rs = [sb(f"rs{i}", [128, TT], F32) for i in range(2)]; b_rs = [Buf("rs0"), Buf("rs1")]
        wo = [sb(f"wo{i}", [128, NKC, 256], BF16) for i in range(2)]; b_wo = [Buf("wo0"), Buf("wo1")]
        xin = xT.rearrange("(c p) t -> p c t", p=128)
        oin = scr["oT"].rearrange("(c p) t -> p c t", p=128)
        gin = scr["sg"].rearrange("(c p) t -> p c t", p=128)
        wr = w_out.rearrange("(c p) f -> p c f", p=128)
        for it in range(S // TT):
            t0 = it * TT
            xs = it % 2
            P.dma("sp", lambda e, t0=t0, xs=xs: e.dma_start(out=x_t[xs][:], in_=xin[:, :, t0:t0 + TT]), writes=[b_x[xs]])
            P.dma("sp", lambda e, t0=t0, xs=xs: e.dma_start(out=o_t[xs][:], in_=oin[:, :, t0:t0 + TT]), writes=[b_o[xs]])
            P.dma("sp", lambda e, t0=t0, xs=xs: e.dma_start(out=g_t[xs][:], in_=gin[:, :, t0:t0 + TT]), writes=[b_gt[xs]])
            P.op("act", lambda e, xs=xs: e.activation(out=sq[:], in_=o_t[xs][:], func=AF.Square), reads=[b_o[xs]], writes=[b_sq])
            for hh in range(4):
                ps, pb = ps_next(C)
                for j in range(2):
                    P.op("pe", lambda e, hh=hh, j=j, ps=ps: e.matmul(ps[:], lhsT=C.ones_f[:], rhs=sq[:, hh * 2 + j, :], start=(j == 0), stop=(j == 1)),
                         reads=[b_sq, C.b_ones_f], writes=[pb], inc=(j == 1))
                ri = hh % 2
                P.op("dve", lambda e, ri=ri, ps=ps: e.tensor_scalar(out=rs[ri][:], in0=ps[:], scalar1=1.0 / 256, scalar2=1e-6, op0=ALU.mult, op1=ALU.add),
                     reads=[pb], writes=[b_rs[ri]])
                P.op("act", lambda e, ri=ri: e.activation(out=rs[ri][:], in_=rs[ri][:], func=AF.Sqrt), reads=[b_rs[ri]], writes=[b_rs[ri]])
                P.op("dve", lambda e, ri=ri: e.reciprocal(out=rs[ri][:], in_=rs[ri][:]), reads=[b_rs[ri]], writes=[b_rs[ri]])
                for j in range(2):
                    c = hh * 2 + j
                    P.op("dve", lambda e, c=c, j=j, ri=ri, xs=xs: e.scalar_tensor_tensor(out=o_t[xs][:, c, :], in0=o_t[xs][:, c, :], scalar=go[:, j:j + 1], in1=rs[ri][:],
                                                                                       op0=ALU.mult, op1=ALU.mult), reads=[b_o[xs], b_go, b_rs[ri]], writes=[b_o[xs]])
            P.op("pool", lambda e, xs=xs: e.tensor_tensor(out=o_t[xs][:], in0=o_t[xs][:], in1=g_t[xs][:], op=ALU.mult), reads=[b_o[xs], b_gt[xs]], writes=[b_o[xs]])
            for g in range(4):
                ws = g % 2
                P.dma("pool", lambda e, ws=ws, g=g: e.dma_start(out=wo[ws][:], in_=wr[:, :, g * 256:(g + 1) * 256]), writes=[b_wo[ws]])
                for j in range(2):
                    dc = g * 2 + j
                    ps, pb = ps_next(C)
                    for kc in range(NKC):
                        P.op("pe", lambda e, kc=kc, ws=ws, j=j, ps=ps, xs=xs: e.matmul(
                            ps[:], lhsT=wo[ws][:, kc, j * 128:(j + 1) * 128], rhs=o_t[xs][:, kc, :], start=(kc == 0), stop=(kc == NKC - 1)),
                            reads=[b_wo[ws], b_o[xs]], writes=[pb], inc=(kc == NKC - 1))
                    P.op("dve", lambda e, dc=dc, ps=ps, xs=xs: e.tensor_tensor(out=x_t[xs][:, dc, :], in0=x_t[xs][:, dc, :], in1=ps[:], op=ALU.add),
                         reads=[pb, b_x[xs]], writes=[b_x[xs]])
            P.dma("sp", lambda e, t0=t0, xs=xs: e.dma_start(out=xin[:, :, t0:t0 + TT], in_=x_t[xs][:]), reads=[b_x[xs]])
        P.phase_end()


from contextlib import ExitStack

NG = 4


def nsa_A(P, C, xT, S, w_in, gnorm, g_q, g_k, scr, consts):
    nc = P.nc
    with ExitStack() as st:
        sb = lambda n, s, d: st.enter_context(nc.sbuf_tensor(f"na_{n}", s, d))
        g_t = sb("g", [128, NKC], F32); b_g = Buf("g")
        P.dma("sp", lambda e: e.dma_start(out=g_t[:], in_=gnorm.rearrange("(c p) -> p c", p=128), allow_slow_non_contiguous=True), writes=[b_g])
        gq = sb("gq", [128, 1], F32); gk = sb("gk", [128, 1], F32); b_gqk = Buf("gqk")
        for half in range(2):
            P.dma("sp", lambda e, half=half: e.dma_start(out=gq[half * 64:(half + 1) * 64, :], in_=g_q.rearrange("(p o) -> p o", o=1), allow_slow_non_contiguous=True), writes=[b_gqk])
            P.dma("sp", lambda e, half=half: e.dma_start(out=gk[half * 64:(half + 1) * 64, :], in_=g_k.rearrange("(p o) -> p o", o=1), allow_slow_non_contiguous=True), writes=[b_gqk])
        bones = sb("bones", [128, 128], F32); b_bones = Buf("bones")
        P.op("pool", lambda e: e.memset(bones[:], 0.0), writes=[b_bones])
        P.op("pool", lambda e: e.memset(bones[0:64, 0:64], 1.0), writes=[b_bones])
        P.op("pool", lambda e: e.memset(bones[64:128, 64:128], 1.0), writes=[b_bones])
        pm = sb("pm", [128, 128], BF16)
        P.dma("sp", lambda e: e.dma_start(out=pm[:], in_=consts["pm"]), writes=[b_bones])
        x_t, b_x, sq_t, b_sq, rstd, b_rstd, h_t, b_h = load_x_norm(P, C, sb, None, 0, g_t, b_g, "na")
        cos_t = sb("cos", [128, TT], F32); sin_t = sb("sin", [128, TT], F32); b_cs = Buf("cs")
        sqq = [sb(f"sqq{i}", [128, TT], F32) for i in range(2)]; b_sqq = [Buf("sqq0"), Buf("sqq1")]
        rr = [sb(f"rr{i}", [128, TT], F32) for i in range(2)]; b_rr = [Buf("rr0"), Buf("rr1")]
        qn = [sb(f"qn{i}", [128, TT], BF16) for i in range(2)]; b_qn = [Buf("qn0"), Buf("qn1")]
        t1 = [sb(f"t1{i}", [128, TT], F32) for i in range(2)]; b_t1 = [Buf("t10"), Buf("t11")]
        t2 = [sb(f"t2{i}", [128, TT], F32) for i in range(2)]; b_t2 = [Buf("t20"), Buf("t21")]
        stg = [sb(f"stg{i}", [128, TT], BF16) for i in range(3)]; b_stg = [Buf(f"stg{i}") for i in range(3)]
        gst = sb("gst", [48, TT], F32); b_gst = Buf("gst")
        vst = [sb(f"vst{i}", [128, 4, 512], BF16) for i in range(2)]; b_vst = [Buf("vst0"), Buf("vst1")]
        wv_t = [sb(f"wv{i}", [128, NKC, 512], BF16) for i in range(2)]; b_wv = [Buf("wv0"), Buf("wv1")]
        wt = [sb(f"w{i}", [128, NKC, 256], BF16) for i in range(2)]; bw = [Buf("w0"), Buf("w1")]
        wgt = sb("wgt", [128, NKC, 48], BF16); b_wgt = Buf("wgt")
        xin = xT.rearrange("(c p) t -> p c t", p=128)
        wr = w_in.rearrange("(c p) f -> p c f", p=128)
        P.dma("pool", lambda e: e.dma_start(out=wgt[:], in_=wr[:, :, 2560:2608]), writes=[b_wgt])
        ns = 0; nq = 0; ng_ = 0
        jobs = [(scr["qT"], 0, 0, 1024, "q"), (scr["kT3"], 0, 1024, 256, "k"), (scr["vcT"], 0, 1280, 256, "raw"),
                (scr["kT3"], 256, 1536, 256, "k"), (scr["kT3"], 512, 2048, 256, "k")]
        for it in range(S // TT):
            t0 = it * TT
            P.dma("sp", lambda e, t0=t0: e.dma_start(out=x_t[:], in_=xin[:, :, t0:t0 + TT]), writes=[b_x])
            P.dma("sp", lambda e, t0=t0: e.dma_start(out=cos_t[:], in_=consts["cos"][:, t0:t0 + TT]), writes=[b_cs])
            P.dma("sp", lambda e, t0=t0: e.dma_start(out=sin_t[:], in_=consts["sin"][:, t0:t0 + TT]), writes=[b_cs])
            emit_norm(P, C, x_t, b_x, sq_t, b_sq, rstd, b_rstd, h_t, b_h, g_t, b_g)
            for dst, row0, col0, ncols, mode in jobs:
                for g in range(ncols // 256):
                    ws = ng_ % 2; ng_ += 1
                    c0 = col0 + g * 256
                    P.dma("pool", lambda e, ws=ws, c0=c0: e.dma_start(out=wt[ws][:], in_=wr[:, :, c0:c0 + 256]), writes=[bw[ws]])
                    for j in range(2):
                        ci = g * 2 + j
                        ps, pb = ps_next(C)
                        for kc in range(NKC):
                            P.op("pe", lambda e, kc=kc, ws=ws, j=j, ps=ps: e.matmul(
                                ps[:], lhsT=wt[ws][:, kc, j * 128:(j + 1) * 128], rhs=h_t[:, kc, :], start=(kc == 0), stop=(kc == NKC - 1)),
                                reads=[bw[ws], b_h], writes=[pb], inc=(kc == NKC - 1))
                        s3 = ns % 3; ns += 1
                        r0 = row0 + ci * 128
                        if mode == "raw":
                            P.op("act", lambda e, s3=s3, ps=ps: e.activation(out=stg[s3][:], in_=ps[:], func=AF.Copy), reads=[pb], writes=[b_stg[s3]])
                        else:
                            gcol = gq if mode == "q" else gk
                            i = nq % 2; nq += 1
                            P.op("act", lambda e, i=i, ps=ps: e.activation(out=sqq[i][:], in_=ps[:], func=AF.Square), reads=[pb], writes=[b_sqq[i]])
                            ps2, pb2 = ps_next(C)
                            P.op("pe", lambda e, i=i, ps2=ps2: e.matmul(ps2[:], lhsT=bones[:], rhs=sqq[i][:], start=True, stop=True),
                                 reads=[b_bones, b_sqq[i]], writes=[pb2])
                            P.op("dve", lambda e, i=i, ps2=ps2: e.tensor_scalar(out=rr[i][:], in0=ps2[:], scalar1=1.0 / DH, scalar2=1e-6, op0=ALU.mult, op1=ALU.add),
                                 reads=[pb2], writes=[b_rr[i]])
                            P.op("act", lambda e, i=i: e.activation(out=rr[i][:], in_=rr[i][:], func=AF.Sqrt), reads=[b_rr[i]], writes=[b_rr[i]])
                            P.op("dve", lambda e, i=i: e.reciprocal(out=rr[i][:], in_=rr[i][:]), reads=[b_rr[i]], writes=[b_rr[i]])
                            P.op("dve", lambda e, i=i, ps=ps, gcol=gcol: e.scalar_tensor_tensor(out=qn[i][:], in0=ps[:], scalar=gcol[:, 0:1], in1=rr[i][:], op0=ALU.mult, op1=ALU.mult),
                                 reads=[pb, b_gqk, b_rr[i]], writes=[b_qn[i]])
                            ps3, pb3 = ps_next(C)
                            P.op("pe", lambda e, i=i, ps3=ps3: e.matmul(ps3[:], lhsT=pm[:], rhs=qn[i][:], start=True, stop=True),
                                 reads=[b_bones, b_qn[i]], writes=[pb3])
                            P.op("pool", lambda e, i=i: e.tensor_tensor(out=t1[i][:], in0=qn[i][:], in1=cos_t[:], op=ALU.mult), reads=[b_qn[i], b_cs], writes=[b_t1[i]])
                            P.op("dve", lambda e, i=i, ps3=ps3: e.tensor_tensor(out=t2[i][:], in0=ps3[:], in1=sin_t[:], op=ALU.mult), reads=[pb3, b_cs], writes=[b_t2[i]])
                            P.op("dve", lambda e, i=i, s3=s3: e.tensor_tensor(out=stg[s3][:], in0=t1[i][:], in1=t2[i][:], op=ALU.add),
                                 reads=[b_t1[i], b_t2[i]], writes=[b_stg[s3]])
                        P.dma("sp", lambda e, s3=s3, r0=r0, dst=dst, t0=t0: e.dma_start(out=dst[r0:r0 + 128, t0:t0 + TT], in_=stg[s3][:]), reads=[b_stg[s3]])
            for cg in range(3):
                ws = cg % 2
                P.dma("pool", lambda e, cg=cg, ws=ws: e.dma_start(out=wv_t[ws][:], in_=wr[:, :, 1024 + cg * 512:1024 + (cg + 1) * 512]), writes=[b_wv[ws]])
                for tb in range(4):
                    ps, pb = ps_next(C)
                    for kc in range(NKC):
                        P.op("pe", lambda e, kc=kc, tb=tb, ws=ws, ps=ps: e.matmul(
                            ps[:], lhsT=h_t[:, kc, tb * 128:(tb + 1) * 128], rhs=wv_t[ws][:, kc, :], start=(kc == 0), stop=(kc == NKC - 1)),
                            reads=[b_wv[ws], b_h], writes=[pb], inc=(kc == NKC - 1))
                    P.op("act", lambda e, tb=tb, ws=ws, ps=ps: e.activation(out=vst[ws][:, tb, :], in_=ps[:], func=AF.Copy), reads=[pb], writes=[b_vst[ws]])
                P.dma("sp", lambda e, ws=ws, cg=cg, t0=t0: e.dma_start(
                    out=scr["KV"][t0:t0 + TT, cg * 512:(cg + 1) * 512].rearrange("(tb p) c -> p tb c", p=128), in_=vst[ws][:]), reads=[b_vst[ws]])
            ps, pb = ps_next(C)
            for kc in range(NKC):
                P.op("pe", lambda e, kc=kc, ps=ps: e.matmul(ps[0:48, :], lhsT=wgt[:, kc, :], rhs=h_t[:, kc, :], start=(kc == 0), stop=(kc == NKC - 1)),
                     reads=[b_wgt, b_h], writes=[pb], inc=(kc == NKC - 1))
            P.op("act", lambda e, ps=ps: e.activation(out=gst[:], in_=ps[0:48, :], func=AF.Sigmoid), reads=[pb], writes=[b_gst])
            P.dma("sp", lambda e, t0=t0: e.dma_start(out=scr["gT"][:, t0:t0 + TT], in_=gst[:]), reads=[b_gst])
        P.phase_end()


def nsa_B(P, C, S, scr, consts, pos_k, pos_v, w_ck, w_cv, g_k, dbg=None):
    nc = P.nc
    NB = S // 128
    NCMP = (S - 32) // 16 + 1
    NNC = (NCMP + 127) // 128
    with ExitStack() as st:
        sb = lambda n, s, d: st.enter_context(nc.sbuf_tensor(f"nb_{n}", s, d))
        cb = Buf("consts")
        identb = sb("identb", [128, 128], BF16)
        P.dma("sp", lambda e: e.dma_start(out=identb[:], in_=consts["identb"]), writes=[cb])
        negtri4 = sb("negtri4", [128, 512], BF16)
        P.dma("sp", lambda e: e.dma_start(out=negtri4[:], in_=consts["negtri4"]), writes=[cb])
        negle4 = sb("negle4", [128, 512], BF16)
        P.dma("sp", lambda e: e.dma_start(out=negle4[:], in_=consts["negle4"]), writes=[cb])
        emat = sb("emat", [128, S], BF16)
        P.dma("sp", lambda e: e.dma_start(out=emat[:], in_=consts["emat"][:, 0:S]), writes=[cb])
        cover = sb("cover", [128, NNC, 128], F32)
        P.dma("sp", lambda e: e.dma_start(out=cover[:], in_=consts["cover"][0:NNC * 128, :].rearrange("(c p) j -> p c j", p=128)), writes=[cb])
        wck = sb("wck", [64, 32, 64], BF16); wcv = sb("wcv", [64, 32, 64], BF16)
        P.dma("pool", lambda e: e.dma_start(out=wck[:], in_=w_ck.rearrange("l d e -> d l e")), writes=[cb])
        P.dma("pool", lambda e: e.dma_start(out=wcv[:], in_=w_cv.rearrange("l d e -> d l e")), writes=[cb])
        pkT = sb("pkT", [64, 32], BF16); pvT = sb("pvT", [64, 32], BF16)
        P.dma("pool", lambda e: e.dma_start(out=pkT[:], in_=pos_k.rearrange("l d -> d l"), allow_slow_non_contiguous=True), writes=[cb])
        P.dma("pool", lambda e: e.dma_start(out=pvT[:], in_=pos_v.rearrange("l d -> d l"), allow_slow_non_contiguous=True), writes=[cb])
        gk = sb("gk", [64, 1], F32)
        P.dma("sp", lambda e: e.dma_start(out=gk[:], in_=g_k.rearrange("(p o) -> p o", o=1), allow_slow_non_contiguous=True), writes=[cb])
        onesb = sb("onesb", [1, 128], BF16)
        P.op("pool", lambda e: e.memset(onesb[:], 1.0), writes=[cb])
        bk = sb("bk", [64, 1], F32); bvrow = sb("bvrow", [1, 64], BF16); b_bias = Buf("bias")
        ps, pb = ps_next(C)
        for l in range(32):
            P.op("pe", lambda e, l=l, ps=ps: e.matmul(ps[0:64, 0:1], lhsT=wck[:, l, :], rhs=pkT[:, l:l + 1], start=(l == 0), stop=(l == 31)),
                 reads=[cb], writes=[pb], inc=(l == 31))
        P.op("dve", lambda e, ps=ps: e.tensor_copy(out=bk[:], in_=ps[0:64, 0:1]), reads=[pb], writes=[b_bias])
        ps, pb = ps_next(C)
        for l in range(32):
            P.op("pe", lambda e, l=l, ps=ps: e.matmul(ps[0:1, 0:64], lhsT=pvT[:, l:l + 1], rhs=wcv[:, l, :], start=(l == 0), stop=(l == 31)),
                 reads=[cb], writes=[pb], inc=(l == 31))
        P.op("dve", lambda e, ps=ps: e.tensor_copy(out=bvrow[:], in_=ps[0:1, 0:64]), reads=[pb], writes=[b_bias])
        kcT = sb("kcT", [64, S], BF16); vcT = sb("vcT", [64, S], BF16); b_kv = Buf("kv")
        ksT = sb("ksT", [64, S], BF16); kwT = sb("kwT", [64, S], BF16)
        vsa = sb("vsa", [128, NB, 65], BF16); vwa = sb("vwa", [128, NB, 65], BF16); vca = sb("vca", [128, NNC, 65], BF16); b_vca = Buf("vca")
        P.op("pool", lambda e: e.memset(vsa[:, :, 64:65], 1.0), writes=[b_kv])
        P.op("pool", lambda e: e.memset(vwa[:, :, 64:65], 1.0), writes=[b_kv])
        P.op("pool", lambda e: e.memset(vca[:], 0.0), writes=[b_vca])
        P.op("pool", lambda e: e.memset(vca[:, :, 64:65], 1.0), writes=[b_vca])
        kcm = sb("kcm", [64, NNC * 128], BF16); b_kcm = Buf("kcm")
        P.op("pool", lambda e: e.memset(kcm[:], 0.0), writes=[b_kcm])
        thr = sb("thr", [128, 1], F32)
        sqc = sb("sqc", [64, 512], F32); rrc = sb("rrc", [64, 512], F32); kcf = sb("kcf", [64, 512], F32); b_c1 = Buf("c1")
        qblk = [sb(f"qblk{i}", [64, 4, 128], BF16) for i in range(2)]; b_qblk = [Buf("qb0"), Buf("qb1")]
        cm4 = [sb(f"cm4{i}", [128, NNC, 4, 128], BF16) for i in range(2)]; b_cm4 = [Buf("cm0"), Buf("cm1")]
        alw = [sb(f"alw{i}", [128, 128], F32) for i in range(2)]; adc = [sb(f"adc{i}", [128, 128], F32) for i in range(2)]; b_al = [Buf("al0"), Buf("al1")]
        grow = [sb(f"grow{i}", [65, 3, 4, 128], F32) for i in range(2)]; b_grow = [Buf("gr0"), Buf("gr1")]
        pcf = [sb(f"pcf{i}", [128, 512], F32) for i in range(NNC)]; b_pcf = [Buf(f"pcf{i}") for i in range(NNC)]
        pcb = [sb(f"pcb{i}", [128, 512], BF16) for i in range(2)]; b_pcb = [Buf("pcb0"), Buf("pcb1")]
        osb = [sb(f"osb{i}", [65, 512], F32) for i in range(4)]; b_osb = [Buf(f"osb{i}") for i in range(4)]
        rec = [sb(f"rec{i}", [65, 512], F32) for i in range(4)]; b_rec = [Buf(f"rec{i}") for i in range(4)]
        impf = sb("impf", [128, 128], F32); imp2 = sb("imp2", [128, 128], F32); mx8 = sb("mx8", [128, 8], F32); mx8b = sb("mx8b", [128, 8], F32); b_imp = Buf("imp")
        mbq = sb("mbq", [128, 128], BF16); b_mbq = Buf("mbq")
        mbT4s = [sb(f"mbT4{i}", [128, 4, 128], BF16) for i in range(2)]; b_mbTs = [Buf("mbT0"), Buf("mbT1")]
        pt = [sb(f"pt{i}", [128, 512], BF16) for i in range(3)]; b_pt = [Buf(f"pt{i}") for i in range(3)]
        oacc = sb("oacc", [64, 512], F32); otmp = sb("otmp", [64, 512], F32); b_oacc = Buf("oacc")
        ost = [sb(f"ost{i}", [64, 512], BF16) for i in range(2)]; b_ost = [Buf("ost0"), Buf("ost1")]
        nptl = [0]
        for g in range(NG):
            P.dma("sp", lambda e, g=g: e.dma_start(out=kcT[:], in_=scr["kT3"][g * 64:(g + 1) * 64, :]), writes=[b_kv])
            P.dma("sp", lambda e, g=g: e.dma_start(out=ksT[:], in_=scr["kT3"][256 + g * 64:256 + (g + 1) * 64, :]), writes=[b_kv])
            P.dma("sp", lambda e, g=g: e.dma_start(out=kwT[:], in_=scr["kT3"][512 + g * 64:512 + (g + 1) * 64, :]), writes=[b_kv])
            P.dma("sp", lambda e, g=g: e.dma_start(out=vcT[:], in_=scr["vcT"][g * 64:(g + 1) * 64, :]), writes=[b_kv])
            P.dma("sp", lambda e, g=g: e.dma_start(out=vsa[:, :, 0:64], in_=scr["KV"][:, 768 + g * 64:768 + (g + 1) * 64].rearrange("(kb p) d -> p kb d", p=128)), writes=[b_kv])
            P.dma("sp", lambda e, g=g: e.dma_start(out=vwa[:, :, 0:64], in_=scr["KV"][:, 1280 + g * 64:1280 + (g + 1) * 64].rearrange("(kb p) d -> p kb d", p=128)), writes=[b_kv])
            for c0 in range(0, NCMP, 512):
                nn = min(512, NCMP - c0)
                ps, pb = ps_next(C)
                for l in range(32):
                    P.op("pe", lambda e, l=l, ps=ps, c0=c0, nn=nn: e.matmul(ps[0:64, 0:nn], lhsT=wck[:, l, :], rhs=kcT[:, c0 * 16 + l:c0 * 16 + l + (nn - 1) * 16 + 1:16],
                                                                         start=(l == 0), stop=(l == 31)), reads=[cb, b_kv], writes=[pb], inc=(l == 31))
                P.op("dve", lambda e, ps=ps, nn=nn: e.tensor_scalar(out=kcf[:, 0:nn], in0=ps[0:64, 0:nn], scalar1=bk[:, 0:1], scalar2=None, op0=ALU.add),
                     reads=[pb, b_bias], writes=[b_c1])
                P.op("act", lambda e, nn=nn: e.activation(out=sqc[:, 0:nn], in_=kcf[:, 0:nn], func=AF.Square), reads=[b_c1], writes=[b_c1])
                ps2, pb2 = ps_next(C)
                P.op("pe", lambda e, ps2=ps2, nn=nn: e.matmul(ps2[0:64, 0:nn], lhsT=C.ones_f[0:64, 0:64], rhs=sqc[:, 0:nn], start=True, stop=True),
                     reads=[b_c1, C.b_ones_f], writes=[pb2])
                P.op("dve", lambda e, ps2=ps2, nn=nn: e.tensor_scalar(out=rrc[:, 0:nn], in0=ps2[0:64, 0:nn], scalar1=1.0 / 64, scalar2=1e-6, op0=ALU.mult, op1=ALU.add),
                     reads=[pb2], writes=[b_c1])
                P.op("act", lambda e, nn=nn: e.activation(out=rrc[:, 0:nn], in_=rrc[:, 0:nn], func=AF.Sqrt), reads=[b_c1], writes=[b_c1])
                P.op("dve", lambda e, nn=nn: e.reciprocal(out=rrc[:, 0:nn], in_=rrc[:, 0:nn]), reads=[b_c1], writes=[b_c1])
                P.op("dve", lambda e, nn=nn, c0=c0: e.scalar_tensor_tensor(out=kcm[:, c0:c0 + nn], in0=kcf[:, 0:nn], scalar=gk[:, 0:1], in1=rrc[:, 0:nn], op0=ALU.mult, op1=ALU.mult),
                     reads=[b_c1, cb], writes=[b_kcm])
            for nch in range(NNC):
                n0 = nch * 128
                nn = min(128, NCMP - n0)
                ps, pb = ps_next(C)
                for l in range(32):
                    P.op("pe", lambda e, l=l, ps=ps, n0=n0, nn=nn: e.matmul(ps[0:nn, 0:64], lhsT=vcT[:, n0 * 16 + l:n0 * 16 + l + (nn - 1) * 16 + 1:16], rhs=wcv[:, l, :],
                                                                         start=(l == 0), stop=False), reads=[cb, b_kv], writes=[pb], inc=False)
                P.op("pe", lambda e, ps=ps, nn=nn: e.matmul(ps[0:nn, 0:64], lhsT=onesb[0:1, 0:nn], rhs=bvrow[:], start=False, stop=True), reads=[cb, b_bias], writes=[pb])
                P.op("act", lambda e, ps=ps, nn=nn, nch=nch: e.activation(out=vca[0:nn, nch, 0:64], in_=ps[0:nn, 0:64], func=AF.Copy), reads=[pb], writes=[b_vca])
            if NCMP % 128:
                pass
            def cmp_stage(qb):
                    t0 = qb * 128
                    qi = qb % 2
                    o0 = 3 * qi
                    P.dma("sp", lambda e, g=g, qi=qi, t0=t0: e.dma_start(out=qblk[qi][:], in_=scr["qT"][g * 256:(g + 1) * 256, t0:t0 + 128].rearrange("(p d) t -> d p t", d=64)),
                          writes=[b_qblk[qi]])
                    ncn = min(NNC, (8 * qb + 6) // 128 + 1)
                    for p4 in range(4):
                        P.dma("sp", lambda e, qi=qi, t0=t0, p4=p4, ncn=ncn: e.dma_start(
                            out=cm4[qi][:, 0:ncn, p4, :], in_=consts["cmask"][0:ncn * 128, t0:t0 + 128].rearrange("(c p) t -> p c t", p=128)), writes=[b_cm4[qi]])
                    P.dma("sp", lambda e, qi=qi, qb=qb: e.dma_start(out=alw[qi][:], in_=consts["allow"][qb]), writes=[b_al[qi]])
                    P.dma("sp", lambda e, qi=qi, qb=qb: e.dma_start(out=adc[qi][:], in_=consts["addc"][qb]), writes=[b_al[qi]])
                    P.dma("sp", lambda e, qi=qi, g=g, t0=t0: e.dma_start(out=grow[qi][64:65, :, :, :],
                                                                         in_=scr["gT"].rearrange("(h c) t -> c h t", c=3)[:, g * 4:(g + 1) * 4, t0:t0 + 128].rearrange("(o c) h t -> o c h t", o=1)),
                          writes=[b_grow[qi]])
                    qr = qblk[qi][:].rearrange("d p t -> d (p t)")
                    poc, pocb = ps_next(C)
                    for nch in range(ncn):
                        ps, pb = ps_next(C)
                        if ps is poc:
                            ps, pb = ps_next(C)
                        P.op("pe", lambda e, nch=nch, ps=ps, qr=qr: e.matmul(ps[:], lhsT=kcm[:, nch * 128:(nch + 1) * 128], rhs=qr, start=True, stop=False),
                             reads=[b_kcm, b_qblk[qi]], writes=[pb], inc=False)
                        P.op("pe", lambda e, nch=nch, ps=ps, qi=qi: e.matmul(ps[:], lhsT=identb[:], rhs=cm4[qi][:, nch, :, :].rearrange("p a b -> p (a b)"), start=False, stop=True),
                             reads=[cb, b_cm4[qi]], writes=[pb])
                        P.op("act", lambda e, nch=nch, ps=ps: e.activation(out=pcf[nch][:], in_=ps[:], func=AF.Exp, scale=0.125), reads=[pb], writes=[b_pcf[nch]])
                        bi = nch % 2
                        P.op("pool", lambda e, nch=nch, bi=bi: e.tensor_copy(out=pcb[bi][:], in_=pcf[nch][:]), reads=[b_pcf[nch]], writes=[b_pcb[bi]])
                        P.op("pe", lambda e, nch=nch, bi=bi, poc=poc, ncn=ncn: e.matmul(poc[0:65, :], lhsT=vca[:, nch, :], rhs=pcb[bi][:], start=(nch == 0), stop=(nch == ncn - 1)),
                             reads=[b_vca, b_pcb[bi]], writes=[pocb])
                    P.op("act", lambda e, poc=poc: e.activation(out=osb[o0][:], in_=poc[0:65, :], func=AF.Copy), reads=[pocb], writes=[b_osb[o0]])
                    P.op("dve", lambda e: e.tensor_scalar(out=rec[o0][64:65, :], in0=osb[o0][64:65, :], scalar1=1e-30, scalar2=None, op0=ALU.add), reads=[b_osb[o0]], writes=[b_rec[o0]])
                    P.op("dve", lambda e: e.reciprocal(out=rec[o0][64:65, :], in_=rec[o0][64:65, :]), reads=[b_rec[o0]], writes=[b_rec[o0]])
                    pbc, pbcb = ps_next(C)
                    P.op("pe", lambda e, pbc=pbc: e.matmul(pbc[:], lhsT=C.ones_f[64:65, :], rhs=rec[o0][64:65, :], start=True, stop=True), reads=[b_rec[o0], C.b_ones_f], writes=[pbcb])
                    for nch in range(ncn):
                        P.op("dve", lambda e, nch=nch, pbc=pbc: e.tensor_tensor(out=pcf[nch][:], in0=pcf[nch][:], in1=pbc[:], op=ALU.mult), reads=[b_pcf[nch], pbcb], writes=[b_pcf[nch]])
                    pim, pimb = ps_next(C)
                    k = 0
                    for nch in range(ncn):
                        for p4 in range(4):
                            P.op("pe", lambda e, nch=nch, p4=p4, pim=pim, k=k, ncn=ncn: e.matmul(pim[:, 0:128], lhsT=pcf[nch][:, p4 * 128:(p4 + 1) * 128], rhs=cover[:, nch, :],
                                                                                              start=(k == 0), stop=(k == ncn * 4 - 1)),
                                 reads=[b_pcf[nch], cb], writes=[pimb], inc=(k == ncn * 4 - 1))
                            k += 1
                    P.op("dve", lambda e, pim=pim, qi=qi: e.tensor_tensor(out=impf[:], in0=pim[:, 0:128], in1=alw[qi][:], op=ALU.mult), reads=[pimb, b_al[qi]], writes=[b_imp])
                    P.op("dve", lambda e, qi=qi: e.tensor_tensor(out=impf[:], in0=impf[:], in1=adc[qi][:], op=ALU.add), reads=[b_imp, b_al[qi]], writes=[b_imp])
                    P.op("dve", lambda e: e.max(out=mx8[:], in_=impf[:]), reads=[b_imp], writes=[b_imp], strict=True)
                    P.op("dve", lambda e: e.match_replace(out=imp2[:], in_to_replace=mx8[:], in_values=impf[:], imm_value=-2.0), reads=[b_imp], writes=[b_imp], strict=True)
                    P.op("dve", lambda e: e.max(out=mx8b[:], in_=imp2[:]), reads=[b_imp], writes=[b_imp], strict=True)
                    P.op("dve", lambda e: e.tensor_reduce(out=thr[:], in_=mx8b[:], axis=AX.X, op=ALU.min), reads=[b_imp], writes=[b_imp], strict=True)
                    P.op("dve", lambda e: e.tensor_scalar(out=imp2[:], in0=impf[:], scalar1=thr[:, 0:1], scalar2=None, op0=ALU.is_ge), reads=[b_imp], writes=[b_imp], strict=True)
                    P.op("dve", lambda e: e.tensor_scalar(out=mbq[:], in0=imp2[:], scalar1=1.0, scalar2=240000.0, op0=ALU.subtract, op1=ALU.mult), reads=[b_imp], writes=[b_mbq], strict=True)
                    if dbg is not None and g == 0 and qb == NB - 1 and "dbg" in scr:
                        P.dma("sp", lambda e: e.dma_start(out=scr["dbg"][:, 0:128], in_=impf[:]), reads=[b_imp])
                        P.dma("sp", lambda e: e.dma_start(out=scr["dbg"][:, 128:256], in_=imp2[:]), reads=[b_imp])
                        P.dma("sp", lambda e: e.dma_start(out=scr["dbg"][:, 256:264], in_=mx8[:]), reads=[b_imp])
                        P.dma("sp", lambda e: e.dma_start(out=scr["dbg"][:, 264:272], in_=mx8b[:]), reads=[b_imp])
                        P.dma("sp", lambda e: e.dma_start(out=scr["dbg"][:, 272:273], in_=thr[:], allow_slow_non_contiguous=True), reads=[b_imp])
                    pmt, pmtb = ps_next(C)
                    P.op("pe", lambda e, pmt=pmt: e.matmul(pmt[:, 0:128], lhsT=mbq[:], rhs=identb[:], start=True, stop=True), reads=[b_mbq, cb], writes=[pmtb])
                    for p4 in range(4):
                        P.op("act" if p4 % 2 else "dve", (lambda e, p4=p4, pmt=pmt: e.activation(out=mbT4s[qi][:, p4, :], in_=pmt[:, 0:128], func=AF.Copy)) if p4 % 2 else
                             (lambda e, p4=p4, pmt=pmt: e.tensor_copy(out=mbT4s[qi][:, p4, :], in_=pmt[:, 0:128])), reads=[pmtb], writes=[b_mbTs[qi]])
            def att_stage(qb):
                    t0 = qb * 128
                    qi = qb % 2
                    o0 = 3 * qi
                    poc = None
                    qr = qblk[qi][:].rearrange("d p t -> d (p t)")
                    pos_, posb = ps_next(C)
                    pend = None
                    for kb in range(qb + 1):
                        ps, pb = ps_next(C)
                        while ps is pos_ or ps is poc:
                            ps, pb = ps_next(C)
                        P.op("pe", lambda e, kb=kb, ps=ps, qr=qr: e.matmul(ps[:], lhsT=ksT[:, kb * 128:(kb + 1) * 128], rhs=qr, start=True, stop=False),
                             reads=[b_kv, b_qblk[qi]], writes=[pb], inc=False)
                        if kb == qb:
                            P.op("pe", lambda e, ps=ps: e.matmul(ps[:], lhsT=identb[:], rhs=negtri4[:], start=False, stop=False), reads=[cb], writes=[pb], inc=False)
                        P.op("pe", lambda e, kb=kb, ps=ps: e.matmul(ps[:], lhsT=emat[:, kb * 128:(kb + 1) * 128], rhs=mbT4s[qi][:].rearrange("p a b -> p (a b)"), start=False, stop=True),
                             reads=[cb, b_mbTs[qi]], writes=[pb])
                        pi = nptl[0] % 3; nptl[0] += 1
                        P.op("act", lambda e, ps=ps, pi=pi: e.activation(out=pt[pi][:], in_=ps[:], func=AF.Exp, scale=0.125), reads=[pb], writes=[b_pt[pi]])
                        def pv(kb=kb, pi=pi, pos_=pos_, qb=qb, posb=posb):
                            P.op("pe", lambda e: e.matmul(pos_[0:65, :], lhsT=vsa[:, kb, :], rhs=pt[pi][:], start=(kb == 0), stop=(kb == qb)),
                                 reads=[b_kv, b_pt[pi]], writes=[posb])
                        if pend is not None:
                            pend()
                        pend = pv
                    pend()
                    pow_, powb = ps_next(C)
                    while pow_ is pos_ or pow_ is poc:
                        pow_, powb = ps_next(C)
                    kb0 = max(0, qb - 4)
                    pend = None
                    for kb in range(kb0, qb + 1):
                        ps, pb = ps_next(C)
                        while ps is pos_ or ps is poc or ps is pow_:
                            ps, pb = ps_next(C)
                        msk = negtri4 if kb == qb else (negle4 if kb == qb - 4 else None)
                        P.op("pe", lambda e, kb=kb, ps=ps, qr=qr, msk=msk: e.matmul(ps[:], lhsT=kwT[:, kb * 128:(kb + 1) * 128], rhs=qr, start=True, stop=(msk is None)),
                             reads=[b_kv, b_qblk[qi]], writes=[pb], inc=(msk is None))
                        if msk is not None:
                            P.op("pe", lambda e, ps=ps, msk=msk: e.matmul(ps[:], lhsT=identb[:], rhs=msk[:], start=False, stop=True), reads=[cb], writes=[pb])
                        pi = nptl[0] % 3; nptl[0] += 1
                        P.op("act", lambda e, ps=ps, pi=pi: e.activation(out=pt[pi][:], in_=ps[:], func=AF.Exp, scale=0.125), reads=[pb], writes=[b_pt[pi]])
                        def pv(kb=kb, pi=pi, pow_=pow_, qb=qb, kb0=kb0, powb=powb):
                            P.op("pe", lambda e: e.matmul(pow_[0:65, :], lhsT=vwa[:, kb, :], rhs=pt[pi][:], start=(kb == kb0), stop=(kb == qb)),
                                 reads=[b_kv, b_pt[pi]], writes=[powb])
                        if pend is not None:
                            pend()
                        pend = pv
                    pend()
                    for c, (po_, pob_) in enumerate(((None, None), (pos_, posb), (pow_, powb))):
                        ci = o0 if c == 0 else c
                        if c > 0:
                            P.op("act", lambda e, c=c, ci=ci, po_=po_: e.activation(out=osb[ci][:], in_=po_[0:65, :], func=AF.Copy), reads=[pob_], writes=[b_osb[ci]])
                            P.op("dve", lambda e, c=c, ci=ci: e.tensor_scalar(out=rec[ci][64:65, :], in0=osb[ci][64:65, :], scalar1=1e-30, scalar2=None, op0=ALU.add), reads=[b_osb[ci]], writes=[b_rec[ci]])
                            P.op("dve", lambda e, c=c, ci=ci: e.reciprocal(out=rec[ci][64:65, :], in_=rec[ci][64:65, :]), reads=[b_rec[ci]], writes=[b_rec[ci]])
                        P.op("dve", lambda e, c=c, ci=ci, qi=qi: e.tensor_tensor(out=rec[ci][64:65, :], in0=rec[ci][64:65, :], in1=grow[qi][64:65, c, :, :].rearrange("o h t -> o (h t)"), op=ALU.mult),
                             reads=[b_rec[ci], b_grow[qi]], writes=[b_rec[ci]])
                        if dbg is not None and c != dbg:
                            P.op("dve", lambda e, c=c, ci=ci: e.memset(rec[ci][64:65, :], 0.0), reads=[b_rec[ci]], writes=[b_rec[ci]])
                        pbc2, pbc2b = ps_next(C)
                        while pbc2 is pos_ or pbc2 is pow_:
                            pbc2, pbc2b = ps_next(C)
                        P.op("pe", lambda e, c=c, ci=ci, pbc2=pbc2: e.matmul(pbc2[0:64, :], lhsT=C.ones_f[64:65, 0:64], rhs=rec[ci][64:65, :], start=True, stop=True),
                             reads=[b_rec[ci], C.b_ones_f], writes=[pbc2b])
                        if c == 0:
                            P.op("dve", lambda e, pbc2=pbc2: e.tensor_tensor(out=oacc[:], in0=osb[o0][0:64, :], in1=pbc2[0:64, :], op=ALU.mult), reads=[b_osb[o0], pbc2b], writes=[b_oacc])
                        else:
                            P.op("dve", lambda e, c=c, ci=ci, pbc2=pbc2: e.tensor_tensor(out=otmp[:], in0=osb[ci][0:64, :], in1=pbc2[0:64, :], op=ALU.mult), reads=[b_osb[ci], pbc2b], writes=[b_oacc])
                            if c == 1:
                                P.op("dve", lambda e: e.tensor_tensor(out=oacc[:], in0=oacc[:], in1=otmp[:], op=ALU.add), reads=[b_oacc], writes=[b_oacc])
                            else:
                                oi = qb % 2
                                P.op("dve", lambda e, oi=oi: e.tensor_tensor(out=ost[oi][:], in0=oacc[:], in1=otmp[:], op=ALU.add), reads=[b_oacc], writes=[b_ost[oi]])
                                P.dma("sp", lambda e, g=g, t0=t0, oi=oi: e.dma_start(out=scr["oT"][g * 256:(g + 1) * 256, t0:t0 + 128].rearrange("(p d) t -> d p t", d=64),
                                                                                    in_=ost[oi][:].rearrange("d (p t) -> d p t", p=4)), reads=[b_ost[oi]])
            cmp_stage(0)
            for qb in range(NB):
                if qb + 1 < NB:
                    cmp_stage(qb + 1)
                att_stage(qb)
        P.phase_end()


bf = ml_dtypes.bfloat16
def nsa_consts(S):
    NB = S // 128
    ncmp = (S - 32) // 16 + 1
    nnc = (ncmp + 127) // 128
    inv = (10000.0 ** (-np.arange(0, 64, 2, dtype=np.float32) / 64)).astype(np.float32)
    ang = np.arange(S, dtype=np.float32)[None, :] * inv[:, None]
    j = np.arange(128) % 32
    cosT = np.cos(ang)[j].astype(np.float32); sinT = np.sin(ang)[j].astype(np.float32)
    pm = np.zeros((128, 128), np.float32)
    for d in range(128):
        if (d % 64) < 32: pm[d + 32, d] = -1.0
        else: pm[d - 32, d] = 1.0
    n = np.arange(nnc * 128)[:, None]; t = np.arange(S)[None, :]
    cmask = np.where((16 * n + 31 <= t) & (n < ncmp), 0.0, -240000.0)
    jj = np.arange(128)[None, :]
    cover = ((n < ncmp) & (16 * n < 64 * jj + 64) & (16 * n + 32 > 64 * jj)).astype(np.float32)
    tq = (np.arange(NB)[:, None, None] * 128 + np.arange(128)[None, :, None]); cur = tq // 64
    jb = np.arange(128)[None, None, :]
    forced = (jb == 0) | (jb == cur) | (jb == cur - 1); allowed = jb <= cur
    allow = (allowed & ~forced).astype(np.float32)
    addc = np.where(forced, 1e9, np.where(allowed, 0.0, -1.0)).astype(np.float32)
    emat = (np.arange(128)[:, None] == (np.arange(S)[None, :] // 64)).astype(np.float32)
    s_ = np.arange(128)[:, None]; t_ = np.arange(128)[None, :]
    negtri = np.where(s_ > t_, -240000.0, 0.0); negle = np.where(s_ <= t_, -240000.0, 0.0)
    return {"cos": cosT, "sin": sinT, "pm": pm.astype(bf), "cmask": cmask.astype(bf), "cover": cover, "allow": allow, "addc": addc,
            "emat": emat.astype(bf), "negtri4": np.tile(negtri, (1, 4)).astype(bf), "negle4": np.tile(negle, (1, 4)).astype(bf),
            "identb": np.eye(128, dtype=np.float32).astype(bf)}


def gla_consts():
    s = np.arange(128)[:, None]; c = np.arange(128)[None, :]
    same = (s // 64) == (c // 64)
    m01 = np.ones((128, 2048), np.float32); m01[:, ::64] = 0
    return {"m01": m01.astype(bf), "umat": (same & (s > c)).astype(np.float32), "bcaus": (same & (s <= c)).astype(np.float32).astype(bf),
            "ident": np.eye(128, dtype=np.float32)}


def sb_consts():
    j = np.arange(128)[:, None]; s = np.arange(128)[None, :]
    return {
        "trii": np.where(j >= s, -8.0, 0.0).astype(bf),
        "nones": np.full((128, 128), -8.0, np.float32).astype(bf),
        "strict": np.where(j < s, 1.0, 0.0).astype(bf),
        "negtri_incl": np.where(j >= s, -240000.0, 0.0).astype(bf),
        "identb": np.eye(128, dtype=np.float32).astype(bf),
    }


SEQ = 8192
NCORE = 4


def all_consts():
    cs = {}
    for pre, d in (("n_", nsa_consts(SEQ)), ("g_", gla_consts()), ("s_", sb_consts())):
        for k, v in d.items():
            cs[pre + k] = v
    cs["f_tri"] = (np.tril(np.ones((128, 128), np.float32), -1) * -240000.0).astype(bf)
    cs["f_ident"] = np.eye(128, dtype=np.float32)
    cs["f_identb"] = np.eye(128, dtype=np.float32).astype(bf)
    return cs


def build_program(inputs, consts):
    S = SEQ
    nc = bass.Bass("TRN2", target_bir_lowering=False)
    W = {}
    for k, v in inputs.items():
        if k == "x":
            continue
        W[k] = nc.dram_tensor(k, list(v.shape), F32, kind="ExternalInput").ap()
    xT = nc.dram_tensor("xT", [D, S], F32, kind="ExternalInput").ap()
    yT = nc.dram_tensor("yT", [D, S], F32, kind="ExternalOutput").ap()
    cs = {k: nc.dram_tensor("c_" + k, list(v.shape), BF16 if v.dtype == bf else F32, kind="ExternalInput").ap() for k, v in consts.items()}
    sub = lambda pre: {k[len(pre):]: v for k, v in cs.items() if k.startswith(pre)}
    dt = lambda n, s, d: nc.dram_tensor("scr_" + n, s, d).ap()
    oT = dt("oT", [D, S], BF16)
    sg = dt("sg", [D, S], BF16)
    qT = dt("qT", [D, S], BF16)
    kT = dt("kT", [D, S], BF16)
    V = dt("V", [S, D], BF16)
    KV = dt("KV", [S, 1536], BF16)
    with ExitStack() as st:
        P = Prog(nc, st)
        C = Common(P, None)
        ng = W["norm_g"]
        ffn_phase(P, C, xT, yT, W["ffn1_w_gate"][0], W["ffn1_w_up"][0], W["ffn1_w_down"][0], ng[0, 0], S, "l0a")
        scr = {"qT": qT, "kT": kT, "V": V, "ls": dt("ls", [16, S], F32), "sg": sg, "crow": dt("crow", [3, 16, S], BF16), "oT": oT}
        fox_A(P, C, yT, S, W["fox_w_in"][0], ng[0, 1], W["fox_b_f"][0], W["fox_g_q"][0], W["fox_g_k"][0], scr)
        fox_B(P, C, S, scr, {"tri": cs["f_tri"], "ident": cs["f_ident"], "identb": cs["f_identb"]})
        mix_C(P, C, yT, S, W["fox_w_out"][0], scr, "sg")
        ffn_phase(P, C, yT, yT, W["ffn2_w_gate"][0], W["ffn2_w_up"][0], W["ffn2_w_down"][0], ng[0, 2], S, "l0b")
        ffn_phase(P, C, yT, yT, W["ffn1_w_gate"][1], W["ffn1_w_up"][1], W["ffn1_w_down"][1], ng[1, 0], S, "l1a")
        scr = {"qT": qT, "kT3": dt("kT3", [768, S], BF16), "vcT": dt("vcT", [256, S], BF16), "KV": KV, "gT": dt("gT", [48, S], F32), "oT": oT}
        ncs = sub("n_")
        nsa_A(P, C, yT, S, W["nsa_w_in"][0], ng[1, 1], W["nsa_g_q"][0], W["nsa_g_k"][0], scr, ncs)
        nsa_B(P, C, S, scr, ncs, W["nsa_cmp_pos_k"][0], W["nsa_cmp_pos_v"][0], W["nsa_w_cmp_k"][0], W["nsa_w_cmp_v"][0], W["nsa_g_k"][0])
        mix_C(P, C, yT, S, W["nsa_w_out"][0], scr, None)
        ffn_phase(P, C, yT, yT, W["ffn2_w_gate"][1], W["ffn2_w_up"][1], W["ffn2_w_down"][1], ng[1, 2], S, "l1b")
        ffn_phase(P, C, yT, yT, W["ffn1_w_gate"][2], W["ffn1_w_up"][2], W["ffn1_w_down"][2], ng[2, 0], S, "l2a")
        scr = {"qT": qT[0:512], "kT": kT[0:512], "KV": KV, "laT": dt("laT", [512, S], F32), "laK": dt("laK", [S, 512], F32), "sg": sg, "oT": oT}
        gcs = sub("g_")
        gla_A(P, C, yT, S, W["gla_w_in"][0], ng[2, 1], W["gla_w_gate_up"][0], W["gla_b_gate"][0], scr, gcs)
        gla_B(P, C, S, scr, gcs)
        gla_C(P, C, yT, S, W["gla_w_out"][0], W["gla_g_out"][0], scr)
        ffn_phase(P, C, yT, yT, W["ffn2_w_gate"][2], W["ffn2_w_up"][2], W["ffn2_w_down"][2], ng[2, 2], S, "l2b")
        ffn_phase(P, C, yT, yT, W["ffn1_w_gate"][3], W["ffn1_w_up"][3], W["ffn1_w_down"][3], ng[3, 0], S, "l3a")
        scr = {"qT": qT, "kT": kT, "V": V, "oT": oT}
        qkv_A(P, C, yT, S, W["sb_w_in"][0], ng[3, 1], scr)
        sb_B(P, C, S, scr, sub("s_"))
        mix_C(P, C, yT, S, W["sb_w_out"][0], scr, None)
        ffn_phase(P, C, yT, yT, W["ffn2_w_gate"][3], W["ffn2_w_up"][3], W["ffn2_w_down"][3], ng[3, 2], S, "l3b")
        P.finish()
        P.emit()
    return nc


def kernel(**inputs):
    inputs = {k: np.asarray(v) for k, v in inputs.items()}
    x = inputs["x"]
    consts = all_consts()
    nc = build_program(inputs, consts)
    base = {k: np.ascontiguousarray(v, dtype=np.float32) for k, v in inputs.items() if k != "x"}
    for k, v in consts.items():
        base["c_" + k] = v
    in_maps = []
    for b in range(NCORE):
        m = dict(base)
        m["xT"] = np.ascontiguousarray(x[b].T)
        in_maps.append(m)
    res = run_bass_kernel_spmd(nc, in_maps, core_ids=list(range(NCORE)))
    out = np.stack([np.ascontiguousarray(res.results[b]["yT"].T) for b in range(NCORE)], axis=0)
    return out.astype(np.float32)
```

```python
from contextlib import ExitStack
import ml_dtypes
from concourse.bass_utils import run_bass_kernel_spmd


import numpy as np
import concourse.bass as bass
import concourse.mybir as mybir

F32 = mybir.dt.float32
BF16 = mybir.dt.bfloat16
AF = mybir.ActivationFunctionType
ALU = mybir.AluOpType
AX = mybir.AxisListType

ENGS = ["pe", "act", "dve", "pool", "sp"]
NDQ = 6


class Buf:
    __slots__ = ("name", "w", "r")

    def __init__(self, name):
        self.name = name
        self.w = None
        self.r = {}


class Prog:
    def __init__(self, nc, stack):
        self.nc = nc
        self.stack = stack
        self.ops = {e: [] for e in ENGS}
        self.cnt = {}
        self.seen = {e: {} for e in ENGS}
        self.sems = {}
        self.dma_n = {"sp": 0, "pool": 0, "act": 0}
        for e in ENGS:
            self._mksem(e)
        for q in ("sp", "pool", "act"):
            for i in range(NDQ):
                self._mksem(f"dq_{q}_{i}")

    def _mksem(self, key):
        self.sems[key] = self.stack.enter_context(self.nc.semaphore("s_" + key))
        self.cnt[key] = 0

    def _deps(self, reads, writes):
        deps = {}
        def add(k, v):
            if v > deps.get(k, 0):
                deps[k] = v
        for b in reads:
            if b.w is not None:
                add(*b.w)
        for b in writes:
            if b.w is not None:
                add(*b.w)
            for k, v in b.r.items():
                add(k, v)
        return deps

    def _emit_waits(self, eng, deps, skip_key=None):
        seen = self.seen[eng]
        waits = []
        for k, v in deps.items():
            if k == skip_key:
                continue
            if v > seen.get(k, 0):
                seen[k] = v
                waits.append((self.sems[k], v))
        return waits

    def op(self, eng, fn, reads=(), writes=(), inc=True, strict=False):
        deps = self._deps(reads, writes)
        waits = self._emit_waits(eng, deps, skip_key=eng if (eng == 'pe' and not strict) else None)
        sem = self.sems[eng]
        if inc:
            self.cnt[eng] += 1
            val = self.cnt[eng]
        else:
            val = self.cnt[eng] + 1
        for b in reads:
            if val > b.r.get(eng, 0):
                b.r[eng] = val
        for b in writes:
            b.w = (eng, val)
            b.r = {}
        self.ops[eng].append((waits, fn, sem if inc else None, 1))

    def dma(self, q, fn, reads=(), writes=()):
        j = self.dma_n[q]
        self.dma_n[q] += 1
        key = f"dq_{q}_{j % NDQ}"
        deps = self._deps(reads, writes)
        if self.cnt[key] > 0:
            deps[key] = max(deps.get(key, 0), self.cnt[key])
        waits = self._emit_waits(q, deps)
        self.cnt[key] += 16
        val = self.cnt[key]
        for b in reads:
            if val > b.r.get(key, 0):
                b.r[key] = val
        for b in writes:
            b.w = (key, val)
            b.r = {}
        self.ops[q].append((waits, fn, self.sems[key], 16))

    def finish(self):
        deps = {k: v for k, v in self.cnt.items() if v > 0 and k != "sp"}
        waits = self._emit_waits("sp", deps)
        self.ops["sp"].append((waits, None, None, 0))

    def emit(self):
        nc = self.nc
        ops_all = self.ops
        self.ops = {e: [] for e in ENGS}
        self.n_phase = getattr(self, "n_phase", 0) + 1
        import contextlib
        scope = nc.named_scope("ph%02d" % self.n_phase) if getattr(self, "scopes", False) else contextlib.nullcontext()
        with scope, nc.Block() as block:
            def run(engname):
                def body(eng):
                    for waits, fn, sem, amt in ops_all[engname]:
                        for s, v in waits:
                            eng.wait_ge(s, v)
                        if fn is not None:
                            inst = fn(eng)
                            if sem is not None:
                                inst.then_inc(sem, amt)
                return body
            block.tensor(run("pe"))
            block.scalar(run("act"))
            block.vector(run("dve"))
            block.gpsimd(run("pool"))
            block.sync(run("sp"))


def _barrier(self):
    snap = {k: v for k, v in self.cnt.items() if v > 0}
    for e in ENGS:
        deps = {k: v for k, v in snap.items() if k != e}
        waits = self._emit_waits(e, deps)
        if waits:
            self.ops[e].append((waits, None, None, 0))
Prog.barrier = _barrier


def _phase_end(self):
    self.barrier()
    self.emit()
Prog.phase_end = _phase_end


D = 1024
F = 2816
NFC = F // 128
NKC = D // 128
FG = 256
NFG = F // FG


class Common:
    def __init__(self, P, consts_dram):
        nc, st = P.nc, P.stack
        self.P = P
        self.psum = []
        self.psb = []
        for i in range(8):
            t = st.enter_context(nc.psum_tensor(f"ps{i}", [128, 512], F32))
            self.psum.append(t)
            self.psb.append(Buf(f"ps{i}"))
        self.ones_f = st.enter_context(nc.sbuf_tensor("ones_f", [128, 128], F32))
        self.b_ones_f = Buf("ones_f")
        P.op("pool", lambda e: e.memset(self.ones_f[:], 1.0), writes=[self.b_ones_f])


def ffn_phase(P, C, xT_in, xT_out, wg, wu, wd, gvec, Tc, tag, wq="pool"):
    from contextlib import ExitStack
    nc = P.nc
    st = ExitStack()
    TT = 512
    NT = Tc // TT
    sb = lambda n, s, d: st.enter_context(nc.sbuf_tensor(f"{tag}_{n}", s, d))
    x_t = [sb(f"x{i}", [128, NKC, TT], F32) for i in range(2)]
    b_x = [Buf(f"x{i}") for i in range(2)]
    sq_t = sb("sq", [128, NKC, TT], F32); b_sq = Buf("sq")
    rstd = sb("rstd", [128, TT], F32); b_rstd = Buf("rstd")
    h_t = sb("h", [128, NKC, TT], BF16); b_h = Buf("h")
    a_t = sb("a", [128, NFC, TT], BF16); b_a = [Buf(f"a{i}") for i in range(NFC)]
    sil = [sb(f"sil{i}", [128, TT], F32) for i in range(2)]; b_sil = [Buf("sil0"), Buf("sil1")]
    wg_t = [sb(f"wg{i}", [128, NKC, FG], BF16) for i in range(2)]
    wu_t = [sb(f"wu{i}", [128, NKC, FG], BF16) for i in range(2)]
    b_wg = [Buf("wg0"), Buf("wg1")]; b_wu = [Buf("wu0"), Buf("wu1")]
    wd_t = [sb(f"wd{i}", [128, NFC, 128], BF16) for i in range(2)]
    b_wd = [Buf("wd0"), Buf("wd1")]
    g_t = sb("g", [128, NKC], F32); b_g = Buf("g")

    P.dma("sp", lambda e: e.dma_start(out=g_t[:], in_=gvec.rearrange("(c p) -> p c", p=128), allow_slow_non_contiguous=True),
          writes=[b_g])
    xin = xT_in.rearrange("(c p) t -> p c t", p=128)
    xout = xT_out.rearrange("(c p) t -> p c t", p=128)
    wgr = wg.rearrange("(c p) f -> p c f", p=128)
    wur = wu.rearrange("(c p) f -> p c f", p=128)
    wdr = wd.rearrange("(c p) d -> p c d", p=128)

    wcount = [0]
    for it in range(NT):
        xs = it % 2
        t0 = it * TT
        P.dma("sp", lambda e, xs=xs, t0=t0: e.dma_start(out=x_t[xs][:], in_=xin[:, :, t0:t0 + TT]),
              writes=[b_x[xs]])
        P.op("act", lambda e, xs=xs: e.activation(out=sq_t[:], in_=x_t[xs][:], func=AF.Square),
             reads=[b_x[xs]], writes=[b_sq])
        pst = 6
        for kc in range(NKC):
            P.op("pe", lambda e, kc=kc: e.matmul(C.psum[pst][:], lhsT=C.ones_f[:], rhs=sq_t[:, kc, :],
                                                 start=(kc == 0), stop=(kc == NKC - 1)),
                 reads=[b_sq, C.b_ones_f], writes=[C.psb[pst]], inc=(kc == NKC - 1))
        P.op("dve", lambda e: e.tensor_scalar(out=rstd[:], in0=C.psum[pst][:], scalar1=1.0 / D, scalar2=1e-6,
                                              op0=ALU.mult, op1=ALU.add),
             reads=[C.psb[pst]], writes=[b_rstd])
        P.op("act", lambda e: e.activation(out=rstd[:], in_=rstd[:], func=AF.Sqrt),
             reads=[b_rstd], writes=[b_rstd])
        P.op("dve", lambda e: e.reciprocal(out=rstd[:], in_=rstd[:]),
             reads=[b_rstd], writes=[b_rstd])
        for kc in range(NKC):
            P.op("dve", lambda e, kc=kc, xs=xs: e.scalar_tensor_tensor(
                out=h_t[:, kc, :], in0=x_t[xs][:, kc, :], scalar=g_t[:, kc:kc + 1], in1=rstd[:],
                op0=ALU.mult, op1=ALU.mult),
                reads=[b_x[xs], b_g, b_rstd], writes=[b_h])
        for fg in range(NFG):
            ws = wcount[0] % 2
            wcount[0] += 1
            f0 = fg * FG
            P.dma(wq, lambda e, ws=ws, f0=f0: e.dma_start(out=wg_t[ws][:], in_=wgr[:, :, f0:f0 + FG]),
                  writes=[b_wg[ws]])
            P.dma(wq, lambda e, ws=ws, f0=f0: e.dma_start(out=wu_t[ws][:], in_=wur[:, :, f0:f0 + FG]),
                  writes=[b_wu[ws]])
            for j in range(FG // 128):
                fc = fg * (FG // 128) + j
                pg = fc % 2
                pu = 2 + fc % 2
                for kc in range(NKC):
                    P.op("pe", lambda e, kc=kc, ws=ws, j=j, pg=pg: e.matmul(
                        C.psum[pg][:], lhsT=wg_t[ws][:, kc, j * 128:(j + 1) * 128], rhs=h_t[:, kc, :],
                        start=(kc == 0), stop=(kc == NKC - 1)),
                        reads=[b_wg[ws], b_h], writes=[C.psb[pg]], inc=(kc == NKC - 1))
                for kc in range(NKC):
                    P.op("pe", lambda e, kc=kc, ws=ws, j=j, pu=pu: e.matmul(
                        C.psum[pu][:], lhsT=wu_t[ws][:, kc, j * 128:(j + 1) * 128], rhs=h_t[:, kc, :],
                        start=(kc == 0), stop=(kc == NKC - 1)),
                        reads=[b_wu[ws], b_h], writes=[C.psb[pu]], inc=(kc == NKC - 1))
                ss = fc % 2
                P.op("act", lambda e, ss=ss, pg=pg: e.activation(out=sil[ss][:], in_=C.psum[pg][:], func=AF.Silu),
                     reads=[C.psb[pg]], writes=[b_sil[ss]])
                P.op("dve", lambda e, ss=ss, pu=pu, fc=fc: e.tensor_tensor(
                    out=a_t[:, fc, :], in0=sil[ss][:], in1=C.psum[pu][:], op=ALU.mult),
                    reads=[b_sil[ss], C.psb[pu]], writes=[b_a[fc]])
        for dc in range(NKC):
            ws = dc % 2
            P.dma(wq, lambda e, ws=ws, dc=dc: e.dma_start(out=wd_t[ws][:], in_=wdr[:, :, dc * 128:(dc + 1) * 128]),
                  writes=[b_wd[ws]])
            py = 4 + dc % 2
            for fc in range(NFC):
                P.op("pe", lambda e, fc=fc, ws=ws, py=py: e.matmul(
                    C.psum[py][:], lhsT=wd_t[ws][:, fc, :], rhs=a_t[:, fc, :],
                    start=(fc == 0), stop=(fc == NFC - 1)),
                    reads=[b_wd[ws], b_a[fc]], writes=[C.psb[py]], inc=(fc == NFC - 1))
            P.op("dve", lambda e, dc=dc, xs=xs, py=py: e.scalar_tensor_tensor(
                out=x_t[xs][:, dc, :], in0=C.psum[py][:], scalar=0.5, in1=x_t[xs][:, dc, :],
                op0=ALU.mult, op1=ALU.add),
                reads=[C.psb[py], b_x[xs]], writes=[b_x[xs]])
        P.dma("sp", lambda e, xs=xs, t0=t0: e.dma_start(out=xout[:, :, t0:t0 + TT], in_=x_t[xs][:]),
              reads=[b_x[xs]])
    P.phase_end()
    st.close()


from contextlib import ExitStack

H = 16
DH = 64
TT = 512
PDEPTH = 2


def ps_next(C):
    i = C.ps_i = (getattr(C, "ps_i", -1) + 1) % 8
    return C.psum[i], C.psb[i]


def load_x_norm(P, C, sb, xin, t0, g_t, b_g, tag):
    x_t = sb("x", [128, NKC, TT], F32); b_x = Buf("x")
    sq_t = sb("sq", [128, NKC, TT], F32); b_sq = Buf("sq")
    rstd = sb("rstd", [128, TT], F32); b_rstd = Buf("rstd")
    h_t = sb("h", [128, NKC, TT], BF16); b_h = Buf("h")
    return x_t, b_x, sq_t, b_sq, rstd, b_rstd, h_t, b_h


def emit_norm(P, C, x_t, b_x, sq_t, b_sq, rstd, b_rstd, h_t, b_h, g_t, b_g):
    P.op("act", lambda e: e.activation(out=sq_t[:], in_=x_t[:], func=AF.Square), reads=[b_x], writes=[b_sq])
    ps, pb = ps_next(C)
    for kc in range(NKC):
        P.op("pe", lambda e, kc=kc: e.matmul(ps[:], lhsT=C.ones_f[:], rhs=sq_t[:, kc, :], start=(kc == 0), stop=(kc == NKC - 1)),
             reads=[b_sq, C.b_ones_f], writes=[pb], inc=(kc == NKC - 1))
    P.op("dve", lambda e: e.tensor_scalar(out=rstd[:], in0=ps[:], scalar1=1.0 / D, scalar2=1e-6, op0=ALU.mult, op1=ALU.add),
         reads=[pb], writes=[b_rstd])
    P.op("act", lambda e: e.activation(out=rstd[:], in_=rstd[:], func=AF.Sqrt), reads=[b_rstd], writes=[b_rstd])
    P.op("dve", lambda e: e.reciprocal(out=rstd[:], in_=rstd[:]), reads=[b_rstd], writes=[b_rstd])
    for kc in range(NKC):
        P.op("dve", lambda e, kc=kc: e.scalar_tensor_tensor(out=h_t[:, kc, :], in0=x_t[:, kc, :], scalar=g_t[:, kc:kc + 1],
                                                            in1=rstd[:], op0=ALU.mult, op1=ALU.mult),
             reads=[b_x, b_g, b_rstd], writes=[b_h])


def lin_fm(P, C, sb, wr, col0, ncols, src_t, b_src, consume, tag, G=256, nkc=NKC):
    wt = [sb(f"{tag}w{i}", [128, nkc, G], BF16) for i in range(2)]
    bw = [Buf(f"{tag}w0"), Buf(f"{tag}w1")]
    ng = (ncols + G - 1) // G
    def run():
        for g in range(ng):
            ws = g % 2
            c0 = col0 + g * G
            gw = min(G, col0 + ncols - c0)
            P.dma("pool", lambda e, ws=ws, c0=c0, gw=gw: e.dma_start(out=wt[ws][:, :, 0:gw], in_=wr[:, :, c0:c0 + gw]),
                  writes=[bw[ws]])
            for j in range((gw + 127) // 128):
                cw = min(128, gw - j * 128)
                ps, pb = ps_next(C)
                for kc in range(nkc):
                    P.op("pe", lambda e, kc=kc, ws=ws, j=j, cw=cw, ps=ps: e.matmul(
                        ps[0:cw, :], lhsT=wt[ws][:, kc, j * 128:j * 128 + cw], rhs=src_t[:, kc, :],
                        start=(kc == 0), stop=(kc == nkc - 1)),
                        reads=[bw[ws], b_src], writes=[pb], inc=(kc == nkc - 1))
                consume(g * (G // 128) + j, ps, pb)
    return run


def fox_A(P, C, xT, S, w_in, gnorm, b_f, g_q, g_k, scr):
    nc = P.nc
    with ExitStack() as st:
        sb = lambda n, s, d: st.enter_context(nc.sbuf_tensor(f"fa_{n}", s, d))
        g_t = sb("g", [128, NKC], F32); b_g = Buf("g")
        P.dma("sp", lambda e: e.dma_start(out=g_t[:], in_=gnorm.rearrange("(c p) -> p c", p=128), allow_slow_non_contiguous=True), writes=[b_g])
        gq = sb("gq", [128, 1], F32); gk = sb("gk", [128, 1], F32); b_gqk = Buf("gqk")
        for half in range(2):
            P.dma("sp", lambda e, half=half: e.dma_start(out=gq[half * 64:(half + 1) * 64, :], in_=g_q.rearrange("(p o) -> p o", o=1), allow_slow_non_contiguous=True), writes=[b_gqk])
            P.dma("sp", lambda e, half=half: e.dma_start(out=gk[half * 64:(half + 1) * 64, :], in_=g_k.rearrange("(p o) -> p o", o=1), allow_slow_non_contiguous=True), writes=[b_gqk])
        nbf = sb("nbf", [16, 1], F32); b_nbf = Buf("nbf")
        P.dma("sp", lambda e: e.dma_start(out=nbf[:], in_=b_f.rearrange("(p o) -> p o", o=1), allow_slow_non_contiguous=True), writes=[b_nbf])
        P.op("dve", lambda e: e.tensor_scalar(out=nbf[:], in0=nbf[:], scalar1=-1.0, scalar2=None, op0=ALU.mult), reads=[b_nbf], writes=[b_nbf])
        bones = sb("bones", [128, 128], F32); b_bones = Buf("bones")
        P.op("pool", lambda e: e.memset(bones[:], 0.0), writes=[b_bones])
        P.op("pool", lambda e: e.memset(bones[0:64, 0:64], 1.0), writes=[b_bones])
        P.op("pool", lambda e: e.memset(bones[64:128, 64:128], 1.0), writes=[b_bones])
        x_t, b_x, sq_t, b_sq, rstd, b_rstd, h_t, b_h = load_x_norm(P, C, sb, None, 0, g_t, b_g, "fa")
        sqq = [sb(f"sqq{i}", [128, TT], F32) for i in range(2)]; b_sqq = [Buf("sqq0"), Buf("sqq1")]
        rr = [sb(f"rr{i}", [128, TT], F32) for i in range(2)]; b_rr = [Buf("rr0"), Buf("rr1")]
        stg = [sb(f"stg{i}", [128, TT], BF16) for i in range(3)]; b_stg = [Buf(f"stg{i}") for i in range(3)]
        vst = [sb(f"vst{i}", [128, 4, 512], BF16) for i in range(2)]; b_vst = [Buf("vst0"), Buf("vst1")]
        wv_t = [sb(f"wv{i}", [128, NKC, 512], BF16) for i in range(2)]; b_wv = [Buf("wv0"), Buf("wv1")]
        fe = sb("fe", [16, TT], F32); b_fe = Buf("fe")
        xin = xT.rearrange("(c p) t -> p c t", p=128)
        wr = w_in.rearrange("(c p) f -> p c f", p=128)
        cnt = {"s": 0, "q": 0}

        def mk_qk(dst, gcol, t0):
            def consume(ci, ps, pb):
                i = cnt["q"] % 2; cnt["q"] += 1
                s3 = cnt["s"] % 3; cnt["s"] += 1
                P.op("act", lambda e: e.activation(out=sqq[i][:], in_=ps[:], func=AF.Square), reads=[pb], writes=[b_sqq[i]])
                ps2, pb2 = ps_next(C)
                P.op("pe", lambda e: e.matmul(ps2[:], lhsT=bones[:], rhs=sqq[i][:], start=True, stop=True),
                     reads=[b_bones, b_sqq[i]], writes=[pb2])
                P.op("dve", lambda e: e.tensor_scalar(out=rr[i][:], in0=ps2[:], scalar1=1.0 / DH, scalar2=1e-6, op0=ALU.mult, op1=ALU.add),
                     reads=[pb2], writes=[b_rr[i]])
                P.op("act", lambda e: e.activation(out=rr[i][:], in_=rr[i][:], func=AF.Sqrt), reads=[b_rr[i]], writes=[b_rr[i]])
                P.op("dve", lambda e: e.reciprocal(out=rr[i][:], in_=rr[i][:]), reads=[b_rr[i]], writes=[b_rr[i]])
                P.op("dve", lambda e: e.scalar_tensor_tensor(out=stg[s3][:], in0=ps[:], scalar=gcol[:, 0:1], in1=rr[i][:],
                                                             op0=ALU.mult, op1=ALU.mult),
                     reads=[pb, b_gqk, b_rr[i]], writes=[b_stg[s3]])
                P.dma("sp", lambda e: e.dma_start(out=dst[ci * 128:(ci + 1) * 128, t0:t0 + TT], in_=stg[s3][:]), reads=[b_stg[s3]])
            return consume

        def mk_og(t0):
            def consume(ci, ps, pb):
                s3 = cnt["s"] % 3; cnt["s"] += 1
                P.op("act", lambda e: e.activation(out=stg[s3][:], in_=ps[:], func=AF.Sigmoid), reads=[pb], writes=[b_stg[s3]])
                P.dma("sp", lambda e: e.dma_start(out=scr["sg"][ci * 128:(ci + 1) * 128, t0:t0 + TT], in_=stg[s3][:]), reads=[b_stg[s3]])
            return consume

        def mk_f(t0):
            def consume(ci, ps, pb):
                P.op("act", lambda e: e.activation(out=fe[:], in_=ps[0:16, :], func=AF.Exp, bias=nbf[:, 0:1], scale=-1.0),
                     reads=[pb, b_nbf], writes=[b_fe])
                P.op("act", lambda e: e.activation(out=fe[:], in_=fe[:], func=AF.Ln, bias=1.0, scale=1.0), reads=[b_fe], writes=[b_fe])
                P.op("dve", lambda e: e.tensor_scalar(out=fe[:], in0=fe[:], scalar1=-1.0, scalar2=None, op0=ALU.mult), reads=[b_fe], writes=[b_fe])
                P.dma("sp", lambda e: e.dma_start(out=scr["ls"][:, t0:t0 + TT], in_=fe[:]), reads=[b_fe])
            return consume

        for it in range(S // TT):
            t0 = it * TT
            P.dma("sp", lambda e, t0=t0: e.dma_start(out=x_t[:], in_=xin[:, :, t0:t0 + TT]), writes=[b_x])
            emit_norm(P, C, x_t, b_x, sq_t, b_sq, rstd, b_rstd, h_t, b_h, g_t, b_g)
            for name, col0, ncols, mk in (("q", 0, 1024, lambda: mk_qk(scr["qT"], gq, t0)),
                                          ("k", 1024, 1024, lambda: mk_qk(scr["kT"], gk, t0)),
                                          ("og", 3088, 1024, lambda: mk_og(t0)),
                                          ("f", 3072, 16, lambda: mk_f(t0))):
                key = "wbuf_" + name
                if key not in cnt:
                    cnt[key] = ([sb(f"{name}w{i}", [128, NKC, 256], BF16) for i in range(2)], [Buf(name + "w0"), Buf(name + "w1")])
                wt, bw = cnt[key]
                consume = mk()
                G = 256
                ng = (ncols + G - 1) // G
                for g in range(ng):
                    ws = g % 2
                    c0 = col0 + g * G
                    gw = min(G, col0 + ncols - c0)
                    P.dma("pool", lambda e, ws=ws, c0=c0, gw=gw, wt=wt: e.dma_start(out=wt[ws][:, :, 0:gw], in_=wr[:, :, c0:c0 + gw]),
                          writes=[bw[ws]])
                    for j in range((gw + 127) // 128):
                        cw = min(128, gw - j * 128)
                        ps, pb = ps_next(C)
                        for kc in range(NKC):
                            P.op("pe", lambda e, kc=kc, ws=ws, j=j, cw=cw, ps=ps, wt=wt: e.matmul(
                                ps[0:cw, :], lhsT=wt[ws][:, kc, j * 128:j * 128 + cw], rhs=h_t[:, kc, :],
                                start=(kc == 0), stop=(kc == NKC - 1)),
                                reads=[bw[ws], b_h], writes=[pb], inc=(kc == NKC - 1))
                        consume(g * (G // 128) + j, ps, pb)
            for cg in range(2):
                P.dma("pool", lambda e, cg=cg: e.dma_start(out=wv_t[cg][:], in_=wr[:, :, 2048 + cg * 512:2048 + (cg + 1) * 512]),
                      writes=[b_wv[cg]])
                vs = cg
                for tb in range(4):
                    ps, pb = ps_next(C)
                    for kc in range(NKC):
                        P.op("pe", lambda e, kc=kc, tb=tb, cg=cg, ps=ps: e.matmul(
                            ps[:], lhsT=h_t[:, kc, tb * 128:(tb + 1) * 128], rhs=wv_t[cg][:, kc, :],
                            start=(kc == 0), stop=(kc == NKC - 1)),
                            reads=[b_wv[cg], b_h], writes=[pb], inc=(kc == NKC - 1))
                    P.op("act", lambda e, tb=tb, vs=vs, ps=ps: e.activation(out=vst[vs][:, tb, :], in_=ps[:], func=AF.Copy),
                         reads=[pb], writes=[b_vst[vs]])
                P.dma("sp", lambda e, vs=vs, cg=cg, t0=t0: e.dma_start(
                    out=scr["V"][t0:t0 + TT, cg * 512:(cg + 1) * 512].rearrange("(tb p) c -> p tb c", p=128), in_=vst[vs][:]),
                    reads=[b_vst[vs]])
        P.phase_end()


def fox_B(P, C, S, scr, consts):
    nc = P.nc
    NB = S // 128
    NQC = S // 512
    with ExitStack() as st:
        sb = lambda n, s, d: st.enter_context(nc.sbuf_tensor(f"fb_{n}", s, d))
        tri = sb("tri", [128, 128], BF16); b_tri = Buf("tri")
        P.dma("sp", lambda e: e.dma_start(out=tri[:], in_=consts["tri"]), writes=[b_tri])
        identb = sb("identb", [128, 128], BF16)
        P.dma("sp", lambda e: e.dma_start(out=identb[:], in_=consts["identb"]), writes=[b_tri])
        ident = sb("ident", [128, 128], F32); b_ident = Buf("ident")
        P.dma("sp", lambda e: e.dma_start(out=ident[:], in_=consts["ident"]), writes=[b_ident])
        c_t = sb("c", [16, S], F32); b_c = Buf("c")
        zc = sb("zc", [16, 1], F32); b_zc = Buf("zc")
        P.op("pool", lambda e: e.memset(zc[:], 0.0), writes=[b_zc])
        P.dma("sp", lambda e: e.dma_start(out=c_t[:], in_=scr["ls"]), writes=[b_c])
        P.op("dve", lambda e: e.tensor_tensor_scan(out=c_t[:], data0=c_t[:], data1=zc[:, 0:1].to_broadcast([16, S]), initial=0.0,
                                                   op0=ALU.add, op1=ALU.add), reads=[b_c, b_zc], writes=[b_c])
        negc = sb("negc", [128, NB, 16], F32); b_negc = Buf("negc")
        GB = 32
        for g0 in range(0, NB, GB):
            ps, pb = ps_next(C)
            nb_ = min(GB, NB - g0)
            for j in range(nb_):
                kb = g0 + j
                P.op("pe", lambda e, kb=kb, j=j, ps=ps: e.transpose(ps[:, j * 16:(j + 1) * 16], c_t[:, kb * 128:(kb + 1) * 128], ident[0:16, 0:16]),
                     reads=[b_c, b_ident], writes=[pb], inc=(j == nb_ - 1))
            P.op("dve", lambda e, g0=g0, nb_=nb_, ps=ps: e.tensor_scalar(
                out=negc[:, g0:g0 + nb_, :].rearrange("p a b -> p (a b)"), in0=ps[:, 0:nb_ * 16], scalar1=-1.0, scalar2=None, op0=ALU.mult),
                reads=[pb], writes=[b_negc])
        b_crow = Buf("crow")
        r_t = sb("r", [16, S], F32); b_r = Buf("r")
        hi = sb("hi", [16, S], BF16); b_hi = Buf("hi")
        P.op("dve", lambda e: e.tensor_scalar(out=r_t[:], in0=c_t[:], scalar1=8.0, scalar2=None, op0=ALU.mult), reads=[b_c], writes=[b_r])
        for j in range(3):
            P.op("dve", lambda e: e.tensor_copy(out=hi[:], in_=r_t[:]), reads=[b_r], writes=[b_hi])
            P.dma("sp", lambda e, j=j: e.dma_start(out=scr["crow"][j], in_=hi[:]), reads=[b_hi], writes=[b_crow])
            if j < 2:
                P.op("dve", lambda e: e.tensor_tensor(out=r_t[:], in0=r_t[:], in1=hi[:], op=ALU.subtract), reads=[b_r, b_hi], writes=[b_r])
        qa = [sb(f"qa{i}", [67, S], BF16) for i in range(2)]; b_qa = [Buf("qa0"), Buf("qa1")]
        ka = [sb(f"ka{i}", [67, S], BF16) for i in range(2)]; b_ka = [Buf("ka0"), Buf("ka1")]
        va = [sb(f"va{i}", [128, NB, 65], BF16) for i in range(2)]; b_va = [Buf("va0"), Buf("va1")]
        for i in range(2):
            P.op("pool", lambda e, i=i: e.memset(ka[i][64:67, :], 1.0), writes=[b_ka[i]])
            P.op("pool", lambda e, i=i: e.memset(va[i][:, :, 64:65], 1.0), writes=[b_va[i]])
        pt = [sb(f"pt{i}", [128, 512], BF16) for i in range(4)]; b_pt = [Buf(f"pt{i}") for i in range(4)]
        osb = sb("osb", [65, 512], F32); b_osb = Buf("osb")
        rec = sb("rec", [65, 512], F32); b_rec = Buf("rec")
        ost = [sb(f"ost{i}", [64, 512], BF16) for i in range(2)]; b_ost = [Buf("ost0"), Buf("ost1")]
        npt = 0
        for h in range(H):
            hs = h % 2
            P.dma("sp", lambda e, h=h, hs=hs: e.dma_start(out=qa[hs][0:64, :], in_=scr["qT"][h * 64:(h + 1) * 64, :]), writes=[b_qa[hs]])
            P.dma("sp", lambda e, h=h, hs=hs: e.dma_start(out=qa[hs][64:67, :], in_=scr["crow"][:, h, :]), reads=[b_crow], writes=[b_qa[hs]])
            P.dma("sp", lambda e, h=h, hs=hs: e.dma_start(out=ka[hs][0:64, :], in_=scr["kT"][h * 64:(h + 1) * 64, :]), writes=[b_ka[hs]])
            P.dma("sp", lambda e, h=h, hs=hs: e.dma_start(out=va[hs][:, :, 0:64], in_=scr["V"][:, h * 64:(h + 1) * 64].rearrange("(kb p) d -> p kb d", p=128)),
                  writes=[b_va[hs]])
            for qc in range(NQC):
                po, pob = ps_next(C)
                nkb = 4 * qc + 4
                pq = []
                for kb in range(nkb):
                    off = max(0, kb - 4 * qc) * 128
                    ps, pb = ps_next(C)
                    if ps is po:
                        ps, pb = ps_next(C)
                    diag = kb >= 4 * qc
                    P.op("pe", lambda e, kb=kb, off=off, ps=ps, hs=hs, qc=qc, diag=diag: e.matmul(
                        ps[:, off:512], lhsT=ka[hs][0:67, kb * 128:(kb + 1) * 128], rhs=qa[hs][0:67, qc * 512 + off:(qc + 1) * 512],
                        start=True, stop=not diag), reads=[b_ka[hs], b_qa[hs]], writes=[pb], inc=not diag)
                    if diag:
                        P.op("pe", lambda e, off=off, ps=ps: e.matmul(
                            ps[:, off:off + 128], lhsT=identb[:], rhs=tri[:], start=False, stop=True),
                            reads=[b_tri], writes=[pb])
                    pi = npt % 4; npt += 1
                    P.op("act", lambda e, kb=kb, off=off, ps=ps, pi=pi, h=h: e.activation(
                        out=pt[pi][:, off:512], in_=ps[:, off:512], func=AF.Exp, bias=negc[:, kb, h:h + 1], scale=0.125),
                        reads=[pb, b_negc], writes=[b_pt[pi]])
                    def pv(kb=kb, off=off, pi=pi, hs=hs, po=po, nkb=nkb, pob=pob):
                        P.op("pe", lambda e: e.matmul(
                            po[0:65, off:512], lhsT=va[hs][:, kb, :], rhs=pt[pi][:, off:512], start=(kb == 0), stop=(kb == nkb - 1)),
                            reads=[b_va[hs], b_pt[pi]], writes=[pob])
                    pq.append(pv)
                    if len(pq) > PDEPTH:
                        pq.pop(0)()
                while pq:
                    pq.pop(0)()
                P.op("act", lambda e, po=po: e.activation(out=osb[:], in_=po[0:65, :], func=AF.Copy), reads=[pob], writes=[b_osb])
                P.op("dve", lambda e: e.reciprocal(out=rec[64:65, :], in_=osb[64:65, :]), reads=[b_osb], writes=[b_rec])
                pbc, pbcb = ps_next(C)
                P.op("pe", lambda e, pbc=pbc: e.matmul(pbc[0:64, :], lhsT=C.ones_f[64:65, 0:64], rhs=rec[64:65, :], start=True, stop=True),
                     reads=[b_rec, C.b_ones_f], writes=[pbcb])
                oi = qc % 2
                P.op("dve", lambda e, pbc=pbc, oi=oi: e.tensor_tensor(out=ost[oi][:], in0=osb[0:64, :], in1=pbc[0:64, :], op=ALU.mult),
                     reads=[b_osb, pbcb], writes=[b_ost[oi]])
                P.dma("sp", lambda e, h=h, qc=qc, oi=oi: e.dma_start(out=scr["oT"][h * 64:(h + 1) * 64, qc * 512:(qc + 1) * 512], in_=ost[oi][:]),
                      reads=[b_ost[oi]])
        P.phase_end()


def mix_C(P, C, xT, S, w_out, scr, gate_key):
    nc = P.nc
    with ExitStack() as st:
        mix_C.n = getattr(mix_C, "n", 0) + 1
        tagc = mix_C.n
        sb = lambda n, s, d: st.enter_context(nc.sbuf_tensor(f"mc{tagc}_{n}", s, d))
        x_t = [sb(f"x{i}", [128, NKC, TT], F32) for i in range(2)]; b_x = [Buf("x0"), Buf("x1")]
        o_t = [sb(f"o{i}", [128, NKC, TT], BF16) for i in range(2)]; b_o = [Buf("o0"), Buf("o1")]
        g_t = [sb(f"gt{i}", [128, NKC, TT], BF16) for i in range(2)]; b_gt = [Buf("gt0"), Buf("gt1")]
        wo = [sb(f"wo{i}", [128, NKC, 256], BF16) for i in range(2)]; b_wo = [Buf("wo0"), Buf("wo1")]
        xin = xT.rearrange("(c p) t -> p c t", p=128)
        oin = scr["oT"].rearrange("(c p) t -> p c t", p=128)
        wr = w_out.rearrange("(c p) f -> p c f", p=128)
        if gate_key:
            gin = scr[gate_key].rearrange("(c p) t -> p c t", p=128)
        for it in range(S // TT):
            t0 = it * TT
            xs = it % 2
            P.dma("sp", lambda e, t0=t0, xs=xs: e.dma_start(out=x_t[xs][:], in_=xin[:, :, t0:t0 + TT]), writes=[b_x[xs]])
            P.dma("sp", lambda e, t0=t0, xs=xs: e.dma_start(out=o_t[xs][:], in_=oin[:, :, t0:t0 + TT]), writes=[b_o[xs]])
            if gate_key:
                P.dma("sp", lambda e, t0=t0, xs=xs: e.dma_start(out=g_t[xs][:], in_=gin[:, :, t0:t0 + TT]), writes=[b_gt[xs]])
                P.op("pool", lambda e, xs=xs: e.tensor_tensor(out=o_t[xs][:], in0=o_t[xs][:], in1=g_t[xs][:], op=ALU.mult),
                     reads=[b_o[xs], b_gt[xs]], writes=[b_o[xs]])
            for g in range(4):
                ws = g % 2
                P.dma("pool", lambda e, ws=ws, g=g: e.dma_start(out=wo[ws][:], in_=wr[:, :, g * 256:(g + 1) * 256]), writes=[b_wo[ws]])
                for j in range(2):
                    dc = g * 2 + j
                    ps, pb = ps_next(C)
                    for kc in range(NKC):
                        P.op("pe", lambda e, kc=kc, ws=ws, j=j, ps=ps, xs=xs: e.matmul(
                            ps[:], lhsT=wo[ws][:, kc, j * 128:(j + 1) * 128], rhs=o_t[xs][:, kc, :], start=(kc == 0), stop=(kc == NKC - 1)),
                            reads=[b_wo[ws], b_o[xs]], writes=[pb], inc=(kc == NKC - 1))
                    P.op("dve", lambda e, dc=dc, ps=ps, xs=xs: e.tensor_tensor(out=x_t[xs][:, dc, :], in0=x_t[xs][:, dc, :], in1=ps[:], op=ALU.add),
                         reads=[pb, b_x[xs]], writes=[b_x[xs]])
            P.dma("sp", lambda e, t0=t0, xs=xs: e.dma_start(out=xin[:, :, t0:t0 + TT], in_=x_t[xs][:]), reads=[b_x[xs]])
        P.phase_end()


def qkv_A(P, C, xT, S, w_in, gnorm, scr):
    nc = P.nc
    with ExitStack() as st:
        sb = lambda n, s, d: st.enter_context(nc.sbuf_tensor(f"sa_{n}", s, d))
        g_t = sb("g", [128, NKC], F32); b_g = Buf("g")
        P.dma("sp", lambda e: e.dma_start(out=g_t[:], in_=gnorm.rearrange("(c p) -> p c", p=128), allow_slow_non_contiguous=True), writes=[b_g])
        x_t, b_x, sq_t, b_sq, rstd, b_rstd, h_t, b_h = load_x_norm(P, C, sb, None, 0, g_t, b_g, "sa")
        stg = [sb(f"stg{i}", [128, TT], BF16) for i in range(3)]; b_stg = [Buf(f"stg{i}") for i in range(3)]
        vst = [sb(f"vst{i}", [128, 4, 512], BF16) for i in range(2)]; b_vst = [Buf("vst0"), Buf("vst1")]
        wv_t = [sb(f"wv{i}", [128, NKC, 512], BF16) for i in range(2)]; b_wv = [Buf("wv0"), Buf("wv1")]
        wt = [sb(f"w{i}", [128, NKC, 256], BF16) for i in range(2)]; bw = [Buf("w0"), Buf("w1")]
        xin = xT.rearrange("(c p) t -> p c t", p=128)
        wr = w_in.rearrange("(c p) f -> p c f", p=128)
        ns = 0
        for it in range(S // TT):
            t0 = it * TT
            P.dma("sp", lambda e, t0=t0: e.dma_start(out=x_t[:], in_=xin[:, :, t0:t0 + TT]), writes=[b_x])
            emit_norm(P, C, x_t, b_x, sq_t, b_sq, rstd, b_rstd, h_t, b_h, g_t, b_g)
            for dst, col0 in ((scr["qT"], 0), (scr["kT"], 1024)):
                for g in range(4):
                    ws = g % 2
                    c0 = col0 + g * 256
                    P.dma("pool", lambda e, ws=ws, c0=c0: e.dma_start(out=wt[ws][:], in_=wr[:, :, c0:c0 + 256]), writes=[bw[ws]])
                    for j in range(2):
                        ci = g * 2 + j
                        ps, pb = ps_next(C)
                        for kc in range(NKC):
                            P.op("pe", lambda e, kc=kc, ws=ws, j=j, ps=ps: e.matmul(
                                ps[:], lhsT=wt[ws][:, kc, j * 128:(j + 1) * 128], rhs=h_t[:, kc, :], start=(kc == 0), stop=(kc == NKC - 1)),
                                reads=[bw[ws], b_h], writes=[pb], inc=(kc == NKC - 1))
                        s3 = ns % 3; ns += 1
                        P.op("act", lambda e, s3=s3, ps=ps: e.activation(out=stg[s3][:], in_=ps[:], func=AF.Copy), reads=[pb], writes=[b_stg[s3]])
                        P.dma("sp", lambda e, s3=s3, ci=ci, dst=dst, t0=t0: e.dma_start(out=dst[ci * 128:(ci + 1) * 128, t0:t0 + TT], in_=stg[s3][:]), reads=[b_stg[s3]])
            for cg in range(2):
                P.dma("pool", lambda e, cg=cg: e.dma_start(out=wv_t[cg][:], in_=wr[:, :, 2048 + cg * 512:2048 + (cg + 1) * 512]), writes=[b_wv[cg]])
                for tb in range(4):
                    ps, pb = ps_next(C)
                    for kc in range(NKC):
                        P.op("pe", lambda e, kc=kc, tb=tb, cg=cg, ps=ps: e.matmul(
                            ps[:], lhsT=h_t[:, kc, tb * 128:(tb + 1) * 128], rhs=wv_t[cg][:, kc, :], start=(kc == 0), stop=(kc == NKC - 1)),
                            reads=[b_wv[cg], b_h], writes=[pb], inc=(kc == NKC - 1))
                    P.op("act", lambda e, tb=tb, cg=cg, ps=ps: e.activation(out=vst[cg][:, tb, :], in_=ps[:], func=AF.Copy), reads=[pb], writes=[b_vst[cg]])
                P.dma("sp", lambda e, cg=cg, t0=t0: e.dma_start(
                    out=scr["V"][t0:t0 + TT, cg * 512:(cg + 1) * 512].rearrange("(tb p) c -> p tb c", p=128), in_=vst[cg][:]), reads=[b_vst[cg]])
        P.phase_end()


def sb_B(P, C, S, scr, consts):
    nc = P.nc
    NB = S // 128
    NQC = S // 512
    with ExitStack() as st:
        sb = lambda n, s, d: st.enter_context(nc.sbuf_tensor(f"sbb_{n}", s, d))
        cb = Buf("consts")
        def ld(name, dt=BF16):
            t = sb(name, [128, 128], dt)
            P.dma("sp", lambda e: e.dma_start(out=t[:], in_=consts[name]), writes=[cb])
            return t
        trii = ld("trii"); nones = ld("nones"); strict = ld("strict"); negti = ld("negtri_incl"); identb = ld("identb")
        zer = sb("zer", [128, 512], BF16)
        P.op("pool", lambda e: e.memset(zer[:], 0.0), writes=[cb])
        qa = [sb(f"qa{i}", [64, S], BF16) for i in range(2)]; b_qa = [Buf("qa0"), Buf("qa1")]
        ka = [sb(f"ka{i}", [64, S], BF16) for i in range(2)]; b_ka = [Buf("ka0"), Buf("ka1")]
        va = [sb(f"va{i}", [128, NB, 64], BF16) for i in range(2)]; b_va = [Buf("va0"), Buf("va1")]
        e1 = [sb(f"e1{i}", [128, 512], F32) for i in range(4)]; b_e1 = [Buf(f"e1{i}") for i in range(4)]
        ew = [sb(f"ew{i}", [128, 512], F32) for i in range(3)]; b_ew = [Buf(f"ew{i}") for i in range(3)]
        strictf = sb("strictf", [128, 128], F32)
        P.op("dve", lambda e: e.tensor_copy(out=strictf[:], in_=strict[:]), reads=[cb], writes=[cb])
        lb = [sb(f"lb{i}", [128, 512], BF16) for i in range(4)]; b_lb = [Buf(f"lb{i}") for i in range(4)]
        acc = sb("acc", [128, 512], F32); b_acc = Buf("acc")
        accb = [sb(f"accb{i}", [128, 512], BF16) for i in range(4)]; b_accb = [Buf(f"accb{i}") for i in range(4)]
        pt = [sb(f"pt{i}", [128, 512], BF16) for i in range(4)]; b_pt = [Buf(f"pt{i}") for i in range(4)]
        ost = [sb(f"ost{i}", [64, 512], BF16) for i in range(2)]; b_ost = [Buf("ost0"), Buf("ost1")]
        n = 0
        for h in range(H):
            hs = h % 2
            P.dma("sp", lambda e, h=h, hs=hs: e.dma_start(out=qa[hs][:], in_=scr["qT"][h * 64:(h + 1) * 64, :]), writes=[b_qa[hs]])
            P.dma("sp", lambda e, h=h, hs=hs: e.dma_start(out=ka[hs][:], in_=scr["kT"][h * 64:(h + 1) * 64, :]), writes=[b_ka[hs]])
            P.dma("sp", lambda e, h=h, hs=hs: e.dma_start(out=va[hs][:], in_=scr["V"][:, h * 64:(h + 1) * 64].rearrange("(kb p) d -> p kb d", p=128)),
                  writes=[b_va[hs]])
            for qc in range(NQC):
                po, pob = ps_next(C)
                P.op("pe", lambda e, po=po, hs=hs: e.matmul(po[0:64, :], lhsT=zer[:, 0:64], rhs=zer[:], start=True, stop=False),
                     reads=[cb], writes=[pob], inc=False)
                nkb = 4 * qc + 4
                first = True
                pq = []
                for kb in range(nkb - 1, -1, -1):
                    off = max(0, kb - 4 * qc) * 128
                    diag = kb >= 4 * qc
                    i2 = n % 3; i3 = n % 4; ip = (n - 1) % 4; n += 1
                    ps, pb = ps_next(C)
                    if ps is po:
                        ps, pb = ps_next(C)
                    P.op("pe", lambda e, kb=kb, off=off, ps=ps, hs=hs, qc=qc: e.matmul(
                        ps[:, off:512], lhsT=ka[hs][:, kb * 128:(kb + 1) * 128], rhs=qa[hs][:, qc * 512 + off:(qc + 1) * 512],
                        start=True, stop=True), reads=[b_ka[hs], b_qa[hs]], writes=[pb])
                    P.op("act", lambda e, off=off, ps=ps, i3=i3: e.activation(out=e1[i3][:, off:512], in_=ps[:, off:512], func=AF.Exp, scale=0.125),
                         reads=[pb], writes=[b_e1[i3]])
                    if off > 0:
                        P.op("pool", lambda e, off=off, i3=i3: e.memset(lb[i3][:, 0:off], 0.0), writes=[b_lb[i3]])
                    P.op("act", lambda e, off=off, i3=i3: e.activation(out=lb[i3][:, off:512], in_=e1[i3][:, off:512], func=AF.Ln, bias=1.0, scale=1.0),
                         reads=[b_e1[i3]], writes=[b_lb[i3]])
                    if diag:
                        P.op("pool", lambda e, off=off, i3=i3: e.tensor_tensor(out=lb[i3][:, off:off + 128], in0=lb[i3][:, off:off + 128], in1=strict[:], op=ALU.mult),
                             reads=[b_lb[i3], cb], writes=[b_lb[i3]])
                        P.op("pool", lambda e, off=off, i3=i3: e.tensor_tensor(out=e1[i3][:, off:off + 128], in0=e1[i3][:, off:off + 128], in1=strictf[:], op=ALU.mult),
                             reads=[b_e1[i3], cb, b_lb[i3]], writes=[b_e1[i3]])
                    if kb > 0:
                        if first:
                            P.op("dve", lambda e, i3=i3: e.tensor_copy(out=accb[i3][:], in_=lb[i3][:]), reads=[b_lb[i3]], writes=[b_accb[i3]])
                        else:
                            P.op("dve", lambda e, i3=i3, ip=ip: e.tensor_tensor(out=accb[i3][:], in0=accb[ip][:], in1=lb[i3][:], op=ALU.add),
                                 reads=[b_lb[i3], b_accb[ip]], writes=[b_accb[i3]])
                    def stage_b(kb=kb, off=off, diag=diag, i3=i3, ip=ip, i2=i2, first=first, hs=hs, qc=qc, po=po, pob=pob):
                        pw, pwb = ps_next(C)
                        if pw is po:
                            pw, pwb = ps_next(C)
                        P.op("pe", lambda e: e.matmul(pw[:, off:512], lhsT=trii[:], rhs=lb[i3][:, off:512], start=True, stop=first),
                             reads=[b_lb[i3], cb], writes=[pwb], inc=first)
                        if not first:
                            P.op("pe", lambda e: e.matmul(pw[:, off:512], lhsT=nones[:], rhs=accb[ip][:, off:512], start=False, stop=True),
                                 reads=[b_accb[ip], cb], writes=[pwb])
                        P.op("act", lambda e: e.activation(out=ew[i2][:, off:512], in_=pw[:, off:512], func=AF.Exp, scale=0.125),
                             reads=[pwb], writes=[b_ew[i2]])
                        P.op("dve", lambda e: e.tensor_tensor(out=pt[i3][:, off:512], in0=e1[i3][:, off:512], in1=ew[i2][:, off:512], op=ALU.mult),
                             reads=[b_e1[i3], b_ew[i2]], writes=[b_pt[i3]])
                        P.op("pe", lambda e: e.matmul(
                            po[0:64, off:512], lhsT=va[hs][:, kb, :], rhs=pt[i3][:, off:512], start=False, stop=(kb == 0)),
                            reads=[b_va[hs], b_pt[i3]], writes=[pob])
                    pq.append(stage_b)
                    if len(pq) > PDEPTH:
                        pq.pop(0)()
                    first = False
                while pq:
                    pq.pop(0)()
                oi = qc % 2
                P.op("act", lambda e, po=po, oi=oi: e.activation(out=ost[oi][:], in_=po[0:64, :], func=AF.Copy), reads=[pob], writes=[b_ost[oi]])
                P.dma("sp", lambda e, h=h, qc=qc, oi=oi: e.dma_start(out=scr["oT"][h * 64:(h + 1) * 64, qc * 512:(qc + 1) * 512], in_=ost[oi][:]),
                      reads=[b_ost[oi]])
        P.phase_end()


from contextlib import ExitStack

GH = 4
DK = 128
DV = 256


def gla_A(P, C, xT, S, w_in, gnorm, wgu, b_gate, scr, consts):
    nc = P.nc
    with ExitStack() as st:
        sb = lambda n, s, d: st.enter_context(nc.sbuf_tensor(f"ga_{n}", s, d))
        g_t = sb("g", [128, NKC], F32); b_g = Buf("g")
        P.dma("sp", lambda e: e.dma_start(out=g_t[:], in_=gnorm.rearrange("(c p) -> p c", p=128), allow_slow_non_contiguous=True), writes=[b_g])
        bg = sb("bg", [128, 4], F32); b_bg = Buf("bg")
        P.dma("sp", lambda e: e.dma_start(out=bg[:], in_=b_gate.rearrange("(c p) -> p c", p=128), allow_slow_non_contiguous=True), writes=[b_bg])
        P.op("dve", lambda e: e.tensor_scalar(out=bg[:], in0=bg[:], scalar1=-1.0, scalar2=None, op0=ALU.mult), reads=[b_bg], writes=[b_bg])
        wgu_t = sb("wgu", [16, 512], BF16); b_wgu = Buf("wgu")
        P.dma("pool", lambda e: e.dma_start(out=wgu_t[:], in_=wgu), writes=[b_wgu])
        ident = sb("ident", [128, 128], F32); b_ident = Buf("ident")
        P.dma("sp", lambda e: e.dma_start(out=ident[:], in_=consts["ident"]), writes=[b_ident])
        x_t, b_x, sq_t, b_sq, rstd, b_rstd, h_t, b_h = load_x_norm(P, C, sb, None, 0, g_t, b_g, "ga")
        stg = [sb(f"stg{i}", [128, TT], BF16) for i in range(3)]; b_stg = [Buf(f"stg{i}") for i in range(3)]
        vst = [sb(f"vst{i}", [128, 4, 512], BF16) for i in range(2)]; b_vst = [Buf("vst0"), Buf("vst1")]
        wv_t = [sb(f"wv{i}", [128, NKC, 512], BF16) for i in range(2)]; b_wv = [Buf("wv0"), Buf("wv1")]
        wt = [sb(f"w{i}", [128, NKC, 256], BF16) for i in range(2)]; bw = [Buf("w0"), Buf("w1")]
        wl = sb("wl", [128, NKC, 16], BF16); b_wl = Buf("wl")
        glT = sb("glT", [16, TT], BF16); b_glT = Buf("glT")
        la = [sb(f"la{i}", [128, TT], F32) for i in range(2)]; b_la = [Buf("la0"), Buf("la1")]
        lat = sb("lat", [128, 4, 512], F32); b_lat = Buf("lat")
        xin = xT.rearrange("(c p) t -> p c t", p=128)
        wr = w_in.rearrange("(c p) f -> p c f", p=128)
        P.dma("pool", lambda e: e.dma_start(out=wl[:], in_=wr[:, :, 2048:2064]), writes=[b_wl])
        ns = 0
        for it in range(S // TT):
            t0 = it * TT
            P.dma("sp", lambda e, t0=t0: e.dma_start(out=x_t[:], in_=xin[:, :, t0:t0 + TT]), writes=[b_x])
            emit_norm(P, C, x_t, b_x, sq_t, b_sq, rstd, b_rstd, h_t, b_h, g_t, b_g)
            for dst, col0, ng, fn in ((scr["qT"], 0, 2, AF.Copy), (scr["kT"], 512, 2, AF.Copy), (scr["sg"], 2064, 4, AF.Silu)):
                for g in range(ng):
                    ws = g % 2
                    c0 = col0 + g * 256
                    P.dma("pool", lambda e, ws=ws, c0=c0: e.dma_start(out=wt[ws][:], in_=wr[:, :, c0:c0 + 256]), writes=[bw[ws]])
                    for j in range(2):
                        ci = g * 2 + j
                        ps, pb = ps_next(C)
                        for kc in range(NKC):
                            P.op("pe", lambda e, kc=kc, ws=ws, j=j, ps=ps: e.matmul(
                                ps[:], lhsT=wt[ws][:, kc, j * 128:(j + 1) * 128], rhs=h_t[:, kc, :], start=(kc == 0), stop=(kc == NKC - 1)),
                                reads=[bw[ws], b_h], writes=[pb], inc=(kc == NKC - 1))
                        s3 = ns % 3; ns += 1
                        P.op("act", lambda e, s3=s3, ps=ps, fn=fn: e.activation(out=stg[s3][:], in_=ps[:], func=fn), reads=[pb], writes=[b_stg[s3]])
                        P.dma("sp", lambda e, s3=s3, ci=ci, dst=dst, t0=t0: e.dma_start(out=dst[ci * 128:(ci + 1) * 128, t0:t0 + TT], in_=stg[s3][:]), reads=[b_stg[s3]])
            for cg in range(3):
                ws = cg % 2
                P.dma("pool", lambda e, cg=cg, ws=ws: e.dma_start(out=wv_t[ws][:], in_=wr[:, :, 512 + cg * 512:512 + (cg + 1) * 512]), writes=[b_wv[ws]])
                for tb in range(4):
                    ps, pb = ps_next(C)
                    for kc in range(NKC):
                        P.op("pe", lambda e, kc=kc, tb=tb, ws=ws, ps=ps: e.matmul(
                            ps[:], lhsT=h_t[:, kc, tb * 128:(tb + 1) * 128], rhs=wv_t[ws][:, kc, :], start=(kc == 0), stop=(kc == NKC - 1)),
                            reads=[b_wv[ws], b_h], writes=[pb], inc=(kc == NKC - 1))
                    P.op("act", lambda e, tb=tb, ws=ws, ps=ps: e.activation(out=vst[ws][:, tb, :], in_=ps[:], func=AF.Copy), reads=[pb], writes=[b_vst[ws]])
                P.dma("sp", lambda e, ws=ws, cg=cg, t0=t0: e.dma_start(
                    out=scr["KV"][t0:t0 + TT, cg * 512:(cg + 1) * 512].rearrange("(tb p) c -> p tb c", p=128), in_=vst[ws][:]), reads=[b_vst[ws]])
            ps, pb = ps_next(C)
            for kc in range(NKC):
                P.op("pe", lambda e, kc=kc, ps=ps: e.matmul(ps[0:16, :], lhsT=wl[:, kc, :], rhs=h_t[:, kc, :], start=(kc == 0), stop=(kc == NKC - 1)),
                     reads=[b_wl, b_h], writes=[pb], inc=(kc == NKC - 1))
            P.op("act", lambda e, ps=ps: e.activation(out=glT[:], in_=ps[0:16, :], func=AF.Copy), reads=[pb], writes=[b_glT])
            for hc in range(4):
                ps, pb = ps_next(C)
                P.op("pe", lambda e, hc=hc, ps=ps: e.matmul(ps[:], lhsT=wgu_t[:, hc * 128:(hc + 1) * 128], rhs=glT[:], start=True, stop=True),
                     reads=[b_wgu, b_glT], writes=[pb])
                li = hc % 2
                P.op("act", lambda e, hc=hc, ps=ps, li=li: e.activation(out=la[li][:], in_=ps[:], func=AF.Exp, bias=bg[:, hc:hc + 1], scale=-1.0),
                     reads=[pb, b_bg], writes=[b_la[li]])
                P.op("act", lambda e, li=li: e.activation(out=la[li][:], in_=la[li][:], func=AF.Ln, bias=1.0, scale=1.0), reads=[b_la[li]], writes=[b_la[li]])
                P.op("dve", lambda e, li=li: e.tensor_scalar(out=la[li][:], in0=la[li][:], scalar1=-1.0 / 16.0, scalar2=None, op0=ALU.mult),
                     reads=[b_la[li]], writes=[b_la[li]])
                P.dma("sp", lambda e, hc=hc, li=li, t0=t0: e.dma_start(out=scr["laT"][hc * 128:(hc + 1) * 128, t0:t0 + TT], in_=la[li][:]), reads=[b_la[li]])
                pst, pstb = ps_next(C)
                for tb in range(4):
                    P.op("pe", lambda e, tb=tb, li=li, pst=pst: e.transpose(pst[:, tb * 128:(tb + 1) * 128], la[li][:, tb * 128:(tb + 1) * 128], ident[:]),
                         reads=[b_la[li], b_ident], writes=[pstb], inc=(tb == 3))
                P.op("dve", lambda e, hc=hc, pst=pst: e.tensor_copy(out=lat[:, :, hc * 128:(hc + 1) * 128], in_=pst[:].rearrange("p (a b) -> p a b", a=4)),
                     reads=[pstb], writes=[b_lat])
            P.dma("sp", lambda e, t0=t0: e.dma_start(out=scr["laK"][t0:t0 + TT, :].rearrange("(tb p) c -> p tb c", p=128), in_=lat[:]), reads=[b_lat])
        P.phase_end()


def gla_B(P, C, S, scr, consts):
    nc = P.nc
    NB = S // 128
    NCH = S // 64
    PW = min(2048, S)
    GRP = min(16, NB)
    with ExitStack() as st:
        sb = lambda n, s, d: st.enter_context(nc.sbuf_tensor(f"gb_{n}", s, d))
        cb = Buf("consts")
        m01 = sb("m01", [128, PW], BF16)
        P.dma("sp", lambda e: e.dma_start(out=m01[:], in_=consts["m01"][:, 0:PW]), writes=[cb])
        umat = sb("umat", [128, 128], F32)
        P.dma("sp", lambda e: e.dma_start(out=umat[:], in_=consts["umat"]), writes=[cb])
        bcaus = sb("bcaus", [128, 128], BF16)
        P.dma("sp", lambda e: e.dma_start(out=bcaus[:], in_=consts["bcaus"]), writes=[cb])
        qd = sb("qd", [128, S], BF16); b_qd = Buf("qd")
        kd = sb("kd", [128, S], BF16); b_kd = Buf("kd")
        bT = sb("bT", [128, S], F32); b_bT = Buf("bT")
        tmp = [sb(f"tmp{i}", [128, PW], F32) for i in range(2)]; b_tmp = [Buf("tmp0"), Buf("tmp1")]
        dcol = sb("dcol", [128, NCH], F32); b_dcol = Buf("dcol")
        ktok = sb("ktok", [128, NB, 128], BF16); b_ktok = Buf("ktok")
        vtok = sb("vtok", [128, NB, 256], BF16); b_vtok = Buf("vtok")
        latk = [sb(f"latk{i}", [128, GRP, 128], F32) for i in range(2)]; b_latk = [Buf("latk0"), Buf("latk1")]
        eu = [sb(f"eu{i}", [128, 128], F32) for i in range(2)]; b_eu = [Buf("eu0"), Buf("eu1")]
        ku = sb("ku", [128, NB, 128], BF16); b_ku = Buf("ku")
        att = [sb(f"att{i}", [128, 128], BF16) for i in range(2)]; b_att = [Buf("att0"), Buf("att1")]
        state = sb("state", [128, 256], F32); b_state = Buf("state")
        stb = [sb(f"stb{i}", [128, 256], BF16) for i in range(2)]; b_stb = [Buf("stb0"), Buf("stb1")]
        ost = [sb(f"ost{i}", [128, 2, 512], BF16) for i in range(2)]; b_ost = [Buf("ost0"), Buf("ost1")]
        for h in range(GH):
            P.dma("sp", lambda e, h=h: e.dma_start(out=qd[:], in_=scr["qT"][h * 128:(h + 1) * 128, :]), writes=[b_qd])
            P.dma("sp", lambda e, h=h: e.dma_start(out=kd[:], in_=scr["kT"][h * 128:(h + 1) * 128, :]), writes=[b_kd])
            P.dma("sp", lambda e, h=h: e.dma_start(out=bT[:], in_=scr["laT"][h * 128:(h + 1) * 128, :]), writes=[b_bT])
            P.dma("sp", lambda e, h=h: e.dma_start(out=ktok[:], in_=scr["KV"][:, h * 128:(h + 1) * 128].rearrange("(kb p) d -> p kb d", p=128)), writes=[b_ktok])
            P.dma("sp", lambda e, h=h: e.dma_start(out=vtok[:], in_=scr["KV"][:, 512 + h * 256:512 + (h + 1) * 256].rearrange("(kb p) d -> p kb d", p=128)), writes=[b_vtok])
            for pc in range(S // PW):
                sl = slice(pc * PW, (pc + 1) * PW)
                P.op("dve", lambda e, sl=sl: e.tensor_tensor_scan(out=bT[:, sl], data0=m01[:], data1=bT[:, sl], initial=0.0, op0=ALU.mult, op1=ALU.add),
                     reads=[b_bT, cb], writes=[b_bT])
                ti = pc % 2
                P.op("act", lambda e, sl=sl, ti=ti: e.activation(out=tmp[ti][:], in_=bT[:, sl], func=AF.Exp), reads=[b_bT], writes=[b_tmp[ti]])
                P.op("dve", lambda e, sl=sl, ti=ti: e.scalar_tensor_tensor(out=qd[:, sl], in0=qd[:, sl], scalar=float(DK) ** -0.5, in1=tmp[ti][:], op0=ALU.mult, op1=ALU.mult),
                     reads=[b_qd, b_tmp[ti]], writes=[b_qd])
                P.op("act", lambda e, sl=sl, ti=ti: e.activation(out=tmp[ti][:], in_=bT[:, sl], func=AF.Exp, scale=-1.0), reads=[b_bT, b_qd], writes=[b_tmp[ti]])
                P.op("dve", lambda e, sl=sl, ti=ti: e.tensor_tensor(out=kd[:, sl], in0=kd[:, sl], in1=tmp[ti][:], op=ALU.mult),
                     reads=[b_kd, b_tmp[ti]], writes=[b_kd])
            P.op("act", lambda e: e.activation(out=dcol[:], in_=bT[:].rearrange("p (n c) -> p n c", c=64)[:, :, 63], func=AF.Exp), reads=[b_bT], writes=[b_dcol])
            for g in range(NB // GRP):
                gi = g % 2
                P.dma("sp", lambda e, g=g, gi=gi, h=h: e.dma_start(
                    out=latk[gi][:], in_=scr["laK"][g * GRP * 128:(g + 1) * GRP * 128, h * 128:(h + 1) * 128].rearrange("(kb p) d -> p kb d", p=128)), writes=[b_latk[gi]])
                for j in range(GRP):
                    tb = g * GRP + j
                    ps, pb = ps_next(C)
                    P.op("pe", lambda e, j=j, gi=gi, ps=ps: e.matmul(ps[:, 0:128], lhsT=umat[:], rhs=latk[gi][:, j, :], start=True, stop=True),
                         reads=[b_latk[gi], cb], writes=[pb])
                    ei = tb % 2
                    P.op("act", lambda e, ei=ei, ps=ps: e.activation(out=eu[ei][:], in_=ps[:, 0:128], func=AF.Exp), reads=[pb], writes=[b_eu[ei]])
                    P.op("dve", lambda e, ei=ei, tb=tb: e.tensor_tensor(out=ku[:, tb, :], in0=ktok[:, tb, :], in1=eu[ei][:], op=ALU.mult),
                         reads=[b_eu[ei], b_ktok], writes=[b_ku])
            P.op("dve", lambda e: e.memset(state[:], 0.0), writes=[b_state])
            P.op("dve", lambda e: e.memset(stb[0][:], 0.0), writes=[b_stb[0]])
            P.op("dve", lambda e: e.memset(stb[1][:], 0.0), writes=[b_stb[1]])
            for tb in range(NB):
                ai = tb % 2
                ps, pb = ps_next(C)
                P.op("pe", lambda e, tb=tb, ps=ps: e.matmul(ps[:, 0:128], lhsT=kd[:, tb * 128:(tb + 1) * 128], rhs=qd[:, tb * 128:(tb + 1) * 128], start=True, stop=True),
                     reads=[b_kd, b_qd], writes=[pb])
                P.op("dve", lambda e, ai=ai, ps=ps: e.tensor_tensor(out=att[ai][:], in0=ps[:, 0:128], in1=bcaus[:], op=ALU.mult), reads=[pb, cb], writes=[b_att[ai]])
                oi = (tb // 4) % 2
                for j in range(2):
                    n = tb * 2 + j
                    si = n % 2
                    r0 = 64 * j
                    po, pob = ps_next(C)
                    for eh in range(2):
                        P.op("pe", lambda e, tb=tb, r0=r0, eh=eh, ai=ai, po=po: e.matmul(
                            po[:, eh * 64:(eh + 1) * 64], lhsT=vtok[r0:r0 + 64, tb, eh * 128:(eh + 1) * 128], rhs=att[ai][r0:r0 + 64, r0:r0 + 64],
                            start=True, stop=False), reads=[b_vtok, b_att[ai]], writes=[pob], inc=False)
                        P.op("pe", lambda e, tb=tb, r0=r0, eh=eh, si=si, po=po: e.matmul(
                            po[:, eh * 64:(eh + 1) * 64], lhsT=stb[si][:, eh * 128:(eh + 1) * 128], rhs=qd[:, tb * 128 + r0:tb * 128 + r0 + 64],
                            start=False, stop=True), reads=[b_stb[si], b_qd], writes=[pob], inc=(eh == 1))
                    c0 = (tb % 4) * 128 + r0
                    P.op("act", lambda e, po=po, oi=oi, c0=c0: e.activation(out=ost[oi][:, :, c0:c0 + 64], in_=po[:, 0:128].rearrange("p (a b) -> p a b", a=2), func=AF.Copy),
                         reads=[pob], writes=[b_ost[oi]])
                    pu, pub = ps_next(C)
                    P.op("pe", lambda e, tb=tb, r0=r0, pu=pu: e.matmul(pu[:, 0:256], lhsT=ku[r0:r0 + 64, tb, :], rhs=vtok[r0:r0 + 64, tb, :], start=True, stop=True),
                         reads=[b_ku, b_vtok], writes=[pub])
                    P.op("dve", lambda e, n=n, pu=pu: e.scalar_tensor_tensor(out=state[:], in0=state[:], scalar=dcol[:, n:n + 1], in1=pu[:, 0:256], op0=ALU.mult, op1=ALU.add),
                         reads=[b_state, b_dcol, pub], writes=[b_state])
                    P.op("dve", lambda e, si=si: e.tensor_copy(out=stb[1 - si][:], in_=state[:]), reads=[b_state], writes=[b_stb[1 - si]])
                if tb % 4 == 3:
                    t0 = (tb // 4) * 512
                    P.dma("sp", lambda e, h=h, oi=oi, t0=t0: e.dma_start(
                        out=scr["oT"][h * 256:(h + 1) * 256, t0:t0 + 512].rearrange("(a p) t -> p a t", p=128), in_=ost[oi][:]), reads=[b_ost[oi]])
        P.phase_end()


def gla_C(P, C, xT, S, w_out, g_out, scr):
    nc = P.nc
    with ExitStack() as st:
        sb = lambda n, s, d: st.enter_context(nc.sbuf_tensor(f"gc_{n}", s, d))
        go = sb("go", [128, 2], F32); b_go = Buf("go")
        P.dma("sp", lambda e: e.dma_start(out=go[:], in_=g_out.rearrange("(c p) -> p c", p=128), allow_slow_non_contiguous=True), writes=[b_go])
        x_t = [sb(f"x{i}", [128, NKC, TT], F32) for i in range(2)]; b_x = [Buf("x0"), Buf("x1")]
        o_t = [sb(f"o{i}", [128, NKC, TT], BF16) for i in range(2)]; b_o = [Buf("o0"), Buf("o1")]
        g_t = [sb(f"gt{i}", [128, NKC, TT], BF16) for i in range(2)]; b_gt = [Buf("gt0"), Buf("gt1")]
        sq = sb("sq", [128, NKC, TT], F32); b_sq = Buf("sq")
        rs = [sb(f"rs{i}", [128, TT], F32) for i in range(2)]; b_rs = [Buf("rs0"), Buf("rs1")]
        wo = [sb(f"wo{i}", [128, NKC, 256], BF16) for i in range(2)]; b_wo = [Buf("wo0"), Buf("wo1")]
        xin = xT.rearrange("(c p) t -> p c t", p=128)
        oin = scr["oT"].rearrange("(c p) t -> p c t", p=128)
        gin = scr["sg"].rearrange("(c p) t -> p c t", p=128)
        wr = w_out.rearrange("(c p) f -> p c f", p=128)
        for it in range(S // TT):
            t0 = it * TT
            xs = it % 2
            P.dma("sp", lambda e, t0=t0, xs=xs: e.dma_start(out=x_t[xs][:], in_=xin[:, :, t0:t0 + TT]), writes=[b_x[xs]])
            P.dma("sp", lambda e, t0=t0, xs=xs: e.dma_start(out=o_t[xs][:], in_=oin[:, :, t0:t0 + TT]), writes=[b_o[xs]])
            P.dma("sp", lambda e, t0=t0, xs=xs: e.dma_start(out=g_t[xs][:], in_=gin[:, :, t0:t0 + TT]), writes=[b_gt[xs]])
            P.op("act", lambda e, xs=xs: e.activation(out=sq[:], in_=o_t[xs][:], func=AF.Square), reads=[b_o[xs]], writes=[b_sq])
            for hh in range(4):
                ps, pb = ps_next(C)
                for j in range(2):
                    P.op("pe", lambda e, hh=hh, j=j, ps=ps: e.matmul(ps[:], lhsT=C.ones_f[:], rhs=sq[:, hh * 2 + j, :], start=(j == 0), stop=(j == 1)),
                         reads=[b_sq, C.b_ones_f], writes=[pb], inc=(j == 1))
                ri = hh % 2
                P.op("dve", lambda e, ri=ri, ps=ps: e.tensor_scalar(out=rs[ri][:], in0=ps[:], scalar1=1.0 / 256, scalar2=1e-6, op0=ALU.mult, op1=ALU.add),
                     reads=[pb], writes=[b_rs[ri]])
                P.op("act", lambda e, ri=ri: e.activation(out=rs[ri][:], in_=rs[ri][:], func=AF.Sqrt), reads=[b_rs[ri]], writes=[b_rs[ri]])
                P.op("dve", lambda e, ri=ri: e.reciprocal(out=rs[ri][:], in_=rs[ri][:]), reads=[b_rs[ri]], writes=[b_rs[ri]])
                for j in range(2):
                    c = hh * 2 + j
                    P.op("dve", lambda e, c=c, j=j, ri=ri, xs=xs: e.scalar_tensor_tensor(out=o_t[xs][:, c, :], in0=o_t[xs][:, c, :], scalar=go[:, j:j + 1], in1=rs[ri][:],
                                                                                       op0=ALU.mult, op1=ALU.mult), reads=[b_o[xs], b_go, b_rs[ri]], writes=[b_o[xs]])
            P.op("pool", lambda e, xs=xs: e.tensor_tensor(out=o_t[xs][:], in0=o_t[xs][:], in1=g_t[xs][:], op=ALU.mult), reads=[b_o[xs], b_gt[xs]], writes=[b_o[xs]])
            for g in range(4):
                ws = g % 2
                P.dma("pool", lambda e, ws=ws, g=g: e.dma_start(out=wo[ws][:], in_=wr[:, :, g * 256:(g + 1) * 256]), writes=[b_wo[ws]])
                for j in range(2):
                    dc = g * 2 + j
                    ps, pb = ps_next(C)
                    for kc in range(NKC):
                        P.op("pe", lambda e, kc=kc, ws=ws, j=j, ps=ps, xs=xs: e.matmul(
                            ps[:], lhsT=wo[ws][:, kc, j * 128:(j + 1) * 128], rhs=o_t[xs][:, kc, :], start=(kc == 0), stop=(kc == NKC - 1)),
                            reads=[b_wo[ws], b_o[xs]], writes=[pb], inc=(kc == NKC - 1))
                    P.op("dve", lambda e, dc=dc, ps=ps, xs=xs: e.tensor_tensor(out=x_t[xs][:, dc, :], in0=x_t[xs][:, dc, :], in1=ps[:], op=ALU.add),
                         reads=[pb, b_x[xs]], writes=[b_x[xs]])
            P.dma("sp", lambda e, t0=t0, xs=xs: e.dma_start(out=xin[:, :, t0:t0 + TT], in_=x_t[xs][:]), reads=[b_x[xs]])
        P.phase_end()


from contextlib import ExitStack

NG = 4


def nsa_A(P, C, xT, S, w_in, gnorm, g_q, g_k, scr, consts):
    nc = P.nc
    with ExitStack() as st:
        sb = lambda n, s, d: st.enter_context(nc.sbuf_tensor(f"na_{n}", s, d))
        g_t = sb("g", [128, NKC], F32); b_g = Buf("g")
        P.dma("sp", lambda e: e.dma_start(out=g_t[:], in_=gnorm.rearrange("(c p) -> p c", p=128), allow_slow_non_contiguous=True), writes=[b_g])
        gq = sb("gq", [128, 1], F32); gk = sb("gk", [128, 1], F32); b_gqk = Buf("gqk")
        for half in range(2):
            P.dma("sp", lambda e, half=half: e.dma_start(out=gq[half * 64:(half + 1) * 64, :], in_=g_q.rearrange("(p o) -> p o", o=1), allow_slow_non_contiguous=True), writes=[b_gqk])
            P.dma("sp", lambda e, half=half: e.dma_start(out=gk[half * 64:(half + 1) * 64, :], in_=g_k.rearrange("(p o) -> p o", o=1), allow_slow_non_contiguous=True), writes=[b_gqk])
        bones = sb("bones", [128, 128], F32); b_bones = Buf("bones")
        P.op("pool", lambda e: e.memset(bones[:], 0.0), writes=[b_bones])
        P.op("pool", lambda e: e.memset(bones[0:64, 0:64], 1.0), writes=[b_bones])
        P.op("pool", lambda e: e.memset(bones[64:128, 64:128], 1.0), writes=[b_bones])
        pm = sb("pm", [128, 128], BF16)
        P.dma("sp", lambda e: e.dma_start(out=pm[:], in_=consts["pm"]), writes=[b_bones])
        x_t, b_x, sq_t, b_sq, rstd, b_rstd, h_t, b_h = load_x_norm(P, C, sb, None, 0, g_t, b_g, "na")
        cos_t = sb("cos", [128, TT], F32); sin_t = sb("sin", [128, TT], F32); b_cs = Buf("cs")
        sqq = [sb(f"sqq{i}", [128, TT], F32) for i in range(2)]; b_sqq = [Buf("sqq0"), Buf("sqq1")]
        rr = [sb(f"rr{i}", [128, TT], F32) for i in range(2)]; b_rr = [Buf("rr0"), Buf("rr1")]
        qn = [sb(f"qn{i}", [128, TT], BF16) for i in range(2)]; b_qn = [Buf("qn0"), Buf("qn1")]
        t1 = [sb(f"t1{i}", [128, TT], F32) for i in range(2)]; b_t1 = [Buf("t10"), Buf("t11")]
        t2 = [sb(f"t2{i}", [128, TT], F32) for i in range(2)]; b_t2 = [Buf("t20"), Buf("t21")]
        stg = [sb(f"stg{i}", [128, TT], BF16) for i in range(3)]; b_stg = [Buf(f"stg{i}") for i in range(3)]
        gst = sb("gst", [48, TT], F32); b_gst = Buf("gst")
        vst = [sb(f"vst{i}", [128, 4, 512], BF16) for i in range(2)]; b_vst = [Buf("vst0"), Buf("vst1")]
        wv_t = [sb(f"wv{i}", [128, NKC, 512], BF16) for i in range(2)]; b_wv = [Buf("wv0"), Buf("wv1")]
        wt = [sb(f"w{i}", [128, NKC, 256], BF16) for i in range(2)]; bw = [Buf("w0"), Buf("w1")]
        wgt = sb("wgt", [128, NKC, 48], BF16); b_wgt = Buf("wgt")
        xin = xT.rearrange("(c p) t -> p c t", p=128)
        wr = w_in.rearrange("(c p) f -> p c f", p=128)
        P.dma("pool", lambda e: e.dma_start(out=wgt[:], in_=wr[:, :, 2560:2608]), writes=[b_wgt])
        ns = 0; nq = 0; ng_ = 0
        jobs = [(scr["qT"], 0, 0, 1024, "q"), (scr["kT3"], 0, 1024, 256, "k"), (scr["vcT"], 0, 1280, 256, "raw"),
                (scr["kT3"], 256, 1536, 256, "k"), (scr["kT3"], 512, 2048, 256, "k")]
        for it in range(S // TT):
            t0 = it * TT
            P.dma("sp", lambda e, t0=t0: e.dma_start(out=x_t[:], in_=xin[:, :, t0:t0 + TT]), writes=[b_x])
            P.dma("sp", lambda e, t0=t0: e.dma_start(out=cos_t[:], in_=consts["cos"][:, t0:t0 + TT]), writes=[b_cs])
            P.dma("sp", lambda e, t0=t0: e.dma_start(out=sin_t[:], in_=consts["sin"][:, t0:t0 + TT]), writes=[b_cs])
            emit_norm(P, C, x_t, b_x, sq_t, b_sq, rstd, b_rstd, h_t, b_h, g_t, b_g)
            for dst, row0, col0, ncols, mode in jobs:
                for g in range(ncols // 256):
                    ws = ng_ % 2; ng_ += 1
                    c0 = col0 + g * 256
                    P.dma("pool", lambda e, ws=ws, c0=c0: e.dma_start(out=wt[ws][:], in_=wr[:, :, c0:c0 + 256]), writes=[bw[ws]])
                    for j in range(2):
                        ci = g * 2 + j
                        ps, pb = ps_next(C)
                        for kc in range(NKC):
                            P.op("pe", lambda e, kc=kc, ws=ws, j=j, ps=ps: e.matmul(
                                ps[:], lhsT=wt[ws][:, kc, j * 128:(j + 1) * 128], rhs=h_t[:, kc, :], start=(kc == 0), stop=(kc == NKC - 1)),
                                reads=[bw[ws], b_h], writes=[pb], inc=(kc == NKC - 1))
                        s3 = ns % 3; ns += 1
                        r0 = row0 + ci * 128
                        if mode == "raw":
                            P.op("act", lambda e, s3=s3, ps=ps: e.activation(out=stg[s3][:], in_=ps[:], func=AF.Copy), reads=[pb], writes=[b_stg[s3]])
                        else:
                            gcol = gq if mode == "q" else gk
                            i = nq % 2; nq += 1
                            P.op("act", lambda e, i=i, ps=ps: e.activation(out=sqq[i][:], in_=ps[:], func=AF.Square), reads=[pb], writes=[b_sqq[i]])
                            ps2, pb2 = ps_next(C)
                            P.op("pe", lambda e, i=i, ps2=ps2: e.matmul(ps2[:], lhsT=bones[:], rhs=sqq[i][:], start=True, stop=True),
                                 reads=[b_bones, b_sqq[i]], writes=[pb2])
                            P.op("dve", lambda e, i=i, ps2=ps2: e.tensor_scalar(out=rr[i][:], in0=ps2[:], scalar1=1.0 / DH, scalar2=1e-6, op0=ALU.mult, op1=ALU.add),
                                 reads=[pb2], writes=[b_rr[i]])
                            P.op("act", lambda e, i=i: e.activation(out=rr[i][:], in_=rr[i][:], func=AF.Sqrt), reads=[b_rr[i]], writes=[b_rr[i]])
                            P.op("dve", lambda e, i=i: e.reciprocal(out=rr[i][:], in_=rr[i][:]), reads=[b_rr[i]], writes=[b_rr[i]])
                            P.op("dve", lambda e, i=i, ps=ps, gcol=gcol: e.scalar_tensor_tensor(out=qn[i][:], in0=ps[:], scalar=gcol[:, 0:1], in1=rr[i][:], op0=ALU.mult, op1=ALU.mult),
                                 reads=[pb, b_gqk, b_rr[i]], writes=[b_qn[i]])
                            ps3, pb3 = ps_next(C)
                            P.op("pe", lambda e, i=i, ps3=ps3: e.matmul(ps3[:], lhsT=pm[:], rhs=qn[i][:], start=True, stop=True),
                                 reads=[b_bones, b_qn[i]], writes=[pb3])
                            P.op("pool", lambda e, i=i: e.tensor_tensor(out=t1[i][:], in0=qn[i][:], in1=cos_t[:], op=ALU.mult), reads=[b_qn[i], b_cs], writes=[b_t1[i]])
                            P.op("dve", lambda e, i=i, ps3=ps3: e.tensor_tensor(out=t2[i][:], in0=ps3[:], in1=sin_t[:], op=ALU.mult), reads=[pb3, b_cs], writes=[b_t2[i]])
                            P.op("dve", lambda e, i=i, s3=s3: e.tensor_tensor(out=stg[s3][:], in0=t1[i][:], in1=t2[i][:], op=ALU.add),
                                 reads=[b_t1[i], b_t2[i]], writes=[b_stg[s3]])
                        P.dma("sp", lambda e, s3=s3, r0=r0, dst=dst, t0=t0: e.dma_start(out=dst[r0:r0 + 128, t0:t0 + TT], in_=stg[s3][:]), reads=[b_stg[s3]])
            for cg in range(3):
                ws = cg % 2
                P.dma("pool", lambda e, cg=cg, ws=ws: e.dma_start(out=wv_t[ws][:], in_=wr[:, :, 1024 + cg * 512:1024 + (cg + 1) * 512]), writes=[b_wv[ws]])
                for tb in range(4):
                    ps, pb = ps_next(C)
                    for kc in range(NKC):
                        P.op("pe", lambda e, kc=kc, tb=tb, ws=ws, ps=ps: e.matmul(
                            ps[:], lhsT=h_t[:, kc, tb * 128:(tb + 1) * 128], rhs=wv_t[ws][:, kc, :], start=(kc == 0), stop=(kc == NKC - 1)),
                            reads=[b_wv[ws], b_h], writes=[pb], inc=(kc == NKC - 1))
                    P.op("act", lambda e, tb=tb, ws=ws, ps=ps: e.activation(out=vst[ws][:, tb, :], in_=ps[:], func=AF.Copy), reads=[pb], writes=[b_vst[ws]])
                P.dma("sp", lambda e, ws=ws, cg=cg, t0=t0: e.dma_start(
                    out=scr["KV"][t0:t0 + TT, cg * 512:(cg + 1) * 512].rearrange("(tb p) c -> p tb c", p=128), in_=vst[ws][:]), reads=[b_vst[ws]])
            ps, pb = ps_next(C)
            for kc in range(NKC):
                P.op("pe", lambda e, kc=kc, ps=ps: e.matmul(ps[0:48, :], lhsT=wgt[:, kc, :], rhs=h_t[:, kc, :], start=(kc == 0), stop=(kc == NKC - 1)),
                     reads=[b_wgt, b_h], writes=[pb], inc=(kc == NKC - 1))
            P.op("act", lambda e, ps=ps: e.activation(out=gst[:], in_=ps[0:48, :], func=AF.Sigmoid), reads=[pb], writes=[b_gst])
            P.dma("sp", lambda e, t0=t0: e.dma_start(out=scr["gT"][:, t0:t0 + TT], in_=gst[:]), reads=[b_gst])
        P.phase_end()


def nsa_B(P, C, S, scr, consts, pos_k, pos_v, w_ck, w_cv, g_k, dbg=None):
    nc = P.nc
    NB = S // 128
    NCMP = (S - 32) // 16 + 1
    NNC = (NCMP + 127) // 128
    with ExitStack() as st:
        sb = lambda n, s, d: st.enter_context(nc.sbuf_tensor(f"nb_{n}", s, d))
        cb = Buf("consts")
        identb = sb("identb", [128, 128], BF16)
        P.dma("sp", lambda e: e.dma_start(out=identb[:], in_=consts["identb"]), writes=[cb])
        negtri4 = sb("negtri4", [128, 512], BF16)
        P.dma("sp", lambda e: e.dma_start(out=negtri4[:], in_=consts["negtri4"]), writes=[cb])
        negle4 = sb("negle4", [128, 512], BF16)
        P.dma("sp", lambda e: e.dma_start(out=negle4[:], in_=consts["negle4"]), writes=[cb])
        emat = sb("emat", [128, S], BF16)
        P.dma("sp", lambda e: e.dma_start(out=emat[:], in_=consts["emat"][:, 0:S]), writes=[cb])
        cover = sb("cover", [128, NNC, 128], F32)
        P.dma("sp", lambda e: e.dma_start(out=cover[:], in_=consts["cover"][0:NNC * 128, :].rearrange("(c p) j -> p c j", p=128)), writes=[cb])
        wck = sb("wck", [64, 32, 64], BF16); wcv = sb("wcv", [64, 32, 64], BF16)
        P.dma("pool", lambda e: e.dma_start(out=wck[:], in_=w_ck.rearrange("l d e -> d l e")), writes=[cb])
        P.dma("pool", lambda e: e.dma_start(out=wcv[:], in_=w_cv.rearrange("l d e -> d l e")), writes=[cb])
        pkT = sb("pkT", [64, 32], BF16); pvT = sb("pvT", [64, 32], BF16)
        P.dma("pool", lambda e: e.dma_start(out=pkT[:], in_=pos_k.rearrange("l d -> d l"), allow_slow_non_contiguous=True), writes=[cb])
        P.dma("pool", lambda e: e.dma_start(out=pvT[:], in_=pos_v.rearrange("l d -> d l"), allow_slow_non_contiguous=True), writes=[cb])
        gk = sb("gk", [64, 1], F32)
        P.dma("sp", lambda e: e.dma_start(out=gk[:], in_=g_k.rearrange("(p o) -> p o", o=1), allow_slow_non_contiguous=True), writes=[cb])
        onesb = sb("onesb", [1, 128], BF16)
        P.op("pool", lambda e: e.memset(onesb[:], 1.0), writes=[cb])
        bk = sb("bk", [64, 1], F32); bvrow = sb("bvrow", [1, 64], BF16); b_bias = Buf("bias")
        ps, pb = ps_next(C)
        for l in range(32):
            P.op("pe", lambda e, l=l, ps=ps: e.matmul(ps[0:64, 0:1], lhsT=wck[:, l, :], rhs=pkT[:, l:l + 1], start=(l == 0), stop=(l == 31)),
                 reads=[cb], writes=[pb], inc=(l == 31))
        P.op("dve", lambda e, ps=ps: e.tensor_copy(out=bk[:], in_=ps[0:64, 0:1]), reads=[pb], writes=[b_bias])
        ps, pb = ps_next(C)
        for l in range(32):
            P.op("pe", lambda e, l=l, ps=ps: e.matmul(ps[0:1, 0:64], lhsT=pvT[:, l:l + 1], rhs=wcv[:, l, :], start=(l == 0), stop=(l == 31)),
                 reads=[cb], writes=[pb], inc=(l == 31))
        P.op("dve", lambda e, ps=ps: e.tensor_copy(out=bvrow[:], in_=ps[0:1, 0:64]), reads=[pb], writes=[b_bias])
        kcT = sb("kcT", [64, S], BF16); vcT = sb("vcT", [64, S], BF16); b_kv = Buf("kv")
        ksT = sb("ksT", [64, S], BF16); kwT = sb("kwT", [64, S], BF16)
        vsa = sb("vsa", [128, NB, 65], BF16); vwa = sb("vwa", [128, NB, 65], BF16); vca = sb("vca", [128, NNC, 65], BF16); b_vca = Buf("vca")
        P.op("pool", lambda e: e.memset(vsa[:, :, 64:65], 1.0), writes=[b_kv])
        P.op("pool", lambda e: e.memset(vwa[:, :, 64:65], 1.0), writes=[b_kv])
        P.op("pool", lambda e: e.memset(vca[:], 0.0), writes=[b_vca])
        P.op("pool", lambda e: e.memset(vca[:, :, 64:65], 1.0), writes=[b_vca])
        kcm = sb("kcm", [64, NNC * 128], BF16); b_kcm = Buf("kcm")
        P.op("pool", lambda e: e.memset(kcm[:], 0.0), writes=[b_kcm])
        thr = sb("thr", [128, 1], F32)
        sqc = sb("sqc", [64, 512], F32); rrc = sb("rrc", [64, 512], F32); kcf = sb("kcf", [64, 512], F32); b_c1 = Buf("c1")
        qblk = [sb(f"qblk{i}", [64, 4, 128], BF16) for i in range(2)]; b_qblk = [Buf("qb0"), Buf("qb1")]
        cm4 = [sb(f"cm4{i}", [128, NNC, 4, 128], BF16) for i in range(2)]; b_cm4 = [Buf("cm0"), Buf("cm1")]
        alw = [sb(f"alw{i}", [128, 128], F32) for i in range(2)]; adc = [sb(f"adc{i}", [128, 128], F32) for i in range(2)]; b_al = [Buf("al0"), Buf("al1")]
        grow = [sb(f"grow{i}", [65, 3, 4, 128], F32) for i in range(2)]; b_grow = [Buf("gr0"), Buf("gr1")]
        pcf = [sb(f"pcf{i}", [128, 512], F32) for i in range(NNC)]; b_pcf = [Buf(f"pcf{i}") for i in range(NNC)]
        pcb = [sb(f"pcb{i}", [128, 512], BF16) for i in range(2)]; b_pcb = [Buf("pcb0"), Buf("pcb1")]
        osb = [sb(f"osb{i}", [65, 512], F32) for i in range(4)]; b_osb = [Buf(f"osb{i}") for i in range(4)]
        rec = [sb(f"rec{i}", [65, 512], F32) for i in range(4)]; b_rec = [Buf(f"rec{i}") for i in range(4)]
        impf = sb("impf", [128, 128], F32); imp2 = sb("imp2", [128, 128], F32); mx8 = sb("mx8", [128, 8], F32); mx8b = sb("mx8b", [128, 8], F32); b_imp = Buf("imp")
        mbq = sb("mbq", [128, 128], BF16); b_mbq = Buf("mbq")
        mbT4s = [sb(f"mbT4{i}", [128, 4, 128], BF16) for i in range(2)]; b_mbTs = [Buf("mbT0"), Buf("mbT1")]
        pt = [sb(f"pt{i}", [128, 512], BF16) for i in range(4)]; b_pt = [Buf(f"pt{i}") for i in range(4)]
        oacc = sb("oacc", [64, 512], F32); otmp = sb("otmp", [64, 512], F32); b_oacc = Buf("oacc")
        ost = [sb(f"ost{i}", [64, 512], BF16) for i in range(2)]; b_ost = [Buf("ost0"), Buf("ost1")]
        nptl = [0]
        for g in range(NG):
            P.dma("sp", lambda e, g=g: e.dma_start(out=kcT[:], in_=scr["kT3"][g * 64:(g + 1) * 64, :]), writes=[b_kv])
            P.dma("sp", lambda e, g=g: e.dma_start(out=ksT[:], in_=scr["kT3"][256 + g * 64:256 + (g + 1) * 64, :]), writes=[b_kv])
            P.dma("sp", lambda e, g=g: e.dma_start(out=kwT[:], in_=scr["kT3"][512 + g * 64:512 + (g + 1) * 64, :]), writes=[b_kv])
            P.dma("sp", lambda e, g=g: e.dma_start(out=vcT[:], in_=scr["vcT"][g * 64:(g + 1) * 64, :]), writes=[b_kv])
            P.dma("sp", lambda e, g=g: e.dma_start(out=vsa[:, :, 0:64], in_=scr["KV"][:, 768 + g * 64:768 + (g + 1) * 64].rearrange("(kb p) d -> p kb d", p=128)), writes=[b_kv])
            P.dma("sp", lambda e, g=g: e.dma_start(out=vwa[:, :, 0:64], in_=scr["KV"][:, 1280 + g * 64:1280 + (g + 1) * 64].rearrange("(kb p) d -> p kb d", p=128)), writes=[b_kv])
            for c0 in range(0, NCMP, 512):
                nn = min(512, NCMP - c0)
                ps, pb = ps_next(C)
                for l in range(32):
                    P.op("pe", lambda e, l=l, ps=ps, c0=c0, nn=nn: e.matmul(ps[0:64, 0:nn], lhsT=wck[:, l, :], rhs=kcT[:, c0 * 16 + l:c0 * 16 + l + (nn - 1) * 16 + 1:16],
                                                                         start=(l == 0), stop=(l == 31)), reads=[cb, b_kv], writes=[pb], inc=(l == 31))
                P.op("dve", lambda e, ps=ps, nn=nn: e.tensor_scalar(out=kcf[:, 0:nn], in0=ps[0:64, 0:nn], scalar1=bk[:, 0:1], scalar2=None, op0=ALU.add),
                     reads=[pb, b_bias], writes=[b_c1])
                P.op("act", lambda e, nn=nn: e.activation(out=sqc[:, 0:nn], in_=kcf[:, 0:nn], func=AF.Square), reads=[b_c1], writes=[b_c1])
                ps2, pb2 = ps_next(C)
                P.op("pe", lambda e, ps2=ps2, nn=nn: e.matmul(ps2[0:64, 0:nn], lhsT=C.ones_f[0:64, 0:64], rhs=sqc[:, 0:nn], start=True, stop=True),
                     reads=[b_c1, C.b_ones_f], writes=[pb2])
                P.op("dve", lambda e, ps2=ps2, nn=nn: e.tensor_scalar(out=rrc[:, 0:nn], in0=ps2[0:64, 0:nn], scalar1=1.0 / 64, scalar2=1e-6, op0=ALU.mult, op1=ALU.add),
                     reads=[pb2], writes=[b_c1])
                P.op("act", lambda e, nn=nn: e.activation(out=rrc[:, 0:nn], in_=rrc[:, 0:nn], func=AF.Sqrt), reads=[b_c1], writes=[b_c1])
                P.op("dve", lambda e, nn=nn: e.reciprocal(out=rrc[:, 0:nn], in_=rrc[:, 0:nn]), reads=[b_c1], writes=[b_c1])
                P.op("dve", lambda e, nn=nn, c0=c0: e.scalar_tensor_tensor(out=kcm[:, c0:c0 + nn], in0=kcf[:, 0:nn], scalar=gk[:, 0:1], in1=rrc[:, 0:nn], op0=ALU.mult, op1=ALU.mult),
                     reads=[b_c1, cb], writes=[b_kcm])
            for nch in range(NNC):
                n0 = nch * 128
                nn = min(128, NCMP - n0)
                ps, pb = ps_next(C)
                for l in range(32):
                    P.op("pe", lambda e, l=l, ps=ps, n0=n0, nn=nn: e.matmul(ps[0:nn, 0:64], lhsT=vcT[:, n0 * 16 + l:n0 * 16 + l + (nn - 1) * 16 + 1:16], rhs=wcv[:, l, :],
                                                                         start=(l == 0), stop=False), reads=[cb, b_kv], writes=[pb], inc=False)
                P.op("pe", lambda e, ps=ps, nn=nn: e.matmul(ps[0:nn, 0:64], lhsT=onesb[0:1, 0:nn], rhs=bvrow[:], start=False, stop=True), reads=[cb, b_bias], writes=[pb])
                P.op("act", lambda e, ps=ps, nn=nn, nch=nch: e.activation(out=vca[0:nn, nch, 0:64], in_=ps[0:nn, 0:64], func=AF.Copy), reads=[pb], writes=[b_vca])
            if NCMP % 128:
                pass
            def cmp_stage(qb):
                    t0 = qb * 128
                    qi = qb % 2
                    o0 = 3 * qi
                    P.dma("sp", lambda e, g=g, qi=qi, t0=t0: e.dma_start(out=qblk[qi][:], in_=scr["qT"][g * 256:(g + 1) * 256, t0:t0 + 128].rearrange("(p d) t -> d p t", d=64)),
                          writes=[b_qblk[qi]])
                    ncn = min(NNC, (8 * qb + 6) // 128 + 1)
                    for p4 in range(4):
                        P.dma("sp", lambda e, qi=qi, t0=t0, p4=p4, ncn=ncn: e.dma_start(
                            out=cm4[qi][:, 0:ncn, p4, :], in_=consts["cmask"][0:ncn * 128, t0:t0 + 128].rearrange("(c p) t -> p c t", p=128)), writes=[b_cm4[qi]])
                    P.dma("sp", lambda e, qi=qi, qb=qb: e.dma_start(out=alw[qi][:], in_=consts["allow"][qb]), writes=[b_al[qi]])
                    P.dma("sp", lambda e, qi=qi, qb=qb: e.dma_start(out=adc[qi][:], in_=consts["addc"][qb]), writes=[b_al[qi]])
                    P.dma("sp", lambda e, qi=qi, g=g, t0=t0: e.dma_start(out=grow[qi][64:65, :, :, :],
                                                                         in_=scr["gT"].rearrange("(h c) t -> c h t", c=3)[:, g * 4:(g + 1) * 4, t0:t0 + 128].rearrange("(o c) h t -> o c h t", o=1)),
                          writes=[b_grow[qi]])
                    qr = qblk[qi][:].rearrange("d p t -> d (p t)")
                    poc, pocb = ps_next(C)
                    for nch in range(ncn):
                        ps, pb = ps_next(C)
                        if ps is poc:
                            ps, pb = ps_next(C)
                        P.op("pe", lambda e, nch=nch, ps=ps, qr=qr: e.matmul(ps[:], lhsT=kcm[:, nch * 128:(nch + 1) * 128], rhs=qr, start=True, stop=False),
                             reads=[b_kcm, b_qblk[qi]], writes=[pb], inc=False)
                        P.op("pe", lambda e, nch=nch, ps=ps, qi=qi: e.matmul(ps[:], lhsT=identb[:], rhs=cm4[qi][:, nch, :, :].rearrange("p a b -> p (a b)"), start=False, stop=True),
                             reads=[cb, b_cm4[qi]], writes=[pb])
                        P.op("act", lambda e, nch=nch, ps=ps: e.activation(out=pcf[nch][:], in_=ps[:], func=AF.Exp, scale=0.125), reads=[pb], writes=[b_pcf[nch]])
                        bi = nch % 2
                        P.op("pool", lambda e, nch=nch, bi=bi: e.tensor_copy(out=pcb[bi][:], in_=pcf[nch][:]), reads=[b_pcf[nch]], writes=[b_pcb[bi]])
                        P.op("pe", lambda e, nch=nch, bi=bi, poc=poc, ncn=ncn: e.matmul(poc[0:65, :], lhsT=vca[:, nch, :], rhs=pcb[bi][:], start=(nch == 0), stop=(nch == ncn - 1)),
                             reads=[b_vca, b_pcb[bi]], writes=[pocb])
                    P.op("act", lambda e, poc=poc: e.activation(out=osb[o0][:], in_=poc[0:65, :], func=AF.Copy), reads=[pocb], writes=[b_osb[o0]])
                    P.op("dve", lambda e: e.tensor_scalar(out=rec[o0][64:65, :], in0=osb[o0][64:65, :], scalar1=1e-30, scalar2=None, op0=ALU.add), reads=[b_osb[o0]], writes=[b_rec[o0]])
                    P.op("dve", lambda e: e.reciprocal(out=rec[o0][64:65, :], in_=rec[o0][64:65, :]), reads=[b_rec[o0]], writes=[b_rec[o0]])
                    pbc, pbcb = ps_next(C)
                    P.op("pe", lambda e, pbc=pbc: e.matmul(pbc[:], lhsT=C.ones_f[64:65, :], rhs=rec[o0][64:65, :], start=True, stop=True), reads=[b_rec[o0], C.b_ones_f], writes=[pbcb])
                    for nch in range(ncn):
                        P.op("dve", lambda e, nch=nch, pbc=pbc: e.tensor_tensor(out=pcf[nch][:], in0=pcf[nch][:], in1=pbc[:], op=ALU.mult), reads=[b_pcf[nch], pbcb], writes=[b_pcf[nch]])
                    pim, pimb = ps_next(C)
                    k = 0
                    for nch in range(ncn):
                        for p4 in range(4):
                            P.op("pe", lambda e, nch=nch, p4=p4, pim=pim, k=k, ncn=ncn: e.matmul(pim[:, 0:128], lhsT=pcf[nch][:, p4 * 128:(p4 + 1) * 128], rhs=cover[:, nch, :],
                                                                                              start=(k == 0), stop=(k == ncn * 4 - 1)),
                                 reads=[b_pcf[nch], cb], writes=[pimb], inc=(k == ncn * 4 - 1))
                            k += 1
                    P.op("dve", lambda e, pim=pim, qi=qi: e.tensor_tensor(out=impf[:], in0=pim[:, 0:128], in1=alw[qi][:], op=ALU.mult), reads=[pimb, b_al[qi]], writes=[b_imp])
                    P.op("dve", lambda e, qi=qi: e.tensor_tensor(out=impf[:], in0=impf[:], in1=adc[qi][:], op=ALU.add), reads=[b_imp, b_al[qi]], writes=[b_imp])
                    P.op("dve", lambda e: e.max(out=mx8[:], in_=impf[:]), reads=[b_imp], writes=[b_imp], strict=True)
                    P.op("dve", lambda e: e.match_replace(out=imp2[:], in_to_replace=mx8[:], in_values=impf[:], imm_value=-2.0), reads=[b_imp], writes=[b_imp], strict=True)
                    P.op("dve", lambda e: e.max(out=mx8b[:], in_=imp2[:]), reads=[b_imp], writes=[b_imp], strict=True)
                    P.op("dve", lambda e: e.tensor_reduce(out=thr[:], in_=mx8b[:], axis=AX.X, op=ALU.min), reads=[b_imp], writes=[b_imp], strict=True)
                    P.op("dve", lambda e: e.tensor_scalar(out=imp2[:], in0=impf[:], scalar1=thr[:, 0:1], scalar2=None, op0=ALU.is_ge), reads=[b_imp], writes=[b_imp], strict=True)
                    P.op("dve", lambda e: e.tensor_scalar(out=mbq[:], in0=imp2[:], scalar1=1.0, scalar2=240000.0, op0=ALU.subtract, op1=ALU.mult), reads=[b_imp], writes=[b_mbq], strict=True)
                    if dbg is not None and g == 0 and qb == NB - 1 and "dbg" in scr:
                        P.dma("sp", lambda e: e.dma_start(out=scr["dbg"][:, 0:128], in_=impf[:]), reads=[b_imp])
                        P.dma("sp", lambda e: e.dma_start(out=scr["dbg"][:, 128:256], in_=imp2[:]), reads=[b_imp])
                        P.dma("sp", lambda e: e.dma_start(out=scr["dbg"][:, 256:264], in_=mx8[:]), reads=[b_imp])
                        P.dma("sp", lambda e: e.dma_start(out=scr["dbg"][:, 264:272], in_=mx8b[:]), reads=[b_imp])
                        P.dma("sp", lambda e: e.dma_start(out=scr["dbg"][:, 272:273], in_=thr[:], allow_slow_non_contiguous=True), reads=[b_imp])
                    pmt, pmtb = ps_next(C)
                    P.op("pe", lambda e, pmt=pmt: e.matmul(pmt[:, 0:128], lhsT=mbq[:], rhs=identb[:], start=True, stop=True), reads=[b_mbq, cb], writes=[pmtb])
                    for p4 in range(4):
                        P.op("act" if p4 % 2 else "dve", (lambda e, p4=p4, pmt=pmt: e.activation(out=mbT4s[qi][:, p4, :], in_=pmt[:, 0:128], func=AF.Copy)) if p4 % 2 else
                             (lambda e, p4=p4, pmt=pmt: e.tensor_copy(out=mbT4s[qi][:, p4, :], in_=pmt[:, 0:128])), reads=[pmtb], writes=[b_mbTs[qi]])
            def att_stage(qb):
                    t0 = qb * 128
                    qi = qb % 2
                    o0 = 3 * qi
                    poc = None
                    qr = qblk[qi][:].rearrange("d p t -> d (p t)")
                    pos_, posb = ps_next(C)
                    pq = []
                    for kb in range(qb + 1):
                        ps, pb = ps_next(C)
                        while ps is pos_ or ps is poc:
                            ps, pb = ps_next(C)
                        P.op("pe", lambda e, kb=kb, ps=ps, qr=qr: e.matmul(ps[:], lhsT=ksT[:, kb * 128:(kb + 1) * 128], rhs=qr, start=True, stop=False),
                             reads=[b_kv, b_qblk[qi]], writes=[pb], inc=False)
                        if kb == qb:
                            P.op("pe", lambda e, ps=ps: e.matmul(ps[:], lhsT=identb[:], rhs=negtri4[:], start=False, stop=False), reads=[cb], writes=[pb], inc=False)
                        P.op("pe", lambda e, kb=kb, ps=ps: e.matmul(ps[:], lhsT=emat[:, kb * 128:(kb + 1) * 128], rhs=mbT4s[qi][:].rearrange("p a b -> p (a b)"), start=False, stop=True),
                             reads=[cb, b_mbTs[qi]], writes=[pb])
                        pi = nptl[0] % 4; nptl[0] += 1
                        P.op("act", lambda e, ps=ps, pi=pi: e.activation(out=pt[pi][:], in_=ps[:], func=AF.Exp, scale=0.125), reads=[pb], writes=[b_pt[pi]])
                        def pv(kb=kb, pi=pi, pos_=pos_, qb=qb, posb=posb):
                            P.op("pe", lambda e: e.matmul(pos_[0:65, :], lhsT=vsa[:, kb, :], rhs=pt[pi][:], start=(kb == 0), stop=(kb == qb)),
                                 reads=[b_kv, b_pt[pi]], writes=[posb])
                        pq.append(pv)
                        if len(pq) > 2:
                            pq.pop(0)()
                    while pq:
                        pq.pop(0)()
                    pow_, powb = ps_next(C)
                    while pow_ is pos_ or pow_ is poc:
                        pow_, powb = ps_next(C)
                    kb0 = max(0, qb - 4)
                    pq = []
                    for kb in range(kb0, qb + 1):
                        ps, pb = ps_next(C)
                        while ps is pos_ or ps is poc or ps is pow_:
                            ps, pb = ps_next(C)
                        msk = negtri4 if kb == qb else (negle4 if kb == qb - 4 else None)
                        P.op("pe", lambda e, kb=kb, ps=ps, qr=qr, msk=msk: e.matmul(ps[:], lhsT=kwT[:, kb * 128:(kb + 1) * 128], rhs=qr, start=True, stop=(msk is None)),
                             reads=[b_kv, b_qblk[qi]], writes=[pb], inc=(msk is None))
                        if msk is not None:
                            P.op("pe", lambda e, ps=ps, msk=msk: e.matmul(ps[:], lhsT=identb[:], rhs=msk[:], start=False, stop=True), reads=[cb], writes=[pb])
                        pi = nptl[0] % 4; nptl[0] += 1
                        P.op("act", lambda e, ps=ps, pi=pi: e.activation(out=pt[pi][:], in_=ps[:], func=AF.Exp, scale=0.125), reads=[pb], writes=[b_pt[pi]])
                        def pv(kb=kb, pi=pi, pow_=pow_, qb=qb, kb0=kb0, powb=powb):
                            P.op("pe", lambda e: e.matmul(pow_[0:65, :], lhsT=vwa[:, kb, :], rhs=pt[pi][:], start=(kb == kb0), stop=(kb == qb)),
                                 reads=[b_kv, b_pt[pi]], writes=[powb])
                        pq.append(pv)
                        if len(pq) > 2:
                            pq.pop(0)()
                    while pq:
                        pq.pop(0)()
                    for c, (po_, pob_) in enumerate(((None, None), (pos_, posb), (pow_, powb))):
                        ci = o0 if c == 0 else c
                        if c > 0:
                            P.op("act", lambda e, c=c, ci=ci, po_=po_: e.activation(out=osb[ci][:], in_=po_[0:65, :], func=AF.Copy), reads=[pob_], writes=[b_osb[ci]])
                            P.op("dve", lambda e, c=c, ci=ci: e.tensor_scalar(out=rec[ci][64:65, :], in0=osb[ci][64:65, :], scalar1=1e-30, scalar2=None, op0=ALU.add), reads=[b_osb[ci]], writes=[b_rec[ci]])
                            P.op("dve", lambda e, c=c, ci=ci: e.reciprocal(out=rec[ci][64:65, :], in_=rec[ci][64:65, :]), reads=[b_rec[ci]], writes=[b_rec[ci]])
                        P.op("dve", lambda e, c=c, ci=ci, qi=qi: e.tensor_tensor(out=rec[ci][64:65, :], in0=rec[ci][64:65, :], in1=grow[qi][64:65, c, :, :].rearrange("o h t -> o (h t)"), op=ALU.mult),
                             reads=[b_rec[ci], b_grow[qi]], writes=[b_rec[ci]])
                        if dbg is not None and c != dbg:
                            P.op("dve", lambda e, c=c, ci=ci: e.memset(rec[ci][64:65, :], 0.0), reads=[b_rec[ci]], writes=[b_rec[ci]])
                        pbc2, pbc2b = ps_next(C)
                        while pbc2 is pos_ or pbc2 is pow_:
                            pbc2, pbc2b = ps_next(C)
                        P.op("pe", lambda e, c=c, ci=ci, pbc2=pbc2: e.matmul(pbc2[0:64, :], lhsT=C.ones_f[64:65, 0:64], rhs=rec[ci][64:65, :], start=True, stop=True),
                             reads=[b_rec[ci], C.b_ones_f], writes=[pbc2b])
                        if c == 0:
                            P.op("dve", lambda e, pbc2=pbc2: e.tensor_tensor(out=oacc[:], in0=osb[o0][0:64, :], in1=pbc2[0:64, :], op=ALU.mult), reads=[b_osb[o0], pbc2b], writes=[b_oacc])
                        else:
                            P.op("dve", lambda e, c=c, ci=ci, pbc2=pbc2: e.tensor_tensor(out=otmp[:], in0=osb[ci][0:64, :], in1=pbc2[0:64, :], op=ALU.mult), reads=[b_osb[ci], pbc2b], writes=[b_oacc])
                            if c == 1:
                                P.op("dve", lambda e: e.tensor_tensor(out=oacc[:], in0=oacc[:], in1=otmp[:], op=ALU.add), reads=[b_oacc], writes=[b_oacc])
                            else:
                                oi = qb % 2
                                P.op("dve", lambda e, oi=oi: e.tensor_tensor(out=ost[oi][:], in0=oacc[:], in1=otmp[:], op=ALU.add), reads=[b_oacc], writes=[b_ost[oi]])
                                P.dma("sp", lambda e, g=g, t0=t0, oi=oi: e.dma_start(out=scr["oT"][g * 256:(g + 1) * 256, t0:t0 + 128].rearrange("(p d) t -> d p t", d=64),
                                                                                    in_=ost[oi][:].rearrange("d (p t) -> d p t", p=4)), reads=[b_ost[oi]])
            cmp_stage(0)
            for qb in range(NB):
                if qb + 1 < NB:
                    cmp_stage(qb + 1)
                att_stage(qb)
        P.phase_end()


bf = ml_dtypes.bfloat16
def nsa_consts(S):
    NB = S // 128
    ncmp = (S - 32) // 16 + 1
    nnc = (ncmp + 127) // 128
    inv = (10000.0 ** (-np.arange(0, 64, 2, dtype=np.float32) / 64)).astype(np.float32)
    ang = np.arange(S, dtype=np.float32)[None, :] * inv[:, None]
    j = np.arange(128) % 32
    cosT = np.cos(ang)[j].astype(np.float32); sinT = np.sin(ang)[j].astype(np.float32)
    pm = np.zeros((128, 128), np.float32)
    for d in range(128):
        if (d % 64) < 32: pm[d + 32, d] = -1.0
        else: pm[d - 32, d] = 1.0
    n = np.arange(nnc * 128)[:, None]; t = np.arange(S)[None, :]
    cmask = np.where((16 * n + 31 <= t) & (n < ncmp), 0.0, -240000.0)
    jj = np.arange(128)[None, :]
    cover = ((n < ncmp) & (16 * n < 64 * jj + 64) & (16 * n + 32 > 64 * jj)).astype(np.float32)
    tq = (np.arange(NB)[:, None, None] * 128 + np.arange(128)[None, :, None]); cur = tq // 64
    jb = np.arange(128)[None, None, :]
    forced = (jb == 0) | (jb == cur) | (jb == cur - 1); allowed = jb <= cur
    allow = (allowed & ~forced).astype(np.float32)
    addc = np.where(forced, 1e9, np.where(allowed, 0.0, -1.0)).astype(np.float32)
    emat = (np.arange(128)[:, None] == (np.arange(S)[None, :] // 64)).astype(np.float32)
    s_ = np.arange(128)[:, None]; t_ = np.arange(128)[None, :]
    negtri = np.where(s_ > t_, -240000.0, 0.0); negle = np.where(s_ <= t_, -240000.0, 0.0)
    return {"cos": cosT, "sin": sinT, "pm": pm.astype(bf), "cmask": cmask.astype(bf), "cover": cover, "allow": allow, "addc": addc,
            "emat": emat.astype(bf), "negtri4": np.tile(negtri, (1, 4)).astype(bf), "negle4": np.tile(negle, (1, 4)).astype(bf),
            "identb": np.eye(128, dtype=np.float32).astype(bf)}


def gla_consts():
    s = np.arange(128)[:, None]; c = np.arange(128)[None, :]
    same = (s // 64) == (c // 64)
    m01 = np.ones((128, 2048), np.float32); m01[:, ::64] = 0
    return {"m01": m01.astype(bf), "umat": (same & (s > c)).astype(np.float32), "bcaus": (same & (s <= c)).astype(np.float32).astype(bf),
            "ident": np.eye(128, dtype=np.float32)}


def sb_consts():
    j = np.arange(128)[:, None]; s = np.arange(128)[None, :]
    return {
        "trii": np.where(j >= s, -8.0, 0.0).astype(bf),
        "nones": np.full((128, 128), -8.0, np.float32).astype(bf),
        "strict": np.where(j < s, 1.0, 0.0).astype(bf),
        "negtri_incl": np.where(j >= s, -240000.0, 0.0).astype(bf),
        "identb": np.eye(128, dtype=np.float32).astype(bf),
    }


SEQ = 8192
NCORE = 4


def all_consts():
    cs = {}
    for pre, d in (("n_", nsa_consts(SEQ)), ("g_", gla_consts()), ("s_", sb_consts())):
        for k, v in d.items():
            cs[pre + k] = v
    cs["f_tri"] = (np.tril(np.ones((128, 128), np.float32), -1) * -240000.0).astype(bf)
    cs["f_ident"] = np.eye(128, dtype=np.float32)
    cs["f_identb"] = np.eye(128, dtype=np.float32).astype(bf)
    return cs


def build_program(inputs, consts):
    S = SEQ
    nc = bass.Bass("TRN2", target_bir_lowering=False)
    W = {}
    for k, v in inputs.items():
        if k == "x":
            continue
        W[k] = nc.dram_tensor(k, list(v.shape), F32, kind="ExternalInput").ap()
    xT = nc.dram_tensor("xT", [D, S], F32, kind="ExternalInput").ap()
    yT = nc.dram_tensor("yT", [D, S], F32, kind="ExternalOutput").ap()
    cs = {k: nc.dram_tensor("c_" + k, list(v.shape), BF16 if v.dtype == bf else F32, kind="ExternalInput").ap() for k, v in consts.items()}
    sub = lambda pre: {k[len(pre):]: v for k, v in cs.items() if k.startswith(pre)}
    dt = lambda n, s, d: nc.dram_tensor("scr_" + n, s, d).ap()
    oT = dt("oT", [D, S], BF16)
    sg = dt("sg", [D, S], BF16)
    qT = dt("qT", [D, S], BF16)
    kT = dt("kT", [D, S], BF16)
    V = dt("V", [S, D], BF16)
    KV = dt("KV", [S, 1536], BF16)
    with ExitStack() as st:
        P = Prog(nc, st)
        C = Common(P, None)
        ng = W["norm_g"]
        ffn_phase(P, C, xT, yT, W["ffn1_w_gate"][0], W["ffn1_w_up"][0], W["ffn1_w_down"][0], ng[0, 0], S, "l0a")
        scr = {"qT": qT, "kT": kT, "V": V, "ls": dt("ls", [16, S], F32), "sg": sg, "crow": dt("crow", [3, 16, S], BF16), "oT": oT}
        fox_A(P, C, yT, S, W["fox_w_in"][0], ng[0, 1], W["fox_b_f"][0], W["fox_g_q"][0], W["fox_g_k"][0], scr)
        fox_B(P, C, S, scr, {"tri": cs["f_tri"], "ident": cs["f_ident"], "identb": cs["f_identb"]})
        mix_C(P, C, yT, S, W["fox_w_out"][0], scr, "sg")
        ffn_phase(P, C, yT, yT, W["ffn2_w_gate"][0], W["ffn2_w_up"][0], W["ffn2_w_down"][0], ng[0, 2], S, "l0b")
        ffn_phase(P, C, yT, yT, W["ffn1_w_gate"][1], W["ffn1_w_up"][1], W["ffn1_w_down"][1], ng[1, 0], S, "l1a")
        scr = {"qT": qT, "kT3": dt("kT3", [768, S], BF16), "vcT": dt("vcT", [256, S], BF16), "KV": KV, "gT": dt("gT", [48, S], F32), "oT": oT}
        ncs = sub("n_")
        nsa_A(P, C, yT, S, W["nsa_w_in"][0], ng[1, 1], W["nsa_g_q"][0], W["nsa_g_k"][0], scr, ncs)
        nsa_B(P, C, S, scr, ncs, W["nsa_cmp_pos_k"][0], W["nsa_cmp_pos_v"][0], W["nsa_w_cmp_k"][0], W["nsa_w_cmp_v"][0], W["nsa_g_k"][0])
        mix_C(P, C, yT, S, W["nsa_w_out"][0], scr, None)
        ffn_phase(P, C, yT, yT, W["ffn2_w_gate"][1], W["ffn2_w_up"][1], W["ffn2_w_down"][1], ng[1, 2], S, "l1b")
        ffn_phase(P, C, yT, yT, W["ffn1_w_gate"][2], W["ffn1_w_up"][2], W["ffn1_w_down"][2], ng[2, 0], S, "l2a")
        scr = {"qT": qT[0:512], "kT": kT[0:512], "KV": KV, "laT": dt("laT", [512, S], F32), "laK": dt("laK", [S, 512], F32), "sg": sg, "oT": oT}
        gcs = sub("g_")
        gla_A(P, C, yT, S, W["gla_w_in"][0], ng[2, 1], W["gla_w_gate_up"][0], W["gla_b_gate"][0], scr, gcs)
        gla_B(P, C, S, scr, gcs)
        gla_C(P, C, yT, S, W["gla_w_out"][0], W["gla_g_out"][0], scr)
        ffn_phase(P, C, yT, yT, W["ffn2_w_gate"][2], W["ffn2_w_up"][2], W["ffn2_w_down"][2], ng[2, 2], S, "l2b")
        ffn_phase(P, C, yT, yT, W["ffn1_w_gate"][3], W["ffn1_w_up"][3], W["ffn1_w_down"][3], ng[3, 0], S, "l3a")
        scr = {"qT": qT, "kT": kT, "V": V, "oT": oT}
        qkv_A(P, C, yT, S, W["sb_w_in"][0], ng[3, 1], scr)
        sb_B(P, C, S, scr, sub("s_"))
        mix_C(P, C, yT, S, W["sb_w_out"][0], scr, None)
        ffn_phase(P, C, yT, yT, W["ffn2_w_gate"][3], W["ffn2_w_up"][3], W["ffn2_w_down"][3], ng[3, 2], S, "l3b")
        P.finish()
        P.emit()
    return nc


def kernel(**inputs):
    inputs = {k: np.asarray(v) for k, v in inputs.items()}
    x = inputs["x"]
    consts = all_consts()
    nc = build_program(inputs, consts)
    base = {k: np.ascontiguousarray(v, dtype=np.float32) for k, v in inputs.items() if k != "x"}
    for k, v in consts.items():
        base["c_" + k] = v
    in_maps = []
    for b in range(NCORE):
        m = dict(base)
        m["xT"] = np.ascontiguousarray(x[b].T)
        in_maps.append(m)
    res = run_bass_kernel_spmd(nc, in_maps, core_ids=list(range(NCORE)))
    out = np.stack([np.ascontiguousarray(res.results[b]["yT"].T) for b in range(NCORE)], axis=0)
    return out.astype(np.float32)
```

```python
from contextlib import ExitStack
import ml_dtypes
from concourse.bass_utils import run_bass_kernel_spmd


import numpy as np
import concourse.bass as bass
import concourse.mybir as mybir

F32 = mybir.dt.float32
BF16 = mybir.dt.bfloat16
AF = mybir.ActivationFunctionType
ALU = mybir.AluOpType
AX = mybir.AxisListType

ENGS = ["pe", "act", "dve", "pool", "sp"]
NDQ = 6


class Buf:
    __slots__ = ("name", "w", "r")

    def __init__(self, name):
        self.name = name
        self.w = None
        self.r = {}


class Prog:
    def __init__(self, nc, stack):
        self.nc = nc
        self.stack = stack
        self.ops = {e: [] for e in ENGS}
        self.cnt = {}
        self.seen = {e: {} for e in ENGS}
        self.sems = {}
        self.dma_n = {"sp": 0, "pool": 0, "act": 0}
        for e in ENGS:
            self._mksem(e)
        for q in ("sp", "pool", "act"):
            for i in range(NDQ):
                self._mksem(f"dq_{q}_{i}")

    def _mksem(self, key):
        self.sems[key] = self.stack.enter_context(self.nc.semaphore("s_" + key))
        self.cnt[key] = 0

    def _deps(self, reads, writes):
        deps = {}
        def add(k, v):
            if v > deps.get(k, 0):
                deps[k] = v
        for b in reads:
            if b.w is not None:
                add(*b.w)
        for b in writes:
            if b.w is not None:
                add(*b.w)
            for k, v in b.r.items():
                add(k, v)
        return deps

    def _emit_waits(self, eng, deps, skip_key=None):
        seen = self.seen[eng]
        waits = []
        for k, v in deps.items():
            if k == skip_key:
                continue
            if v > seen.get(k, 0):
                seen[k] = v
                waits.append((self.sems[k], v))
        return waits

    def op(self, eng, fn, reads=(), writes=(), inc=True, strict=False):
        deps = self._deps(reads, writes)
        waits = self._emit_waits(eng, deps, skip_key=eng if (eng == 'pe' and not strict) else None)
        sem = self.sems[eng]
        if inc:
            self.cnt[eng] += 1
            val = self.cnt[eng]
        else:
            val = self.cnt[eng] + 1
        for b in reads:
            if val > b.r.get(eng, 0):
                b.r[eng] = val
        for b in writes:
            b.w = (eng, val)
            b.r = {}
        self.ops[eng].append((waits, fn, sem if inc else None, 1))

    def dma(self, q, fn, reads=(), writes=()):
        j = self.dma_n[q]
        self.dma_n[q] += 1
        key = f"dq_{q}_{j % NDQ}"
        deps = self._deps(reads, writes)
        if self.cnt[key] > 0:
            deps[key] = max(deps.get(key, 0), self.cnt[key])
        waits = self._emit_waits(q, deps)
        self.cnt[key] += 16
        val = self.cnt[key]
        for b in reads:
            if val > b.r.get(key, 0):
                b.r[key] = val
        for b in writes:
            b.w = (key, val)
            b.r = {}
        self.ops[q].append((waits, fn, self.sems[key], 16))

    def finish(self):
        deps = {k: v for k, v in self.cnt.items() if v > 0 and k != "sp"}
        waits = self._emit_waits("sp", deps)
        self.ops["sp"].append((waits, None, None, 0))

    def emit(self):
        nc = self.nc
        ops_all = self.ops
        self.ops = {e: [] for e in ENGS}
        self.n_phase = getattr(self, "n_phase", 0) + 1
        import contextlib
        scope = nc.named_scope("ph%02d" % self.n_phase) if getattr(self, "scopes", False) else contextlib.nullcontext()
        with scope, nc.Block() as block:
            def run(engname):
                def body(eng):
                    for waits, fn, sem, amt in ops_all[engname]:
                        for s, v in waits:
                            eng.wait_ge(s, v)
                        if fn is not None:
                            inst = fn(eng)
                            if sem is not None:
                                inst.then_inc(sem, amt)
                return body
            block.tensor(run("pe"))
            block.scalar(run("act"))
            block.vector(run("dve"))
            block.gpsimd(run("pool"))
            block.sync(run("sp"))


def _barrier(self):
    snap = {k: v for k, v in self.cnt.items() if v > 0}
    for e in ENGS:
        deps = {k: v for k, v in snap.items() if k != e}
        waits = self._emit_waits(e, deps)
        if waits:
            self.ops[e].append((waits, None, None, 0))
Prog.barrier = _barrier


def _phase_end(self):
    self.barrier()
    self.emit()
Prog.phase_end = _phase_end


D = 1024
F = 2816
NFC = F // 128
NKC = D // 128
FG = 256
NFG = F // FG


class Common:
    def __init__(self, P, consts_dram):
        nc, st = P.nc, P.stack
        self.P = P
        self.psum = []
        self.psb = []
        for i in range(8):
            t = st.enter_context(nc.psum_tensor(f"ps{i}", [128, 512], F32))
            self.psum.append(t)
            self.psb.append(Buf(f"ps{i}"))
        self.ones_f = st.enter_context(nc.sbuf_tensor("ones_f", [128, 128], F32))
        self.b_ones_f = Buf("ones_f")
        P.op("pool", lambda e: e.memset(self.ones_f[:], 1.0), writes=[self.b_ones_f])
        self.warm_rhs = st.enter_context(nc.sbuf_tensor("warm_rhs", [128, 512], F32))
        P.op("pool", lambda e: e.memset(self.warm_rhs[:], 0.0), writes=[self.b_ones_f])


def ffn_phase(P, C, xT_in, xT_out, wg, wu, wd, gvec, Tc, tag, wq="pool"):
    from contextlib import ExitStack
    nc = P.nc
    st = ExitStack()
    TT = 512
    NT = Tc // TT
    sb = lambda n, s, d: st.enter_context(nc.sbuf_tensor(f"{tag}_{n}", s, d))
    x_t = [sb(f"x{i}", [128, NKC, TT], F32) for i in range(2)]
    b_x = [Buf(f"x{i}") for i in range(2)]
    sq_t = sb("sq", [128, NKC, TT], F32); b_sq = Buf("sq")
    rstd = sb("rstd", [128, TT], F32); b_rstd = Buf("rstd")
    h_t = sb("h", [128, NKC, TT], BF16); b_h = Buf("h")
    a_t = sb("a", [128, NFC, TT], BF16); b_a = [Buf(f"a{i}") for i in range(NFC)]
    sil = [sb(f"sil{i}", [128, TT], F32) for i in range(2)]; b_sil = [Buf("sil0"), Buf("sil1")]
    wg_t = [sb(f"wg{i}", [128, NKC, FG], BF16) for i in range(2)]
    wu_t = [sb(f"wu{i}", [128, NKC, FG], BF16) for i in range(2)]
    b_wg = [Buf("wg0"), Buf("wg1")]; b_wu = [Buf("wu0"), Buf("wu1")]
    wd_t = [sb(f"wd{i}", [128, NFC, 128], BF16) for i in range(2)]
    b_wd = [Buf("wd0"), Buf("wd1")]
    g_t = sb("g", [128, NKC], F32); b_g = Buf("g")

    P.dma("sp", lambda e: e.dma_start(out=g_t[:], in_=gvec.rearrange("(c p) -> p c", p=128), allow_slow_non_contiguous=True),
          writes=[b_g])
    xin = xT_in.rearrange("(c p) t -> p c t", p=128)
    xout = xT_out.rearrange("(c p) t -> p c t", p=128)
    wgr = wg.rearrange("(c p) f -> p c f", p=128)
    wur = wu.rearrange("(c p) f -> p c f", p=128)
    wdr = wd.rearrange("(c p) d -> p c d", p=128)

    wcount = [0]
    for it in range(NT):
        xs = it % 2
        t0 = it * TT
        P.dma("sp", lambda e, xs=xs, t0=t0: e.dma_start(out=x_t[xs][:], in_=xin[:, :, t0:t0 + TT]),
              writes=[b_x[xs]])
        P.op("act", lambda e, xs=xs: e.activation(out=sq_t[:], in_=x_t[xs][:], func=AF.Square),
             reads=[b_x[xs]], writes=[b_sq])
        pst = 6
        for kc in range(NKC):
            P.op("pe", lambda e, kc=kc: e.matmul(C.psum[pst][:], lhsT=C.ones_f[:], rhs=sq_t[:, kc, :],
                                                 start=(kc == 0), stop=(kc == NKC - 1)),
                 reads=[b_sq, C.b_ones_f], writes=[C.psb[pst]], inc=(kc == NKC - 1))
        P.op("dve", lambda e: e.tensor_scalar(out=rstd[:], in0=C.psum[pst][:], scalar1=1.0 / D, scalar2=1e-6,
                                              op0=ALU.mult, op1=ALU.add),
             reads=[C.psb[pst]], writes=[b_rstd])
        P.op("act", lambda e: e.activation(out=rstd[:], in_=rstd[:], func=AF.Sqrt),
             reads=[b_rstd], writes=[b_rstd])
        P.op("dve", lambda e: e.reciprocal(out=rstd[:], in_=rstd[:]),
             reads=[b_rstd], writes=[b_rstd])
        for kc in range(NKC):
            P.op("dve", lambda e, kc=kc, xs=xs: e.scalar_tensor_tensor(
                out=h_t[:, kc, :], in0=x_t[xs][:, kc, :], scalar=g_t[:, kc:kc + 1], in1=rstd[:],
                op0=ALU.mult, op1=ALU.mult),
                reads=[b_x[xs], b_g, b_rstd], writes=[b_h])
        for fg in range(NFG):
            ws = wcount[0] % 2
            wcount[0] += 1
            f0 = fg * FG
            P.dma(wq, lambda e, ws=ws, f0=f0: e.dma_start(out=wg_t[ws][:], in_=wgr[:, :, f0:f0 + FG]),
                  writes=[b_wg[ws]])
            P.dma(wq, lambda e, ws=ws, f0=f0: e.dma_start(out=wu_t[ws][:], in_=wur[:, :, f0:f0 + FG]),
                  writes=[b_wu[ws]])
            for j in range(FG // 128):
                fc = fg * (FG // 128) + j
                pg = fc % 2
                pu = 2 + fc % 2
                for kc in range(NKC):
                    P.op("pe", lambda e, kc=kc, ws=ws, j=j, pg=pg: e.matmul(
                        C.psum[pg][:], lhsT=wg_t[ws][:, kc, j * 128:(j + 1) * 128], rhs=h_t[:, kc, :],
                        start=(kc == 0), stop=(kc == NKC - 1)),
                        reads=[b_wg[ws], b_h], writes=[C.psb[pg]], inc=(kc == NKC - 1))
                for kc in range(NKC):
                    P.op("pe", lambda e, kc=kc, ws=ws, j=j, pu=pu: e.matmul(
                        C.psum[pu][:], lhsT=wu_t[ws][:, kc, j * 128:(j + 1) * 128], rhs=h_t[:, kc, :],
                        start=(kc == 0), stop=(kc == NKC - 1)),
                        reads=[b_wu[ws], b_h], writes=[C.psb[pu]], inc=(kc == NKC - 1))
                ss = fc % 2
                P.op("act", lambda e, ss=ss, pg=pg: e.activation(out=sil[ss][:], in_=C.psum[pg][:], func=AF.Silu),
                     reads=[C.psb[pg]], writes=[b_sil[ss]])
                P.op("dve", lambda e, ss=ss, pu=pu, fc=fc: e.tensor_tensor(
                    out=a_t[:, fc, :], in0=sil[ss][:], in1=C.psum[pu][:], op=ALU.mult),
                    reads=[b_sil[ss], C.psb[pu]], writes=[b_a[fc]])
        for dc in range(NKC):
            ws = dc % 2
            P.dma(wq, lambda e, ws=ws, dc=dc: e.dma_start(out=wd_t[ws][:], in_=wdr[:, :, dc * 128:(dc + 1) * 128]),
                  writes=[b_wd[ws]])
            py = 4 + dc % 2
            for fc in range(NFC):
                P.op("pe", lambda e, fc=fc, ws=ws, py=py: e.matmul(
                    C.psum[py][:], lhsT=wd_t[ws][:, fc, :], rhs=a_t[:, fc, :],
                    start=(fc == 0), stop=(fc == NFC - 1)),
                    reads=[b_wd[ws], b_a[fc]], writes=[C.psb[py]], inc=(fc == NFC - 1))
            P.op("dve", lambda e, dc=dc, xs=xs, py=py: e.scalar_tensor_tensor(
                out=x_t[xs][:, dc, :], in0=C.psum[py][:], scalar=0.5, in1=x_t[xs][:, dc, :],
                op0=ALU.mult, op1=ALU.add),
                reads=[C.psb[py], b_x[xs]], writes=[b_x[xs]])
        P.dma("sp", lambda e, xs=xs, t0=t0: e.dma_start(out=xout[:, :, t0:t0 + TT], in_=x_t[xs][:]),
              reads=[b_x[xs]])
    P.phase_end()
    st.close()


from contextlib import ExitStack

H = 16
DH = 64
TT = 512
PDEPTH = 2


def ps_next(C):
    i = C.ps_i = (getattr(C, "ps_i", -1) + 1) % getattr(C, "nrot", 8)
    return C.psum[i], C.psb[i]


def pe_warm(P, C, n=1):
    if getattr(C, "nrot", 8) != 7:
        return
    for _ in range(n):
        P.op("pe", lambda e: e.matmul(C.psum[7][:], lhsT=C.ones_f[:], rhs=C.warm_rhs[:], start=True, stop=True), inc=False)


def load_x_norm(P, C, sb, xin, t0, g_t, b_g, tag):
    x_t = sb("x", [128, NKC, TT], F32); b_x = Buf("x")
    sq_t = sb("sq", [128, NKC, TT], F32); b_sq = Buf("sq")
    rstd = sb("rstd", [128, TT], F32); b_rstd = Buf("rstd")
    h_t = sb("h", [128, NKC, TT], BF16); b_h = Buf("h")
    return x_t, b_x, sq_t, b_sq, rstd, b_rstd, h_t, b_h


def emit_norm(P, C, x_t, b_x, sq_t, b_sq, rstd, b_rstd, h_t, b_h, g_t, b_g):
    P.op("act", lambda e: e.activation(out=sq_t[:], in_=x_t[:], func=AF.Square), reads=[b_x], writes=[b_sq])
    ps, pb = ps_next(C)
    for kc in range(NKC):
        P.op("pe", lambda e, kc=kc: e.matmul(ps[:], lhsT=C.ones_f[:], rhs=sq_t[:, kc, :], start=(kc == 0), stop=(kc == NKC - 1)),
             reads=[b_sq, C.b_ones_f], writes=[pb], inc=(kc == NKC - 1))
    P.op("dve", lambda e: e.tensor_scalar(out=rstd[:], in0=ps[:], scalar1=1.0 / D, scalar2=1e-6, op0=ALU.mult, op1=ALU.add),
         reads=[pb], writes=[b_rstd])
    P.op("act", lambda e: e.activation(out=rstd[:], in_=rstd[:], func=AF.Sqrt), reads=[b_rstd], writes=[b_rstd])
    P.op("dve", lambda e: e.reciprocal(out=rstd[:], in_=rstd[:]), reads=[b_rstd], writes=[b_rstd])
    for kc in range(NKC):
        P.op("dve", lambda e, kc=kc: e.scalar_tensor_tensor(out=h_t[:, kc, :], in0=x_t[:, kc, :], scalar=g_t[:, kc:kc + 1],
                                                            in1=rstd[:], op0=ALU.mult, op1=ALU.mult),
             reads=[b_x, b_g, b_rstd], writes=[b_h])


def lin_fm(P, C, sb, wr, col0, ncols, src_t, b_src, consume, tag, G=256, nkc=NKC):
    wt = [sb(f"{tag}w{i}", [128, nkc, G], BF16) for i in range(2)]
    bw = [Buf(f"{tag}w0"), Buf(f"{tag}w1")]
    ng = (ncols + G - 1) // G
    def run():
        for g in range(ng):
            ws = g % 2
            c0 = col0 + g * G
            gw = min(G, col0 + ncols - c0)
            P.dma("pool", lambda e, ws=ws, c0=c0, gw=gw: e.dma_start(out=wt[ws][:, :, 0:gw], in_=wr[:, :, c0:c0 + gw]),
                  writes=[bw[ws]])
            for j in range((gw + 127) // 128):
                cw = min(128, gw - j * 128)
                ps, pb = ps_next(C)
                for kc in range(nkc):
                    P.op("pe", lambda e, kc=kc, ws=ws, j=j, cw=cw, ps=ps: e.matmul(
                        ps[0:cw, :], lhsT=wt[ws][:, kc, j * 128:j * 128 + cw], rhs=src_t[:, kc, :],
                        start=(kc == 0), stop=(kc == nkc - 1)),
                        reads=[bw[ws], b_src], writes=[pb], inc=(kc == nkc - 1))
                consume(g * (G // 128) + j, ps, pb)
    return run


def fox_A(P, C, xT, S, w_in, gnorm, b_f, g_q, g_k, scr):
    nc = P.nc
    with ExitStack() as st:
        sb = lambda n, s, d: st.enter_context(nc.sbuf_tensor(f"fa_{n}", s, d))
        g_t = sb("g", [128, NKC], F32); b_g = Buf("g")
        P.dma("sp", lambda e: e.dma_start(out=g_t[:], in_=gnorm.rearrange("(c p) -> p c", p=128), allow_slow_non_contiguous=True), writes=[b_g])
        gq = sb("gq", [128, 1], F32); gk = sb("gk", [128, 1], F32); b_gqk = Buf("gqk")
        for half in range(2):
            P.dma("sp", lambda e, half=half: e.dma_start(out=gq[half * 64:(half + 1) * 64, :], in_=g_q.rearrange("(p o) -> p o", o=1), allow_slow_non_contiguous=True), writes=[b_gqk])
            P.dma("sp", lambda e, half=half: e.dma_start(out=gk[half * 64:(half + 1) * 64, :], in_=g_k.rearrange("(p o) -> p o", o=1), allow_slow_non_contiguous=True), writes=[b_gqk])
        nbf = sb("nbf", [16, 1], F32); b_nbf = Buf("nbf")
        P.dma("sp", lambda e: e.dma_start(out=nbf[:], in_=b_f.rearrange("(p o) -> p o", o=1), allow_slow_non_contiguous=True), writes=[b_nbf])
        P.op("dve", lambda e: e.tensor_scalar(out=nbf[:], in0=nbf[:], scalar1=-1.0, scalar2=None, op0=ALU.mult), reads=[b_nbf], writes=[b_nbf])
        bones = sb("bones", [128, 128], F32); b_bones = Buf("bones")
        P.op("pool", lambda e: e.memset(bones[:], 0.0), writes=[b_bones])
        P.op("pool", lambda e: e.memset(bones[0:64, 0:64], 1.0), writes=[b_bones])
        P.op("pool", lambda e: e.memset(bones[64:128, 64:128], 1.0), writes=[b_bones])
        x_t, b_x, sq_t, b_sq, rstd, b_rstd, h_t, b_h = load_x_norm(P, C, sb, None, 0, g_t, b_g, "fa")
        sqq = [sb(f"sqq{i}", [128, TT], F32) for i in range(2)]; b_sqq = [Buf("sqq0"), Buf("sqq1")]
        rr = [sb(f"rr{i}", [128, TT], F32) for i in range(2)]; b_rr = [Buf("rr0"), Buf("rr1")]
        stg = [sb(f"stg{i}", [128, TT], BF16) for i in range(3)]; b_stg = [Buf(f"stg{i}") for i in range(3)]
        vst = [sb(f"vst{i}", [128, 4, 512], BF16) for i in range(2)]; b_vst = [Buf("vst0"), Buf("vst1")]
        wv_t = [sb(f"wv{i}", [128, NKC, 512], BF16) for i in range(2)]; b_wv = [Buf("wv0"), Buf("wv1")]
        fe = sb("fe", [16, TT], F32); b_fe = Buf("fe")
        xin = xT.rearrange("(c p) t -> p c t", p=128)
        wr = w_in.rearrange("(c p) f -> p c f", p=128)
        cnt = {"s": 0, "q": 0}

        def mk_qk(dst, gcol, t0):
            def consume(ci, ps, pb):
                i = cnt["q"] % 2; cnt["q"] += 1
                s3 = cnt["s"] % 3; cnt["s"] += 1
                P.op("act", lambda e: e.activation(out=sqq[i][:], in_=ps[:], func=AF.Square), reads=[pb], writes=[b_sqq[i]])
                ps2, pb2 = ps_next(C)
                P.op("pe", lambda e: e.matmul(ps2[:], lhsT=bones[:], rhs=sqq[i][:], start=True, stop=True),
                     reads=[b_bones, b_sqq[i]], writes=[pb2])
                P.op("dve", lambda e: e.tensor_scalar(out=rr[i][:], in0=ps2[:], scalar1=1.0 / DH, scalar2=1e-6, op0=ALU.mult, op1=ALU.add),
                     reads=[pb2], writes=[b_rr[i]])
                P.op("act", lambda e: e.activation(out=rr[i][:], in_=rr[i][:], func=AF.Sqrt), reads=[b_rr[i]], writes=[b_rr[i]])
                P.op("dve", lambda e: e.reciprocal(out=rr[i][:], in_=rr[i][:]), reads=[b_rr[i]], writes=[b_rr[i]])
                P.op("dve", lambda e: e.scalar_tensor_tensor(out=stg[s3][:], in0=ps[:], scalar=gcol[:, 0:1], in1=rr[i][:],
                                                             op0=ALU.mult, op1=ALU.mult),
                     reads=[pb, b_gqk, b_rr[i]], writes=[b_stg[s3]])
                P.dma("sp", lambda e: e.dma_start(out=dst[ci * 128:(ci + 1) * 128, t0:t0 + TT], in_=stg[s3][:]), reads=[b_stg[s3]])
            return consume

        def mk_og(t0):
            def consume(ci, ps, pb):
                s3 = cnt["s"] % 3; cnt["s"] += 1
                P.op("act", lambda e: e.activation(out=stg[s3][:], in_=ps[:], func=AF.Sigmoid), reads=[pb], writes=[b_stg[s3]])
                P.dma("sp", lambda e: e.dma_start(out=scr["sg"][ci * 128:(ci + 1) * 128, t0:t0 + TT], in_=stg[s3][:]), reads=[b_stg[s3]])
            return consume

        def mk_f(t0):
            def consume(ci, ps, pb):
                P.op("act", lambda e: e.activation(out=fe[:], in_=ps[0:16, :], func=AF.Exp, bias=nbf[:, 0:1], scale=-1.0),
                     reads=[pb, b_nbf], writes=[b_fe])
                P.op("act", lambda e: e.activation(out=fe[:], in_=fe[:], func=AF.Ln, bias=1.0, scale=1.0), reads=[b_fe], writes=[b_fe])
                P.op("dve", lambda e: e.tensor_scalar(out=fe[:], in0=fe[:], scalar1=-1.0, scalar2=None, op0=ALU.mult), reads=[b_fe], writes=[b_fe])
                P.dma("sp", lambda e: e.dma_start(out=scr["ls"][:, t0:t0 + TT], in_=fe[:]), reads=[b_fe])
            return consume

        for it in range(S // TT):
            t0 = it * TT
            P.dma("sp", lambda e, t0=t0: e.dma_start(out=x_t[:], in_=xin[:, :, t0:t0 + TT]), writes=[b_x])
            emit_norm(P, C, x_t, b_x, sq_t, b_sq, rstd, b_rstd, h_t, b_h, g_t, b_g)
            for name, col0, ncols, mk in (("q", 0, 1024, lambda: mk_qk(scr["qT"], gq, t0)),
                                          ("k", 1024, 1024, lambda: mk_qk(scr["kT"], gk, t0)),
                                          ("og", 3088, 1024, lambda: mk_og(t0)),
                                          ("f", 3072, 16, lambda: mk_f(t0))):
                key = "wbuf_" + name
                if key not in cnt:
                    cnt[key] = ([sb(f"{name}w{i}", [128, NKC, 256], BF16) for i in range(2)], [Buf(name + "w0"), Buf(name + "w1")])
                wt, bw = cnt[key]
                consume = mk()
                G = 256
                ng = (ncols + G - 1) // G
                for g in range(ng):
                    ws = g % 2
                    c0 = col0 + g * G
                    gw = min(G, col0 + ncols - c0)
                    P.dma("pool", lambda e, ws=ws, c0=c0, gw=gw, wt=wt: e.dma_start(out=wt[ws][:, :, 0:gw], in_=wr[:, :, c0:c0 + gw]),
                          writes=[bw[ws]])
                    for j in range((gw + 127) // 128):
                        cw = min(128, gw - j * 128)
                        ps, pb = ps_next(C)
                        for kc in range(NKC):
                            P.op("pe", lambda e, kc=kc, ws=ws, j=j, cw=cw, ps=ps, wt=wt: e.matmul(
                                ps[0:cw, :], lhsT=wt[ws][:, kc, j * 128:j * 128 + cw], rhs=h_t[:, kc, :],
                                start=(kc == 0), stop=(kc == NKC - 1)),
                                reads=[bw[ws], b_h], writes=[pb], inc=(kc == NKC - 1))
                        consume(g * (G // 128) + j, ps, pb)
            for cg in range(2):
                P.dma("pool", lambda e, cg=cg: e.dma_start(out=wv_t[cg][:], in_=wr[:, :, 2048 + cg * 512:2048 + (cg + 1) * 512]),
                      writes=[b_wv[cg]])
                vs = cg
                for tb in range(4):
                    ps, pb = ps_next(C)
                    for kc in range(NKC):
                        P.op("pe", lambda e, kc=kc, tb=tb, cg=cg, ps=ps: e.matmul(
                            ps[:], lhsT=h_t[:, kc, tb * 128:(tb + 1) * 128], rhs=wv_t[cg][:, kc, :],
                            start=(kc == 0), stop=(kc == NKC - 1)),
                            reads=[b_wv[cg], b_h], writes=[pb], inc=(kc == NKC - 1))
                    P.op("act", lambda e, tb=tb, vs=vs, ps=ps: e.activation(out=vst[vs][:, tb, :], in_=ps[:], func=AF.Copy),
                         reads=[pb], writes=[b_vst[vs]])
                P.dma("sp", lambda e, vs=vs, cg=cg, t0=t0: e.dma_start(
                    out=scr["V"][t0:t0 + TT, cg * 512:(cg + 1) * 512].rearrange("(tb p) c -> p tb c", p=128), in_=vst[vs][:]),
                    reads=[b_vst[vs]])
        P.phase_end()


def fox_B(P, C, S, scr, consts):
    nc = P.nc
    NB = S // 128
    NQC = S // 512
    with ExitStack() as st:
        sb = lambda n, s, d: st.enter_context(nc.sbuf_tensor(f"fb_{n}", s, d))
        tri = sb("tri", [128, 128], BF16); b_tri = Buf("tri")
        P.dma("sp", lambda e: e.dma_start(out=tri[:], in_=consts["tri"]), writes=[b_tri])
        identb = sb("identb", [128, 128], BF16)
        P.dma("sp", lambda e: e.dma_start(out=identb[:], in_=consts["identb"]), writes=[b_tri])
        ident = sb("ident", [128, 128], F32); b_ident = Buf("ident")
        P.dma("sp", lambda e: e.dma_start(out=ident[:], in_=consts["ident"]), writes=[b_ident])
        c_t = sb("c", [16, S], F32); b_c = Buf("c")
        zc = sb("zc", [16, 1], F32); b_zc = Buf("zc")
        P.op("pool", lambda e: e.memset(zc[:], 0.0), writes=[b_zc])
        P.dma("sp", lambda e: e.dma_start(out=c_t[:], in_=scr["ls"]), writes=[b_c])
        P.op("dve", lambda e: e.tensor_tensor_scan(out=c_t[:], data0=c_t[:], data1=zc[:, 0:1].to_broadcast([16, S]), initial=0.0,
                                                   op0=ALU.add, op1=ALU.add), reads=[b_c, b_zc], writes=[b_c])
        negc = sb("negc", [128, NB, 16], F32); b_negc = Buf("negc")
        GB = 32
        for g0 in range(0, NB, GB):
            ps, pb = ps_next(C)
            nb_ = min(GB, NB - g0)
            for j in range(nb_):
                kb = g0 + j
                P.op("pe", lambda e, kb=kb, j=j, ps=ps: e.transpose(ps[:, j * 16:(j + 1) * 16], c_t[:, kb * 128:(kb + 1) * 128], ident[0:16, 0:16]),
                     reads=[b_c, b_ident], writes=[pb], inc=(j == nb_ - 1))
            P.op("dve", lambda e, g0=g0, nb_=nb_, ps=ps: e.tensor_scalar(
                out=negc[:, g0:g0 + nb_, :].rearrange("p a b -> p (a b)"), in0=ps[:, 0:nb_ * 16], scalar1=-1.0, scalar2=None, op0=ALU.mult),
                reads=[pb], writes=[b_negc])
        b_crow = Buf("crow")
        r_t = sb("r", [16, S], F32); b_r = Buf("r")
        hi = sb("hi", [16, S], BF16); b_hi = Buf("hi")
        P.op("dve", lambda e: e.tensor_scalar(out=r_t[:], in0=c_t[:], scalar1=8.0, scalar2=None, op0=ALU.mult), reads=[b_c], writes=[b_r])
        for j in range(3):
            P.op("dve", lambda e: e.tensor_copy(out=hi[:], in_=r_t[:]), reads=[b_r], writes=[b_hi])
            P.dma("sp", lambda e, j=j: e.dma_start(out=scr["crow"][j], in_=hi[:]), reads=[b_hi], writes=[b_crow])
            if j < 2:
                P.op("dve", lambda e: e.tensor_tensor(out=r_t[:], in0=r_t[:], in1=hi[:], op=ALU.subtract), reads=[b_r, b_hi], writes=[b_r])
        qa = [sb(f"qa{i}", [67, S], BF16) for i in range(2)]; b_qa = [Buf("qa0"), Buf("qa1")]
        ka = [sb(f"ka{i}", [67, S], BF16) for i in range(2)]; b_ka = [Buf("ka0"), Buf("ka1")]
        va = [sb(f"va{i}", [128, NB, 65], BF16) for i in range(2)]; b_va = [Buf("va0"), Buf("va1")]
        for i in range(2):
            P.op("pool", lambda e, i=i: e.memset(ka[i][64:67, :], 1.0), writes=[b_ka[i]])
            P.op("pool", lambda e, i=i: e.memset(va[i][:, :, 64:65], 1.0), writes=[b_va[i]])
        pt = [sb(f"pt{i}", [128, 512], BF16) for i in range(4)]; b_pt = [Buf(f"pt{i}") for i in range(4)]
        osb = sb("osb", [65, 512], F32); b_osb = Buf("osb")
        rec = sb("rec", [65, 512], F32); b_rec = Buf("rec")
        ost = [sb(f"ost{i}", [64, 512], BF16) for i in range(2)]; b_ost = [Buf("ost0"), Buf("ost1")]
        npt = 0
        for h in range(H):
            hs = h % 2
            P.dma("sp", lambda e, h=h, hs=hs: e.dma_start(out=qa[hs][0:64, :], in_=scr["qT"][h * 64:(h + 1) * 64, :]), writes=[b_qa[hs]])
            P.dma("sp", lambda e, h=h, hs=hs: e.dma_start(out=qa[hs][64:67, :], in_=scr["crow"][:, h, :]), reads=[b_crow], writes=[b_qa[hs]])
            P.dma("sp", lambda e, h=h, hs=hs: e.dma_start(out=ka[hs][0:64, :], in_=scr["kT"][h * 64:(h + 1) * 64, :]), writes=[b_ka[hs]])
            P.dma("sp", lambda e, h=h, hs=hs: e.dma_start(out=va[hs][:, :, 0:64], in_=scr["V"][:, h * 64:(h + 1) * 64].rearrange("(kb p) d -> p kb d", p=128)),
                  writes=[b_va[hs]])
            for qc in range(NQC):
                po, pob = ps_next(C)
                nkb = 4 * qc + 4
                pq = []
                for kb in range(nkb):
                    off = max(0, kb - 4 * qc) * 128
                    ps, pb = ps_next(C)
                    if ps is po:
                        ps, pb = ps_next(C)
                    diag = kb >= 4 * qc
                    P.op("pe", lambda e, kb=kb, off=off, ps=ps, hs=hs, qc=qc, diag=diag: e.matmul(
                        ps[:, off:512], lhsT=ka[hs][0:67, kb * 128:(kb + 1) * 128], rhs=qa[hs][0:67, qc * 512 + off:(qc + 1) * 512],
                        start=True, stop=not diag), reads=[b_ka[hs], b_qa[hs]], writes=[pb], inc=not diag)
                    if diag:
                        P.op("pe", lambda e, off=off, ps=ps: e.matmul(
                            ps[:, off:off + 128], lhsT=identb[:], rhs=tri[:], start=False, stop=True),
                            reads=[b_tri], writes=[pb])
                    pi = npt % 4; npt += 1
                    P.op("act", lambda e, kb=kb, off=off, ps=ps, pi=pi, h=h: e.activation(
                        out=pt[pi][:, off:512], in_=ps[:, off:512], func=AF.Exp, bias=negc[:, kb, h:h + 1], scale=0.125),
                        reads=[pb, b_negc], writes=[b_pt[pi]])
                    def pv(kb=kb, off=off, pi=pi, hs=hs, po=po, nkb=nkb, pob=pob):
                        P.op("pe", lambda e: e.matmul(
                            po[0:65, off:512], lhsT=va[hs][:, kb, :], rhs=pt[pi][:, off:512], start=(kb == 0), stop=(kb == nkb - 1)),
                            reads=[b_va[hs], b_pt[pi]], writes=[pob])
                    pe_warm(P, C, getattr(C, "nwarm", 1))
                    pq.append(pv)
                    if len(pq) > PDEPTH:
                        pq.pop(0)()
                while pq:
                    pq.pop(0)()
                P.op("act", lambda e, po=po: e.activation(out=osb[:], in_=po[0:65, :], func=AF.Copy), reads=[pob], writes=[b_osb])
                P.op("dve", lambda e: e.reciprocal(out=rec[64:65, :], in_=osb[64:65, :]), reads=[b_osb], writes=[b_rec])
                pbc, pbcb = ps_next(C)
                P.op("pe", lambda e, pbc=pbc: e.matmul(pbc[0:64, :], lhsT=C.ones_f[64:65, 0:64], rhs=rec[64:65, :], start=True, stop=True),
                     reads=[b_rec, C.b_ones_f], writes=[pbcb])
                oi = qc % 2
                P.op("dve", lambda e, pbc=pbc, oi=oi: e.tensor_tensor(out=ost[oi][:], in0=osb[0:64, :], in1=pbc[0:64, :], op=ALU.mult),
                     reads=[b_osb, pbcb], writes=[b_ost[oi]])
                P.dma("sp", lambda e, h=h, qc=qc, oi=oi: e.dma_start(out=scr["oT"][h * 64:(h + 1) * 64, qc * 512:(qc + 1) * 512], in_=ost[oi][:]),
                      reads=[b_ost[oi]])
        P.phase_end()


def mix_C(P, C, xT, S, w_out, scr, gate_key):
    nc = P.nc
    with ExitStack() as st:
        mix_C.n = getattr(mix_C, "n", 0) + 1
        tagc = mix_C.n
        sb = lambda n, s, d: st.enter_context(nc.sbuf_tensor(f"mc{tagc}_{n}", s, d))
        x_t = [sb(f"x{i}", [128, NKC, TT], F32) for i in range(2)]; b_x = [Buf("x0"), Buf("x1")]
        o_t = [sb(f"o{i}", [128, NKC, TT], BF16) for i in range(2)]; b_o = [Buf("o0"), Buf("o1")]
        g_t = [sb(f"gt{i}", [128, NKC, TT], BF16) for i in range(2)]; b_gt = [Buf("gt0"), Buf("gt1")]
        wo = [sb(f"wo{i}", [128, NKC, 256], BF16) for i in range(2)]; b_wo = [Buf("wo0"), Buf("wo1")]
        xin = xT.rearrange("(c p) t -> p c t", p=128)
        oin = scr["oT"].rearrange("(c p) t -> p c t", p=128)
        wr = w_out.rearrange("(c p) f -> p c f", p=128)
        if gate_key:
            gin = scr[gate_key].rearrange("(c p) t -> p c t", p=128)
        for it in range(S // TT):
            t0 = it * TT
            xs = it % 2
            P.dma("sp", lambda e, t0=t0, xs=xs: e.dma_start(out=x_t[xs][:], in_=xin[:, :, t0:t0 + TT]), writes=[b_x[xs]])
            P.dma("sp", lambda e, t0=t0, xs=xs: e.dma_start(out=o_t[xs][:], in_=oin[:, :, t0:t0 + TT]), writes=[b_o[xs]])
            if gate_key:
                P.dma("sp", lambda e, t0=t0, xs=xs: e.dma_start(out=g_t[xs][:], in_=gin[:, :, t0:t0 + TT]), writes=[b_gt[xs]])
                P.op("pool", lambda e, xs=xs: e.tensor_tensor(out=o_t[xs][:], in0=o_t[xs][:], in1=g_t[xs][:], op=ALU.mult),
                     reads=[b_o[xs], b_gt[xs]], writes=[b_o[xs]])
            for g in range(4):
                ws = g % 2
                P.dma("pool", lambda e, ws=ws, g=g: e.dma_start(out=wo[ws][:], in_=wr[:, :, g * 256:(g + 1) * 256]), writes=[b_wo[ws]])
                for j in range(2):
                    dc = g * 2 + j
                    ps, pb = ps_next(C)
                    for kc in range(NKC):
                        P.op("pe", lambda e, kc=kc, ws=ws, j=j, ps=ps, xs=xs: e.matmul(
                            ps[:], lhsT=wo[ws][:, kc, j * 128:(j + 1) * 128], rhs=o_t[xs][:, kc, :], start=(kc == 0), stop=(kc == NKC - 1)),
                            reads=[b_wo[ws], b_o[xs]], writes=[pb], inc=(kc == NKC - 1))
                    P.op("dve", lambda e, dc=dc, ps=ps, xs=xs: e.tensor_tensor(out=x_t[xs][:, dc, :], in0=x_t[xs][:, dc, :], in1=ps[:], op=ALU.add),
                         reads=[pb, b_x[xs]], writes=[b_x[xs]])
            P.dma("sp", lambda e, t0=t0, xs=xs: e.dma_start(out=xin[:, :, t0:t0 + TT], in_=x_t[xs][:]), reads=[b_x[xs]])
        P.phase_end()


def qkv_A(P, C, xT, S, w_in, gnorm, scr):
    nc = P.nc
    with ExitStack() as st:
        sb = lambda n, s, d: st.enter_context(nc.sbuf_tensor(f"sa_{n}", s, d))
        g_t = sb("g", [128, NKC], F32); b_g = Buf("g")
        P.dma("sp", lambda e: e.dma_start(out=g_t[:], in_=gnorm.rearrange("(c p) -> p c", p=128), allow_slow_non_contiguous=True), writes=[b_g])
        x_t, b_x, sq_t, b_sq, rstd, b_rstd, h_t, b_h = load_x_norm(P, C, sb, None, 0, g_t, b_g, "sa")
        stg = [sb(f"stg{i}", [128, TT], BF16) for i in range(3)]; b_stg = [Buf(f"stg{i}") for i in range(3)]
        vst = [sb(f"vst{i}", [128, 4, 512], BF16) for i in range(2)]; b_vst = [Buf("vst0"), Buf("vst1")]
        wv_t = [sb(f"wv{i}", [128, NKC, 512], BF16) for i in range(2)]; b_wv = [Buf("wv0"), Buf("wv1")]
        wt = [sb(f"w{i}", [128, NKC, 256], BF16) for i in range(2)]; bw = [Buf("w0"), Buf("w1")]
        xin = xT.rearrange("(c p) t -> p c t", p=128)
        wr = w_in.rearrange("(c p) f -> p c f", p=128)
        ns = 0
        for it in range(S // TT):
            t0 = it * TT
            P.dma("sp", lambda e, t0=t0: e.dma_start(out=x_t[:], in_=xin[:, :, t0:t0 + TT]), writes=[b_x])
            emit_norm(P, C, x_t, b_x, sq_t, b_sq, rstd, b_rstd, h_t, b_h, g_t, b_g)
            for dst, col0 in ((scr["qT"], 0), (scr["kT"], 1024)):
                for g in range(4):
                    ws = g % 2
                    c0 = col0 + g * 256
                    P.dma("pool", lambda e, ws=ws, c0=c0: e.dma_start(out=wt[ws][:], in_=wr[:, :, c0:c0 + 256]), writes=[bw[ws]])
                    for j in range(2):
                        ci = g * 2 + j
                        ps, pb = ps_next(C)
                        for kc in range(NKC):
                            P.op("pe", lambda e, kc=kc, ws=ws, j=j, ps=ps: e.matmul(
                                ps[:], lhsT=wt[ws][:, kc, j * 128:(j + 1) * 128], rhs=h_t[:, kc, :], start=(kc == 0), stop=(kc == NKC - 1)),
                                reads=[bw[ws], b_h], writes=[pb], inc=(kc == NKC - 1))
                        s3 = ns % 3; ns += 1
                        P.op("act", lambda e, s3=s3, ps=ps: e.activation(out=stg[s3][:], in_=ps[:], func=AF.Copy), reads=[pb], writes=[b_stg[s3]])
                        P.dma("sp", lambda e, s3=s3, ci=ci, dst=dst, t0=t0: e.dma_start(out=dst[ci * 128:(ci + 1) * 128, t0:t0 + TT], in_=stg[s3][:]), reads=[b_stg[s3]])
            for cg in range(2):
                P.dma("pool", lambda e, cg=cg: e.dma_start(out=wv_t[cg][:], in_=wr[:, :, 2048 + cg * 512:2048 + (cg + 1) * 512]), writes=[b_wv[cg]])
                for tb in range(4):
                    ps, pb = ps_next(C)
                    for kc in range(NKC):
                        P.op("pe", lambda e, kc=kc, tb=tb, cg=cg, ps=ps: e.matmul(
                            ps[:], lhsT=h_t[:, kc, tb * 128:(tb + 1) * 128], rhs=wv_t[cg][:, kc, :], start=(kc == 0), stop=(kc == NKC - 1)),
                            reads=[b_wv[cg], b_h], writes=[pb], inc=(kc == NKC - 1))
                    P.op("act", lambda e, tb=tb, cg=cg, ps=ps: e.activation(out=vst[cg][:, tb, :], in_=ps[:], func=AF.Copy), reads=[pb], writes=[b_vst[cg]])
                P.dma("sp", lambda e, cg=cg, t0=t0: e.dma_start(
                    out=scr["V"][t0:t0 + TT, cg * 512:(cg + 1) * 512].rearrange("(tb p) c -> p tb c", p=128), in_=vst[cg][:]), reads=[b_vst[cg]])
        P.phase_end()


def sb_B(P, C, S, scr, consts):
    nc = P.nc
    NB = S // 128
    NQC = S // 512
    with ExitStack() as st:
        sb = lambda n, s, d: st.enter_context(nc.sbuf_tensor(f"sbb_{n}", s, d))
        cb = Buf("consts")
        def ld(name, dt=BF16):
            t = sb(name, [128, 128], dt)
            P.dma("sp", lambda e: e.dma_start(out=t[:], in_=consts[name]), writes=[cb])
            return t
        trii = ld("trii"); nones = ld("nones"); strict = ld("strict"); negti = ld("negtri_incl"); identb = ld("identb")
        zer = sb("zer", [128, 512], BF16)
        P.op("pool", lambda e: e.memset(zer[:], 0.0), writes=[cb])
        qa = [sb(f"qa{i}", [64, S], BF16) for i in range(2)]; b_qa = [Buf("qa0"), Buf("qa1")]
        ka = [sb(f"ka{i}", [64, S], BF16) for i in range(2)]; b_ka = [Buf("ka0"), Buf("ka1")]
        va = [sb(f"va{i}", [128, NB, 64], BF16) for i in range(2)]; b_va = [Buf("va0"), Buf("va1")]
        e1 = [sb(f"e1{i}", [128, 512], F32) for i in range(4)]; b_e1 = [Buf(f"e1{i}") for i in range(4)]
        ew = [sb(f"ew{i}", [128, 512], F32) for i in range(3)]; b_ew = [Buf(f"ew{i}") for i in range(3)]
        strictf = sb("strictf", [128, 128], F32)
        P.op("dve", lambda e: e.tensor_copy(out=strictf[:], in_=strict[:]), reads=[cb], writes=[cb])
        lb = [sb(f"lb{i}", [128, 512], BF16) for i in range(4)]; b_lb = [Buf(f"lb{i}") for i in range(4)]
        acc = sb("acc", [128, 512], F32); b_acc = Buf("acc")
        accb = [sb(f"accb{i}", [128, 512], BF16) for i in range(4)]; b_accb = [Buf(f"accb{i}") for i in range(4)]
        pt = [sb(f"pt{i}", [128, 512], BF16) for i in range(4)]; b_pt = [Buf(f"pt{i}") for i in range(4)]
        ost = [sb(f"ost{i}", [64, 512], BF16) for i in range(2)]; b_ost = [Buf("ost0"), Buf("ost1")]
        n = 0
        for h in range(H):
            hs = h % 2
            P.dma("sp", lambda e, h=h, hs=hs: e.dma_start(out=qa[hs][:], in_=scr["qT"][h * 64:(h + 1) * 64, :]), writes=[b_qa[hs]])
            P.dma("sp", lambda e, h=h, hs=hs: e.dma_start(out=ka[hs][:], in_=scr["kT"][h * 64:(h + 1) * 64, :]), writes=[b_ka[hs]])
            P.dma("sp", lambda e, h=h, hs=hs: e.dma_start(out=va[hs][:], in_=scr["V"][:, h * 64:(h + 1) * 64].rearrange("(kb p) d -> p kb d", p=128)),
                  writes=[b_va[hs]])
            for qc in range(NQC):
                po, pob = ps_next(C)
                P.op("pe", lambda e, po=po, hs=hs: e.matmul(po[0:64, :], lhsT=zer[:, 0:64], rhs=zer[:], start=True, stop=False),
                     reads=[cb], writes=[pob], inc=False)
                nkb = 4 * qc + 4
                first = True
                pq = []
                pq2 = []
                for kb in range(nkb - 1, -1, -1):
                    off = max(0, kb - 4 * qc) * 128
                    diag = kb >= 4 * qc
                    i2 = n % 3; i3 = n % 4; ip = (n - 1) % 4; n += 1
                    ps, pb = ps_next(C)
                    if ps is po:
                        ps, pb = ps_next(C)
                    P.op("pe", lambda e, kb=kb, off=off, ps=ps, hs=hs, qc=qc: e.matmul(
                        ps[:, off:512], lhsT=ka[hs][:, kb * 128:(kb + 1) * 128], rhs=qa[hs][:, qc * 512 + off:(qc + 1) * 512],
                        start=True, stop=True), reads=[b_ka[hs], b_qa[hs]], writes=[pb])
                    P.op("act", lambda e, off=off, ps=ps, i3=i3: e.activation(out=e1[i3][:, off:512], in_=ps[:, off:512], func=AF.Exp, scale=0.125),
                         reads=[pb], writes=[b_e1[i3]])
                    if off > 0:
                        P.op("pool", lambda e, off=off, i3=i3: e.memset(lb[i3][:, 0:off], 0.0), writes=[b_lb[i3]])
                    P.op("act", lambda e, off=off, i3=i3: e.activation(out=lb[i3][:, off:512], in_=e1[i3][:, off:512], func=AF.Ln, bias=1.0, scale=1.0),
                         reads=[b_e1[i3]], writes=[b_lb[i3]])
                    if diag:
                        P.op("pool", lambda e, off=off, i3=i3: e.tensor_tensor(out=lb[i3][:, off:off + 128], in0=lb[i3][:, off:off + 128], in1=strict[:], op=ALU.mult),
                             reads=[b_lb[i3], cb], writes=[b_lb[i3]])
                        P.op("pool", lambda e, off=off, i3=i3: e.tensor_tensor(out=e1[i3][:, off:off + 128], in0=e1[i3][:, off:off + 128], in1=strictf[:], op=ALU.mult),
                             reads=[b_e1[i3], cb, b_lb[i3]], writes=[b_e1[i3]])
                    if kb > 0:
                        if first:
                            P.op("dve", lambda e, i3=i3: e.tensor_copy(out=accb[i3][:], in_=lb[i3][:]), reads=[b_lb[i3]], writes=[b_accb[i3]])
                        else:
                            P.op("dve", lambda e, i3=i3, ip=ip: e.tensor_tensor(out=accb[i3][:], in0=accb[ip][:], in1=lb[i3][:], op=ALU.add),
                                 reads=[b_lb[i3], b_accb[ip]], writes=[b_accb[i3]])
                    def stage_b(kb=kb, off=off, diag=diag, i3=i3, ip=ip, i2=i2, first=first, hs=hs, qc=qc, po=po, pob=pob):
                        pw, pwb = ps_next(C)
                        if pw is po:
                            pw, pwb = ps_next(C)
                        P.op("pe", lambda e: e.matmul(pw[:, off:512], lhsT=trii[:], rhs=lb[i3][:, off:512], start=True, stop=first),
                             reads=[b_lb[i3], cb], writes=[pwb], inc=first)
                        if not first:
                            P.op("pe", lambda e: e.matmul(pw[:, off:512], lhsT=nones[:], rhs=accb[ip][:, off:512], start=False, stop=True),
                                 reads=[b_accb[ip], cb], writes=[pwb])
                        P.op("act", lambda e: e.activation(out=ew[i2][:, off:512], in_=pw[:, off:512], func=AF.Exp, scale=0.125),
                             reads=[pwb], writes=[b_ew[i2]])
                        P.op("dve", lambda e: e.tensor_tensor(out=pt[i3][:, off:512], in0=e1[i3][:, off:512], in1=ew[i2][:, off:512], op=ALU.mult),
                             reads=[b_e1[i3], b_ew[i2]], writes=[b_pt[i3]])
                        def stage_c():
                            P.op("pe", lambda e: e.matmul(
                                po[0:64, off:512], lhsT=va[hs][:, kb, :], rhs=pt[i3][:, off:512], start=False, stop=(kb == 0)),
                                reads=[b_va[hs], b_pt[i3]], writes=[pob])
                        pq2.append(stage_c)
                        if len(pq2) > 1:
                            pq2.pop(0)()
                    pq.append(stage_b)
                    if len(pq) > PDEPTH:
                        pq.pop(0)()
                    first = False
                while pq:
                    pq.pop(0)()
                while pq2:
                    pq2.pop(0)()
                oi = qc % 2
                P.op("act", lambda e, po=po, oi=oi: e.activation(out=ost[oi][:], in_=po[0:64, :], func=AF.Copy), reads=[pob], writes=[b_ost[oi]])
                P.dma("sp", lambda e, h=h, qc=qc, oi=oi: e.dma_start(out=scr["oT"][h * 64:(h + 1) * 64, qc * 512:(qc + 1) * 512], in_=ost[oi][:]),
                      reads=[b_ost[oi]])
        P.phase_end()


from contextlib import ExitStack

GH = 4
DK = 128
DV = 256


def gla_A(P, C, xT, S, w_in, gnorm, wgu, b_gate, scr, consts):
    nc = P.nc
    with ExitStack() as st:
        sb = lambda n, s, d: st.enter_context(nc.sbuf_tensor(f"ga_{n}", s, d))
        g_t = sb("g", [128, NKC], F32); b_g = Buf("g")
        P.dma("sp", lambda e: e.dma_start(out=g_t[:], in_=gnorm.rearrange("(c p) -> p c", p=128), allow_slow_non_contiguous=True), writes=[b_g])
        bg = sb("bg", [128, 4], F32); b_bg = Buf("bg")
        P.dma("sp", lambda e: e.dma_start(out=bg[:], in_=b_gate.rearrange("(c p) -> p c", p=128), allow_slow_non_contiguous=True), writes=[b_bg])
        P.op("dve", lambda e: e.tensor_scalar(out=bg[:], in0=bg[:], scalar1=-1.0, scalar2=None, op0=ALU.mult), reads=[b_bg], writes=[b_bg])
        wgu_t = sb("wgu", [16, 512], BF16); b_wgu = Buf("wgu")
        P.dma("pool", lambda e: e.dma_start(out=wgu_t[:], in_=wgu), writes=[b_wgu])
        ident = sb("ident", [128, 128], F32); b_ident = Buf("ident")
        P.dma("sp", lambda e: e.dma_start(out=ident[:], in_=consts["ident"]), writes=[b_ident])
        x_t, b_x, sq_t, b_sq, rstd, b_rstd, h_t, b_h = load_x_norm(P, C, sb, None, 0, g_t, b_g, "ga")
        stg = [sb(f"stg{i}", [128, TT], BF16) for i in range(3)]; b_stg = [Buf(f"stg{i}") for i in range(3)]
        vst = [sb(f"vst{i}", [128, 4, 512], BF16) for i in range(2)]; b_vst = [Buf("vst0"), Buf("vst1")]
        wv_t = [sb(f"wv{i}", [128, NKC, 512], BF16) for i in range(2)]; b_wv = [Buf("wv0"), Buf("wv1")]
        wt = [sb(f"w{i}", [128, NKC, 256], BF16) for i in range(2)]; bw = [Buf("w0"), Buf("w1")]
        wl = sb("wl", [128, NKC, 16], BF16); b_wl = Buf("wl")
        glT = sb("glT", [16, TT], BF16); b_glT = Buf("glT")
        la = [sb(f"la{i}", [128, TT], F32) for i in range(2)]; b_la = [Buf("la0"), Buf("la1")]
        lat = sb("lat", [128, 4, 512], F32); b_lat = Buf("lat")
        xin = xT.rearrange("(c p) t -> p c t", p=128)
        wr = w_in.rearrange("(c p) f -> p c f", p=128)
        P.dma("pool", lambda e: e.dma_start(out=wl[:], in_=wr[:, :, 2048:2064]), writes=[b_wl])
        ns = 0
        for it in range(S // TT):
            t0 = it * TT
            P.dma("sp", lambda e, t0=t0: e.dma_start(out=x_t[:], in_=xin[:, :, t0:t0 + TT]), writes=[b_x])
            emit_norm(P, C, x_t, b_x, sq_t, b_sq, rstd, b_rstd, h_t, b_h, g_t, b_g)
            for dst, col0, ng, fn in ((scr["qT"], 0, 2, AF.Copy), (scr["kT"], 512, 2, AF.Copy), (scr["sg"], 2064, 4, AF.Silu)):
                for g in range(ng):
                    ws = g % 2
                    c0 = col0 + g * 256
                    P.dma("pool", lambda e, ws=ws, c0=c0: e.dma_start(out=wt[ws][:], in_=wr[:, :, c0:c0 + 256]), writes=[bw[ws]])
                    for j in range(2):
                        ci = g * 2 + j
                        ps, pb = ps_next(C)
                        for kc in range(NKC):
                            P.op("pe", lambda e, kc=kc, ws=ws, j=j, ps=ps: e.matmul(
                                ps[:], lhsT=wt[ws][:, kc, j * 128:(j + 1) * 128], rhs=h_t[:, kc, :], start=(kc == 0), stop=(kc == NKC - 1)),
                                reads=[bw[ws], b_h], writes=[pb], inc=(kc == NKC - 1))
                        s3 = ns % 3; ns += 1
                        P.op("act", lambda e, s3=s3, ps=ps, fn=fn: e.activation(out=stg[s3][:], in_=ps[:], func=fn), reads=[pb], writes=[b_stg[s3]])
                        P.dma("sp", lambda e, s3=s3, ci=ci, dst=dst, t0=t0: e.dma_start(out=dst[ci * 128:(ci + 1) * 128, t0:t0 + TT], in_=stg[s3][:]), reads=[b_stg[s3]])
            for cg in range(3):
                ws = cg % 2
                P.dma("pool", lambda e, cg=cg, ws=ws: e.dma_start(out=wv_t[ws][:], in_=wr[:, :, 512 + cg * 512:512 + (cg + 1) * 512]), writes=[b_wv[ws]])
                for tb in range(4):
                    ps, pb = ps_next(C)
                    for kc in range(NKC):
                        P.op("pe", lambda e, kc=kc, tb=tb, ws=ws, ps=ps: e.matmul(
                            ps[:], lhsT=h_t[:, kc, tb * 128:(tb + 1) * 128], rhs=wv_t[ws][:, kc, :], start=(kc == 0), stop=(kc == NKC - 1)),
                            reads=[b_wv[ws], b_h], writes=[pb], inc=(kc == NKC - 1))
                    P.op("act", lambda e, tb=tb, ws=ws, ps=ps: e.activation(out=vst[ws][:, tb, :], in_=ps[:], func=AF.Copy), reads=[pb], writes=[b_vst[ws]])
                P.dma("sp", lambda e, ws=ws, cg=cg, t0=t0: e.dma_start(
                    out=scr["KV"][t0:t0 + TT, cg * 512:(cg + 1) * 512].rearrange("(tb p) c -> p tb c", p=128), in_=vst[ws][:]), reads=[b_vst[ws]])
            ps, pb = ps_next(C)
            for kc in range(NKC):
                P.op("pe", lambda e, kc=kc, ps=ps: e.matmul(ps[0:16, :], lhsT=wl[:, kc, :], rhs=h_t[:, kc, :], start=(kc == 0), stop=(kc == NKC - 1)),
                     reads=[b_wl, b_h], writes=[pb], inc=(kc == NKC - 1))
            P.op("act", lambda e, ps=ps: e.activation(out=glT[:], in_=ps[0:16, :], func=AF.Copy), reads=[pb], writes=[b_glT])
            for hc in range(4):
                ps, pb = ps_next(C)
                P.op("pe", lambda e, hc=hc, ps=ps: e.matmul(ps[:], lhsT=wgu_t[:, hc * 128:(hc + 1) * 128], rhs=glT[:], start=True, stop=True),
                     reads=[b_wgu, b_glT], writes=[pb])
                li = hc % 2
                P.op("act", lambda e, hc=hc, ps=ps, li=li: e.activation(out=la[li][:], in_=ps[:], func=AF.Exp, bias=bg[:, hc:hc + 1], scale=-1.0),
                     reads=[pb, b_bg], writes=[b_la[li]])
                P.op("act", lambda e, li=li: e.activation(out=la[li][:], in_=la[li][:], func=AF.Ln, bias=1.0, scale=1.0), reads=[b_la[li]], writes=[b_la[li]])
                P.op("dve", lambda e, li=li: e.tensor_scalar(out=la[li][:], in0=la[li][:], scalar1=-1.0 / 16.0, scalar2=None, op0=ALU.mult),
                     reads=[b_la[li]], writes=[b_la[li]])
                P.dma("sp", lambda e, hc=hc, li=li, t0=t0: e.dma_start(out=scr["laT"][hc * 128:(hc + 1) * 128, t0:t0 + TT], in_=la[li][:]), reads=[b_la[li]])
                pst, pstb = ps_next(C)
                for tb in range(4):
                    P.op("pe", lambda e, tb=tb, li=li, pst=pst: e.transpose(pst[:, tb * 128:(tb + 1) * 128], la[li][:, tb * 128:(tb + 1) * 128], ident[:]),
                         reads=[b_la[li], b_ident], writes=[pstb], inc=(tb == 3))
                P.op("dve", lambda e, hc=hc, pst=pst: e.tensor_copy(out=lat[:, :, hc * 128:(hc + 1) * 128], in_=pst[:].rearrange("p (a b) -> p a b", a=4)),
                     reads=[pstb], writes=[b_lat])
            P.dma("sp", lambda e, t0=t0: e.dma_start(out=scr["laK"][t0:t0 + TT, :].rearrange("(tb p) c -> p tb c", p=128), in_=lat[:]), reads=[b_lat])
        P.phase_end()


def gla_B(P, C, S, scr, consts):
    nc = P.nc
    NB = S // 128
    NCH = S // 64
    PW = min(2048, S)
    GRP = min(16, NB)
    with ExitStack() as st:
        sb = lambda n, s, d: st.enter_context(nc.sbuf_tensor(f"gb_{n}", s, d))
        cb = Buf("consts")
        m01 = sb("m01", [128, PW], BF16)
        P.dma("sp", lambda e: e.dma_start(out=m01[:], in_=consts["m01"][:, 0:PW]), writes=[cb])
        umat = sb("umat", [128, 128], F32)
        P.dma("sp", lambda e: e.dma_start(out=umat[:], in_=consts["umat"]), writes=[cb])
        bcaus = sb("bcaus", [128, 128], BF16)
        P.dma("sp", lambda e: e.dma_start(out=bcaus[:], in_=consts["bcaus"]), writes=[cb])
        qd = sb("qd", [128, S], BF16); b_qd = Buf("qd")
        kd = sb("kd", [128, S], BF16); b_kd = Buf("kd")
        bT = sb("bT", [128, S], F32); b_bT = Buf("bT")
        tmp = [sb(f"tmp{i}", [128, PW], F32) for i in range(2)]; b_tmp = [Buf("tmp0"), Buf("tmp1")]
        dcol = sb("dcol", [128, NCH], F32); b_dcol = Buf("dcol")
        ktok = sb("ktok", [128, NB, 128], BF16); b_ktok = Buf("ktok")
        vtok = sb("vtok", [128, NB, 256], BF16); b_vtok = Buf("vtok")
        latk = [sb(f"latk{i}", [128, GRP, 128], F32) for i in range(2)]; b_latk = [Buf("latk0"), Buf("latk1")]
        eu = [sb(f"eu{i}", [128, 128], F32) for i in range(2)]; b_eu = [Buf("eu0"), Buf("eu1")]
        ku = sb("ku", [128, NB, 128], BF16); b_ku = Buf("ku")
        att = [sb(f"att{i}", [128, 128], BF16) for i in range(2)]; b_att = [Buf("att0"), Buf("att1")]
        state = sb("state", [128, 256], F32); b_state = Buf("state")
        stb = [sb(f"stb{i}", [128, 256], BF16) for i in range(2)]; b_stb = [Buf("stb0"), Buf("stb1")]
        ost = [sb(f"ost{i}", [128, 2, 512], BF16) for i in range(2)]; b_ost = [Buf("ost0"), Buf("ost1")]
        for h in range(GH):
            P.dma("sp", lambda e, h=h: e.dma_start(out=qd[:], in_=scr["qT"][h * 128:(h + 1) * 128, :]), writes=[b_qd])
            P.dma("sp", lambda e, h=h: e.dma_start(out=kd[:], in_=scr["kT"][h * 128:(h + 1) * 128, :]), writes=[b_kd])
            P.dma("sp", lambda e, h=h: e.dma_start(out=bT[:], in_=scr["laT"][h * 128:(h + 1) * 128, :]), writes=[b_bT])
            P.dma("sp", lambda e, h=h: e.dma_start(out=ktok[:], in_=scr["KV"][:, h * 128:(h + 1) * 128].rearrange("(kb p) d -> p kb d", p=128)), writes=[b_ktok])
            P.dma("sp", lambda e, h=h: e.dma_start(out=vtok[:], in_=scr["KV"][:, 512 + h * 256:512 + (h + 1) * 256].rearrange("(kb p) d -> p kb d", p=128)), writes=[b_vtok])
            for pc in range(S // PW):
                sl = slice(pc * PW, (pc + 1) * PW)
                P.op("dve", lambda e, sl=sl: e.tensor_tensor_scan(out=bT[:, sl], data0=m01[:], data1=bT[:, sl], initial=0.0, op0=ALU.mult, op1=ALU.add),
                     reads=[b_bT, cb], writes=[b_bT])
                ti = pc % 2
                P.op("act", lambda e, sl=sl, ti=ti: e.activation(out=tmp[ti][:], in_=bT[:, sl], func=AF.Exp), reads=[b_bT], writes=[b_tmp[ti]])
                P.op("dve", lambda e, sl=sl, ti=ti: e.scalar_tensor_tensor(out=qd[:, sl], in0=qd[:, sl], scalar=float(DK) ** -0.5, in1=tmp[ti][:], op0=ALU.mult, op1=ALU.mult),
                     reads=[b_qd, b_tmp[ti]], writes=[b_qd])
                P.op("act", lambda e, sl=sl, ti=ti: e.activation(out=tmp[ti][:], in_=bT[:, sl], func=AF.Exp, scale=-1.0), reads=[b_bT, b_qd], writes=[b_tmp[ti]])
                P.op("dve", lambda e, sl=sl, ti=ti: e.tensor_tensor(out=kd[:, sl], in0=kd[:, sl], in1=tmp[ti][:], op=ALU.mult),
                     reads=[b_kd, b_tmp[ti]], writes=[b_kd])
            P.op("act", lambda e: e.activation(out=dcol[:], in_=bT[:].rearrange("p (n c) -> p n c", c=64)[:, :, 63], func=AF.Exp), reads=[b_bT], writes=[b_dcol])
            for g in range(NB // GRP):
                gi = g % 2
                P.dma("sp", lambda e, g=g, gi=gi, h=h: e.dma_start(
                    out=latk[gi][:], in_=scr["laK"][g * GRP * 128:(g + 1) * GRP * 128, h * 128:(h + 1) * 128].rearrange("(kb p) d -> p kb d", p=128)), writes=[b_latk[gi]])
                for j in range(GRP):
                    tb = g * GRP + j
                    ps, pb = ps_next(C)
                    P.op("pe", lambda e, j=j, gi=gi, ps=ps: e.matmul(ps[:, 0:128], lhsT=umat[:], rhs=latk[gi][:, j, :], start=True, stop=True),
                         reads=[b_latk[gi], cb], writes=[pb])
                    ei = tb % 2
                    P.op("act", lambda e, ei=ei, ps=ps: e.activation(out=eu[ei][:], in_=ps[:, 0:128], func=AF.Exp), reads=[pb], writes=[b_eu[ei]])
                    P.op("dve", lambda e, ei=ei, tb=tb: e.tensor_tensor(out=ku[:, tb, :], in0=ktok[:, tb, :], in1=eu[ei][:], op=ALU.mult),
                         reads=[b_eu[ei], b_ktok], writes=[b_ku])
            P.op("dve", lambda e: e.memset(state[:], 0.0), writes=[b_state])
            P.op("dve", lambda e: e.memset(stb[0][:], 0.0), writes=[b_stb[0]])
            P.op("dve", lambda e: e.memset(stb[1][:], 0.0), writes=[b_stb[1]])
            for tb in range(NB):
                ai = tb % 2
                ps, pb = ps_next(C)
                P.op("pe", lambda e, tb=tb, ps=ps: e.matmul(ps[:, 0:128], lhsT=kd[:, tb * 128:(tb + 1) * 128], rhs=qd[:, tb * 128:(tb + 1) * 128], start=True, stop=True),
                     reads=[b_kd, b_qd], writes=[pb])
                P.op("dve", lambda e, ai=ai, ps=ps: e.tensor_tensor(out=att[ai][:], in0=ps[:, 0:128], in1=bcaus[:], op=ALU.mult), reads=[pb, cb], writes=[b_att[ai]])
                oi = (tb // 4) % 2
                for j in range(2):
                    n = tb * 2 + j
                    si = n % 2
                    r0 = 64 * j
                    po, pob = ps_next(C)
                    for eh in range(2):
                        P.op("pe", lambda e, tb=tb, r0=r0, eh=eh, ai=ai, po=po: e.matmul(
                            po[:, eh * 64:(eh + 1) * 64], lhsT=vtok[r0:r0 + 64, tb, eh * 128:(eh + 1) * 128], rhs=att[ai][r0:r0 + 64, r0:r0 + 64],
                            start=True, stop=False), reads=[b_vtok, b_att[ai]], writes=[pob], inc=False)
                        P.op("pe", lambda e, tb=tb, r0=r0, eh=eh, si=si, po=po: e.matmul(
                            po[:, eh * 64:(eh + 1) * 64], lhsT=stb[si][:, eh * 128:(eh + 1) * 128], rhs=qd[:, tb * 128 + r0:tb * 128 + r0 + 64],
                            start=False, stop=True), reads=[b_stb[si], b_qd], writes=[pob], inc=(eh == 1))
                    c0 = (tb % 4) * 128 + r0
                    P.op("act", lambda e, po=po, oi=oi, c0=c0: e.activation(out=ost[oi][:, :, c0:c0 + 64], in_=po[:, 0:128].rearrange("p (a b) -> p a b", a=2), func=AF.Copy),
                         reads=[pob], writes=[b_ost[oi]])
                    pu, pub = ps_next(C)
                    P.op("pe", lambda e, tb=tb, r0=r0, pu=pu: e.matmul(pu[:, 0:256], lhsT=ku[r0:r0 + 64, tb, :], rhs=vtok[r0:r0 + 64, tb, :], start=True, stop=True),
                         reads=[b_ku, b_vtok], writes=[pub])
                    P.op("dve", lambda e, n=n, pu=pu: e.scalar_tensor_tensor(out=state[:], in0=state[:], scalar=dcol[:, n:n + 1], in1=pu[:, 0:256], op0=ALU.mult, op1=ALU.add),
                         reads=[b_state, b_dcol, pub], writes=[b_state])
                    P.op("dve", lambda e, si=si: e.tensor_copy(out=stb[1 - si][:], in_=state[:]), reads=[b_state], writes=[b_stb[1 - si]])
                if tb % 4 == 3:
                    t0 = (tb // 4) * 512
                    P.dma("sp", lambda e, h=h, oi=oi, t0=t0: e.dma_start(
                        out=scr["oT"][h * 256:(h + 1) * 256, t0:t0 + 512].rearrange("(a p) t -> p a t", p=128), in_=ost[oi][:]), reads=[b_ost[oi]])
        P.phase_end()


def gla_C(P, C, xT, S, w_out, g_out, scr):
    nc = P.nc
    with ExitStack() as st:
        sb = lambda n, s, d: st.enter_context(nc.sbuf_tensor(f"gc_{n}", s, d))
        go = sb("go", [128, 2], F32); b_go = Buf("go")
        P.dma("sp", lambda e: e.dma_start(out=go[:], in_=g_out.rearrange("(c p) -> p c", p=128), allow_slow_non_contiguous=True), writes=[b_go])
        x_t = [sb(f"x{i}", [128, NKC, TT], F32) for i in range(2)]; b_x = [Buf("x0"), Buf("x1")]
        o_t = [sb(f"o{i}", [128, NKC, TT], BF16) for i in range(2)]; b_o = [Buf("o0"), Buf("o1")]
        g_t = [sb(f"gt{i}", [128, NKC, TT], BF16) for i in range(2)]; b_gt = [Buf("gt0"), Buf("gt1")]
        sq = sb("sq", [128, NKC, TT], F32); b_sq = Buf("sq")
        rs = [sb(f"rs{i}", [128, TT], F32) for i in range(2)]; b_rs = [Buf("rs0"), Buf("rs1")]
        wo = [sb(f"wo{i}", [128, NKC, 256], BF16) for i in range(2)]; b_wo = [Buf("wo0"), Buf("wo1")]
        xin = xT.rearrange("(c p) t -> p c t", p=128)
        oin = scr["oT"].rearrange("(c p) t -> p c t", p=128)
        gin = scr["sg"].rearrange("(c p) t -> p c t", p=128)
        wr = w_out.rearrange("(c p) f -> p c f", p=128)
        for it in range(S // TT):
            t0 = it * TT
            xs = it % 2
            P.dma("sp", lambda e, t0=t0, xs=xs: e.dma_start(out=x_t[xs][:], in_=xin[:, :, t0:t0 + TT]), writes=[b_x[xs]])
            P.dma("sp", lambda e, t0=t0, xs=xs: e.dma_start(out=o_t[xs][:], in_=oin[:, :, t0:t0 + TT]), writes=[b_o[xs]])
            P.dma("sp", lambda e, t0=t0, xs=xs: e.dma_start(out=g_t[xs][:], in_=gin[:, :, t0:t0 + TT]), writes=[b_gt[xs]])
            P.op("act", lambda e, xs=xs: e.activation(out=sq[:], in_=o_t[xs][:], func=AF.Square), reads=[b_o[xs]], writes=[b_sq])
            for hh in range(4):
                ps, pb = ps_next(C)
                for j in range(2):
                    P.op("pe", lambda e, hh=hh, j=j, ps=ps: e.matmul(ps[:], lhsT=C.ones_f[:], rhs=sq[:, hh * 2 + j, :], start=(j == 0), stop=(j == 1)),
                         reads=[b_sq, C.b_ones_f], writes=[pb], inc=(j == 1))
                ri = hh % 2
                P.op("dve", lambda e, ri=ri, ps=ps: e.tensor_scalar(out=rs[ri][:], in0=ps[:], scalar1=1.0 / 256, scalar2=1e-6, op0=ALU.mult, op1=ALU.add),
                     reads=[pb], writes=[b_rs[ri]])
                P.op("act", lambda e, ri=ri: e.activation(out=rs[ri][:], in_=rs[ri][:], func=AF.Sqrt), reads=[b_rs[ri]], writes=[b_rs[ri]])
                P.op("dve", lambda e, ri=ri: e.reciprocal(out=rs[ri][:], in_=rs[ri][:]), reads=[b_rs[ri]], writes=[b_rs[ri]])
                for j in range(2):
                    c = hh * 2 + j
                    P.op("dve", lambda e, c=c, j=j, ri=ri, xs=xs: e.scalar_tensor_tensor(out=o_t[xs][:, c, :], in0=o_t[xs][:, c, :], scalar=go[:, j:j + 1], in1=rs[ri][:],
                                                                                       op0=ALU.mult, op1=ALU.mult), reads=[b_o[xs], b_go, b_rs[ri]], writes=[b_o[xs]])
            P.op("pool", lambda e, xs=xs: e.tensor_tensor(out=o_t[xs][:], in0=o_t[xs][:], in1=g_t[xs][:], op=ALU.mult), reads=[b_o[xs], b_gt[xs]], writes=[b_o[xs]])
            for g in range(4):
                ws = g % 2
                P.dma("pool", lambda e, ws=ws, g=g: e.dma_start(out=wo[ws][:], in_=wr[:, :, g * 256:(g + 1) * 256]), writes=[b_wo[ws]])
                for j in range(2):
                    dc = g * 2 + j
                    ps, pb = ps_next(C)
                    for kc in range(NKC):
                        P.op("pe", lambda e, kc=kc, ws=ws, j=j, ps=ps, xs=xs: e.matmul(
                            ps[:], lhsT=wo[ws][:, kc, j * 128:(j + 1) * 128], rhs=o_t[xs][:, kc, :], start=(kc == 0), stop=(kc == NKC - 1)),
                            reads=[b_wo[ws], b_o[xs]], writes=[pb], inc=(kc == NKC - 1))
                    P.op("dve", lambda e, dc=dc, ps=ps, xs=xs: e.tensor_tensor(out=x_t[xs][:, dc, :], in0=x_t[xs][:, dc, :], in1=ps[:], op=ALU.add),
                         reads=[pb, b_x[xs]], writes=[b_x[xs]])
            P.dma("sp", lambda e, t0=t0, xs=xs: e.dma_start(out=xin[:, :, t0:t0 + TT], in_=x_t[xs][:]), reads=[b_x[xs]])
        P.phase_end()


from contextlib import ExitStack

NG = 4
NSA_ILV = True


def nsa_A(P, C, xT, S, w_in, gnorm, g_q, g_k, scr, consts):
    nc = P.nc
    with ExitStack() as st:
        sb = lambda n, s, d: st.enter_context(nc.sbuf_tensor(f"na_{n}", s, d))
        g_t = sb("g", [128, NKC], F32); b_g = Buf("g")
        P.dma("sp", lambda e: e.dma_start(out=g_t[:], in_=gnorm.rearrange("(c p) -> p c", p=128), allow_slow_non_contiguous=True), writes=[b_g])
        gq = sb("gq", [128, 1], F32); gk = sb("gk", [128, 1], F32); b_gqk = Buf("gqk")
        for half in range(2):
            P.dma("sp", lambda e, half=half: e.dma_start(out=gq[half * 64:(half + 1) * 64, :], in_=g_q.rearrange("(p o) -> p o", o=1), allow_slow_non_contiguous=True), writes=[b_gqk])
            P.dma("sp", lambda e, half=half: e.dma_start(out=gk[half * 64:(half + 1) * 64, :], in_=g_k.rearrange("(p o) -> p o", o=1), allow_slow_non_contiguous=True), writes=[b_gqk])
        bones = sb("bones", [128, 128], F32); b_bones = Buf("bones")
        P.op("pool", lambda e: e.memset(bones[:], 0.0), writes=[b_bones])
        P.op("pool", lambda e: e.memset(bones[0:64, 0:64], 1.0), writes=[b_bones])
        P.op("pool", lambda e: e.memset(bones[64:128, 64:128], 1.0), writes=[b_bones])
        pm = sb("pm", [128, 128], BF16)
        P.dma("sp", lambda e: e.dma_start(out=pm[:], in_=consts["pm"]), writes=[b_bones])
        x_t, b_x, sq_t, b_sq, rstd, b_rstd, h_t, b_h = load_x_norm(P, C, sb, None, 0, g_t, b_g, "na")
        cos_t = sb("cos", [128, TT], F32); sin_t = sb("sin", [128, TT], F32); b_cs = Buf("cs")
        sqq = [sb(f"sqq{i}", [128, TT], F32) for i in range(2)]; b_sqq = [Buf("sqq0"), Buf("sqq1")]
        rr = [sb(f"rr{i}", [128, TT], F32) for i in range(2)]; b_rr = [Buf("rr0"), Buf("rr1")]
        qn = [sb(f"qn{i}", [128, TT], BF16) for i in range(2)]; b_qn = [Buf("qn0"), Buf("qn1")]
        t1 = [sb(f"t1{i}", [128, TT], F32) for i in range(2)]; b_t1 = [Buf("t10"), Buf("t11")]
        t2 = [sb(f"t2{i}", [128, TT], F32) for i in range(2)]; b_t2 = [Buf("t20"), Buf("t21")]
        stg = [sb(f"stg{i}", [128, TT], BF16) for i in range(3)]; b_stg = [Buf(f"stg{i}") for i in range(3)]
        gst = sb("gst", [48, TT], F32); b_gst = Buf("gst")
        vst = [sb(f"vst{i}", [128, 4, 512], BF16) for i in range(2)]; b_vst = [Buf("vst0"), Buf("vst1")]
        wv_t = [sb(f"wv{i}", [128, NKC, 512], BF16) for i in range(2)]; b_wv = [Buf("wv0"), Buf("wv1")]
        wt = [sb(f"w{i}", [128, NKC, 256], BF16) for i in range(2)]; bw = [Buf("w0"), Buf("w1")]
        wgt = sb("wgt", [128, NKC, 48], BF16); b_wgt = Buf("wgt")
        xin = xT.rearrange("(c p) t -> p c t", p=128)
        wr = w_in.rearrange("(c p) f -> p c f", p=128)
        P.dma("pool", lambda e: e.dma_start(out=wgt[:], in_=wr[:, :, 2560:2608]), writes=[b_wgt])
        ns = 0; nq = 0; ng_ = 0
        jobs = [(scr["qT"], 0, 0, 1024, "q"), (scr["kT3"], 0, 1024, 256, "k"), (scr["vcT"], 0, 1280, 256, "raw"),
                (scr["kT3"], 256, 1536, 256, "k"), (scr["kT3"], 512, 2048, 256, "k")]
        for it in range(S // TT):
            t0 = it * TT
            P.dma("sp", lambda e, t0=t0: e.dma_start(out=x_t[:], in_=xin[:, :, t0:t0 + TT]), writes=[b_x])
            P.dma("sp", lambda e, t0=t0: e.dma_start(out=cos_t[:], in_=consts["cos"][:, t0:t0 + TT]), writes=[b_cs])
            P.dma("sp", lambda e, t0=t0: e.dma_start(out=sin_t[:], in_=consts["sin"][:, t0:t0 + TT]), writes=[b_cs])
            emit_norm(P, C, x_t, b_x, sq_t, b_sq, rstd, b_rstd, h_t, b_h, g_t, b_g)
            for dst, row0, col0, ncols, mode in jobs:
                for g in range(ncols // 256):
                    ws = ng_ % 2; ng_ += 1
                    c0 = col0 + g * 256
                    P.dma("pool", lambda e, ws=ws, c0=c0: e.dma_start(out=wt[ws][:], in_=wr[:, :, c0:c0 + 256]), writes=[bw[ws]])
                    for j in range(2):
                        ci = g * 2 + j
                        ps, pb = ps_next(C)
                        for kc in range(NKC):
                            P.op("pe", lambda e, kc=kc, ws=ws, j=j, ps=ps: e.matmul(
                                ps[:], lhsT=wt[ws][:, kc, j * 128:(j + 1) * 128], rhs=h_t[:, kc, :], start=(kc == 0), stop=(kc == NKC - 1)),
                                reads=[bw[ws], b_h], writes=[pb], inc=(kc == NKC - 1))
                        s3 = ns % 3; ns += 1
                        r0 = row0 + ci * 128
                        if mode == "raw":
                            P.op("act", lambda e, s3=s3, ps=ps: e.activation(out=stg[s3][:], in_=ps[:], func=AF.Copy), reads=[pb], writes=[b_stg[s3]])
                        else:
                            gcol = gq if mode == "q" else gk
                            i = nq % 2; nq += 1
                            P.op("act", lambda e, i=i, ps=ps: e.activation(out=sqq[i][:], in_=ps[:], func=AF.Square), reads=[pb], writes=[b_sqq[i]])
                            ps2, pb2 = ps_next(C)
                            P.op("pe", lambda e, i=i, ps2=ps2: e.matmul(ps2[:], lhsT=bones[:], rhs=sqq[i][:], start=True, stop=True),
                                 reads=[b_bones, b_sqq[i]], writes=[pb2])
                            P.op("dve", lambda e, i=i, ps2=ps2: e.tensor_scalar(out=rr[i][:], in0=ps2[:], scalar1=1.0 / DH, scalar2=1e-6, op0=ALU.mult, op1=ALU.add),
                                 reads=[pb2], writes=[b_rr[i]])
                            P.op("act", lambda e, i=i: e.activation(out=rr[i][:], in_=rr[i][:], func=AF.Sqrt), reads=[b_rr[i]], writes=[b_rr[i]])
                            P.op("dve", lambda e, i=i: e.reciprocal(out=rr[i][:], in_=rr[i][:]), reads=[b_rr[i]], writes=[b_rr[i]])
                            P.op("dve", lambda e, i=i, ps=ps, gcol=gcol: e.scalar_tensor_tensor(out=qn[i][:], in0=ps[:], scalar=gcol[:, 0:1], in1=rr[i][:], op0=ALU.mult, op1=ALU.mult),
                                 reads=[pb, b_gqk, b_rr[i]], writes=[b_qn[i]])
                            ps3, pb3 = ps_next(C)
                            P.op("pe", lambda e, i=i, ps3=ps3: e.matmul(ps3[:], lhsT=pm[:], rhs=qn[i][:], start=True, stop=True),
                                 reads=[b_bones, b_qn[i]], writes=[pb3])
                            P.op("pool", lambda e, i=i: e.tensor_tensor(out=t1[i][:], in0=qn[i][:], in1=cos_t[:], op=ALU.mult), reads=[b_qn[i], b_cs], writes=[b_t1[i]])
                            P.op("dve", lambda e, i=i, ps3=ps3: e.tensor_tensor(out=t2[i][:], in0=ps3[:], in1=sin_t[:], op=ALU.mult), reads=[pb3, b_cs], writes=[b_t2[i]])
                            P.op("dve", lambda e, i=i, s3=s3: e.tensor_tensor(out=stg[s3][:], in0=t1[i][:], in1=t2[i][:], op=ALU.add),
                                 reads=[b_t1[i], b_t2[i]], writes=[b_stg[s3]])
                        P.dma("sp", lambda e, s3=s3, r0=r0, dst=dst, t0=t0: e.dma_start(out=dst[r0:r0 + 128, t0:t0 + TT], in_=stg[s3][:]), reads=[b_stg[s3]])
            for cg in range(3):
                ws = cg % 2
                P.dma("pool", lambda e, cg=cg, ws=ws: e.dma_start(out=wv_t[ws][:], in_=wr[:, :, 1024 + cg * 512:1024 + (cg + 1) * 512]), writes=[b_wv[ws]])
                for tb in range(4):
                    ps, pb = ps_next(C)
                    for kc in range(NKC):
                        P.op("pe", lambda e, kc=kc, tb=tb, ws=ws, ps=ps: e.matmul(
                            ps[:], lhsT=h_t[:, kc, tb * 128:(tb + 1) * 128], rhs=wv_t[ws][:, kc, :], start=(kc == 0), stop=(kc == NKC - 1)),
                            reads=[b_wv[ws], b_h], writes=[pb], inc=(kc == NKC - 1))
                    P.op("act", lambda e, tb=tb, ws=ws, ps=ps: e.activation(out=vst[ws][:, tb, :], in_=ps[:], func=AF.Copy), reads=[pb], writes=[b_vst[ws]])
                P.dma("sp", lambda e, ws=ws, cg=cg, t0=t0: e.dma_start(
                    out=scr["KV"][t0:t0 + TT, cg * 512:(cg + 1) * 512].rearrange("(tb p) c -> p tb c", p=128), in_=vst[ws][:]), reads=[b_vst[ws]])
            ps, pb = ps_next(C)
            for kc in range(NKC):
                P.op("pe", lambda e, kc=kc, ps=ps: e.matmul(ps[0:48, :], lhsT=wgt[:, kc, :], rhs=h_t[:, kc, :], start=(kc == 0), stop=(kc == NKC - 1)),
                     reads=[b_wgt, b_h], writes=[pb], inc=(kc == NKC - 1))
            P.op("act", lambda e, ps=ps: e.activation(out=gst[:], in_=ps[0:48, :], func=AF.Sigmoid), reads=[pb], writes=[b_gst])
            P.dma("sp", lambda e, t0=t0: e.dma_start(out=scr["gT"][:, t0:t0 + TT], in_=gst[:]), reads=[b_gst])
        C.nrot = 8
        P.phase_end()


def nsa_B(P, C, S, scr, consts, pos_k, pos_v, w_ck, w_cv, g_k, dbg=None):
    nc = P.nc
    NB = S // 128
    NCMP = (S - 32) // 16 + 1
    NNC = (NCMP + 127) // 128
    with ExitStack() as st:
        sb = lambda n, s, d: st.enter_context(nc.sbuf_tensor(f"nb_{n}", s, d))
        cb = Buf("consts")
        identb = sb("identb", [128, 128], BF16)
        P.dma("sp", lambda e: e.dma_start(out=identb[:], in_=consts["identb"]), writes=[cb])
        negtri4 = sb("negtri4", [128, 512], BF16)
        P.dma("sp", lambda e: e.dma_start(out=negtri4[:], in_=consts["negtri4"]), writes=[cb])
        negle4 = sb("negle4", [128, 512], BF16)
        P.dma("sp", lambda e: e.dma_start(out=negle4[:], in_=consts["negle4"]), writes=[cb])
        emat = sb("emat", [128, S], BF16)
        P.dma("sp", lambda e: e.dma_start(out=emat[:], in_=consts["emat"][:, 0:S]), writes=[cb])
        cover = sb("cover", [128, NNC, 128], F32)
        P.dma("sp", lambda e: e.dma_start(out=cover[:], in_=consts["cover"][0:NNC * 128, :].rearrange("(c p) j -> p c j", p=128)), writes=[cb])
        wck = sb("wck", [64, 32, 64], BF16); wcv = sb("wcv", [64, 32, 64], BF16)
        P.dma("pool", lambda e: e.dma_start(out=wck[:], in_=w_ck.rearrange("l d e -> d l e")), writes=[cb])
        P.dma("pool", lambda e: e.dma_start(out=wcv[:], in_=w_cv.rearrange("l d e -> d l e")), writes=[cb])
        pkT = sb("pkT", [64, 32], BF16); pvT = sb("pvT", [64, 32], BF16)
        P.dma("pool", lambda e: e.dma_start(out=pkT[:], in_=pos_k.rearrange("l d -> d l"), allow_slow_non_contiguous=True), writes=[cb])
        P.dma("pool", lambda e: e.dma_start(out=pvT[:], in_=pos_v.rearrange("l d -> d l"), allow_slow_non_contiguous=True), writes=[cb])
        gk = sb("gk", [64, 1], F32)
        P.dma("sp", lambda e: e.dma_start(out=gk[:], in_=g_k.rearrange("(p o) -> p o", o=1), allow_slow_non_contiguous=True), writes=[cb])
        onesb = sb("onesb", [1, 128], BF16)
        P.op("pool", lambda e: e.memset(onesb[:], 1.0), writes=[cb])
        bk = sb("bk", [64, 1], F32); bvrow = sb("bvrow", [1, 64], BF16); b_bias = Buf("bias")
        ps, pb = ps_next(C)
        for l in range(32):
            P.op("pe", lambda e, l=l, ps=ps: e.matmul(ps[0:64, 0:1], lhsT=wck[:, l, :], rhs=pkT[:, l:l + 1], start=(l == 0), stop=(l == 31)),
                 reads=[cb], writes=[pb], inc=(l == 31))
        P.op("dve", lambda e, ps=ps: e.tensor_copy(out=bk[:], in_=ps[0:64, 0:1]), reads=[pb], writes=[b_bias])
        ps, pb = ps_next(C)
        for l in range(32):
            P.op("pe", lambda e, l=l, ps=ps: e.matmul(ps[0:1, 0:64], lhsT=pvT[:, l:l + 1], rhs=wcv[:, l, :], start=(l == 0), stop=(l == 31)),
                 reads=[cb], writes=[pb], inc=(l == 31))
        P.op("dve", lambda e, ps=ps: e.tensor_copy(out=bvrow[:], in_=ps[0:1, 0:64]), reads=[pb], writes=[b_bias])
        kcT = sb("kcT", [64, S], BF16); vcT = sb("vcT", [64, S], BF16); b_kv = Buf("kv")
        ksT = sb("ksT", [64, S], BF16); kwT = sb("kwT", [64, S], BF16)
        vsa = sb("vsa", [128, NB, 65], BF16); vwa = sb("vwa", [128, NB, 65], BF16); vca = sb("vca", [128, NNC, 65], BF16); b_vca = Buf("vca")
        P.op("pool", lambda e: e.memset(vsa[:, :, 64:65], 1.0), writes=[b_kv])
        P.op("pool", lambda e: e.memset(vwa[:, :, 64:65], 1.0), writes=[b_kv])
        P.op("pool", lambda e: e.memset(vca[:], 0.0), writes=[b_vca])
        P.op("pool", lambda e: e.memset(vca[:, :, 64:65], 1.0), writes=[b_vca])
        kcm = sb("kcm", [64, NNC * 128], BF16); b_kcm = Buf("kcm")
        P.op("pool", lambda e: e.memset(kcm[:], 0.0), writes=[b_kcm])
        thr = sb("thr", [128, 1], F32)
        sqc = sb("sqc", [64, 512], F32); rrc = sb("rrc", [64, 512], F32); kcf = sb("kcf", [64, 512], F32); b_c1 = Buf("c1")
        qblk = [sb(f"qblk{i}", [64, 4, 128], BF16) for i in range(2)]; b_qblk = [Buf("qb0"), Buf("qb1")]
        cm4 = [sb(f"cm4{i}", [128, NNC, 4, 128], BF16) for i in range(2)]; b_cm4 = [Buf("cm0"), Buf("cm1")]
        alw = [sb(f"alw{i}", [128, 128], F32) for i in range(2)]; adc = [sb(f"adc{i}", [128, 128], F32) for i in range(2)]; b_al = [Buf("al0"), Buf("al1")]
        grow = [sb(f"grow{i}", [65, 3, 4, 128], F32) for i in range(2)]; b_grow = [Buf("gr0"), Buf("gr1")]
        pcf = [sb(f"pcf{i}", [128, 512], F32) for i in range(NNC)]; b_pcf = [Buf(f"pcf{i}") for i in range(NNC)]
        pcb = [sb(f"pcb{i}", [128, 512], BF16) for i in range(2)]; b_pcb = [Buf("pcb0"), Buf("pcb1")]
        osb = [sb(f"osb{i}", [65, 512], F32) for i in range(4)]; b_osb = [Buf(f"osb{i}") for i in range(4)]
        rec = [sb(f"rec{i}", [65, 512], F32) for i in range(4)]; b_rec = [Buf(f"rec{i}") for i in range(4)]
        impf = sb("impf", [128, 128], F32); imp2 = sb("imp2", [128, 128], F32); mx8 = sb("mx8", [128, 8], F32); mx8b = sb("mx8b", [128, 8], F32); b_imp = Buf("imp")
        mbq = sb("mbq", [128, 128], BF16); b_mbq = Buf("mbq")
        mbT4s = [sb(f"mbT4{i}", [128, 4, 128], BF16) for i in range(2)]; b_mbTs = [Buf("mbT0"), Buf("mbT1")]
        pt = [sb(f"pt{i}", [128, 512], BF16) for i in range(4)]; b_pt = [Buf(f"pt{i}") for i in range(4)]
        oacc = sb("oacc", [64, 512], F32); otmp = sb("otmp", [64, 512], F32); b_oacc = Buf("oacc")
        ost = [sb(f"ost{i}", [64, 512], BF16) for i in range(2)]; b_ost = [Buf("ost0"), Buf("ost1")]
        nptl = [0]
        C.nrot = 4
        for g in range(NG):
            P.dma("sp", lambda e, g=g: e.dma_start(out=kcT[:], in_=scr["kT3"][g * 64:(g + 1) * 64, :]), writes=[b_kv])
            P.dma("sp", lambda e, g=g: e.dma_start(out=ksT[:], in_=scr["kT3"][256 + g * 64:256 + (g + 1) * 64, :]), writes=[b_kv])
            P.dma("sp", lambda e, g=g: e.dma_start(out=kwT[:], in_=scr["kT3"][512 + g * 64:512 + (g + 1) * 64, :]), writes=[b_kv])
            P.dma("sp", lambda e, g=g: e.dma_start(out=vcT[:], in_=scr["vcT"][g * 64:(g + 1) * 64, :]), writes=[b_kv])
            P.dma("sp", lambda e, g=g: e.dma_start(out=vsa[:, :, 0:64], in_=scr["KV"][:, 768 + g * 64:768 + (g + 1) * 64].rearrange("(kb p) d -> p kb d", p=128)), writes=[b_kv])
            P.dma("sp", lambda e, g=g: e.dma_start(out=vwa[:, :, 0:64], in_=scr["KV"][:, 1280 + g * 64:1280 + (g + 1) * 64].rearrange("(kb p) d -> p kb d", p=128)), writes=[b_kv])
            for c0 in range(0, NCMP, 512):
                nn = min(512, NCMP - c0)
                ps, pb = ps_next(C)
                for l in range(32):
                    P.op("pe", lambda e, l=l, ps=ps, c0=c0, nn=nn: e.matmul(ps[0:64, 0:nn], lhsT=wck[:, l, :], rhs=kcT[:, c0 * 16 + l:c0 * 16 + l + (nn - 1) * 16 + 1:16],
                                                                         start=(l == 0), stop=(l == 31)), reads=[cb, b_kv], writes=[pb], inc=(l == 31))
                P.op("dve", lambda e, ps=ps, nn=nn: e.tensor_scalar(out=kcf[:, 0:nn], in0=ps[0:64, 0:nn], scalar1=bk[:, 0:1], scalar2=None, op0=ALU.add),
                     reads=[pb, b_bias], writes=[b_c1])
                P.op("act", lambda e, nn=nn: e.activation(out=sqc[:, 0:nn], in_=kcf[:, 0:nn], func=AF.Square), reads=[b_c1], writes=[b_c1])
                ps2, pb2 = ps_next(C)
                P.op("pe", lambda e, ps2=ps2, nn=nn: e.matmul(ps2[0:64, 0:nn], lhsT=C.ones_f[0:64, 0:64], rhs=sqc[:, 0:nn], start=True, stop=True),
                     reads=[b_c1, C.b_ones_f], writes=[pb2])
                P.op("dve", lambda e, ps2=ps2, nn=nn: e.tensor_scalar(out=rrc[:, 0:nn], in0=ps2[0:64, 0:nn], scalar1=1.0 / 64, scalar2=1e-6, op0=ALU.mult, op1=ALU.add),
                     reads=[pb2], writes=[b_c1])
                P.op("act", lambda e, nn=nn: e.activation(out=rrc[:, 0:nn], in_=rrc[:, 0:nn], func=AF.Sqrt), reads=[b_c1], writes=[b_c1])
                P.op("dve", lambda e, nn=nn: e.reciprocal(out=rrc[:, 0:nn], in_=rrc[:, 0:nn]), reads=[b_c1], writes=[b_c1])
                P.op("dve", lambda e, nn=nn, c0=c0: e.scalar_tensor_tensor(out=kcm[:, c0:c0 + nn], in0=kcf[:, 0:nn], scalar=gk[:, 0:1], in1=rrc[:, 0:nn], op0=ALU.mult, op1=ALU.mult),
                     reads=[b_c1, cb], writes=[b_kcm])
            for nch in range(NNC):
                n0 = nch * 128
                nn = min(128, NCMP - n0)
                ps, pb = ps_next(C)
                for l in range(32):
                    P.op("pe", lambda e, l=l, ps=ps, n0=n0, nn=nn: e.matmul(ps[0:nn, 0:64], lhsT=vcT[:, n0 * 16 + l:n0 * 16 + l + (nn - 1) * 16 + 1:16], rhs=wcv[:, l, :],
                                                                         start=(l == 0), stop=False), reads=[cb, b_kv], writes=[pb], inc=False)
                P.op("pe", lambda e, ps=ps, nn=nn: e.matmul(ps[0:nn, 0:64], lhsT=onesb[0:1, 0:nn], rhs=bvrow[:], start=False, stop=True), reads=[cb, b_bias], writes=[pb])
                P.op("act", lambda e, ps=ps, nn=nn, nch=nch: e.activation(out=vca[0:nn, nch, 0:64], in_=ps[0:nn, 0:64], func=AF.Copy), reads=[pb], writes=[b_vca])
            if NCMP % 128:
                pass
            def cmp_stage(qb):
                    t0 = qb * 128
                    qi = qb % 2
                    o0 = 3 * qi
                    P.dma("sp", lambda e, g=g, qi=qi, t0=t0: e.dma_start(out=qblk[qi][:], in_=scr["qT"][g * 256:(g + 1) * 256, t0:t0 + 128].rearrange("(p d) t -> d p t", d=64)),
                          writes=[b_qblk[qi]])
                    ncn = min(NNC, (8 * qb + 6) // 128 + 1)
                    for p4 in range(4):
                        P.dma("sp", lambda e, qi=qi, t0=t0, p4=p4, ncn=ncn: e.dma_start(
                            out=cm4[qi][:, 0:ncn, p4, :], in_=consts["cmask"][0:ncn * 128, t0:t0 + 128].rearrange("(c p) t -> p c t", p=128)), writes=[b_cm4[qi]])
                    P.dma("sp", lambda e, qi=qi, qb=qb: e.dma_start(out=alw[qi][:], in_=consts["allow"][qb]), writes=[b_al[qi]])
                    P.dma("sp", lambda e, qi=qi, qb=qb: e.dma_start(out=adc[qi][:], in_=consts["addc"][qb]), writes=[b_al[qi]])
                    P.dma("sp", lambda e, qi=qi, g=g, t0=t0: e.dma_start(out=grow[qi][64:65, :, :, :],
                                                                         in_=scr["gT"].rearrange("(h c) t -> c h t", c=3)[:, g * 4:(g + 1) * 4, t0:t0 + 128].rearrange("(o c) h t -> o c h t", o=1)),
                          writes=[b_grow[qi]])
                    qr = qblk[qi][:].rearrange("d p t -> d (p t)")
                    yield
                    poc, pocb = C.psum[6], C.psb[6]
                    for nch in range(ncn):
                        ps, pb = ps_next(C)
                        if ps is poc:
                            ps, pb = ps_next(C)
                        P.op("pe", lambda e, nch=nch, ps=ps, qr=qr: e.matmul(ps[:], lhsT=kcm[:, nch * 128:(nch + 1) * 128], rhs=qr, start=True, stop=False),
                             reads=[b_kcm, b_qblk[qi]], writes=[pb], inc=False)
                        P.op("pe", lambda e, nch=nch, ps=ps, qi=qi: e.matmul(ps[:], lhsT=identb[:], rhs=cm4[qi][:, nch, :, :].rearrange("p a b -> p (a b)"), start=False, stop=True),
                             reads=[cb, b_cm4[qi]], writes=[pb])
                        P.op("act", lambda e, nch=nch, ps=ps: e.activation(out=pcf[nch][:], in_=ps[:], func=AF.Exp, scale=0.125), reads=[pb], writes=[b_pcf[nch]])
                        bi = nch % 2
                        P.op("pool", lambda e, nch=nch, bi=bi: e.tensor_copy(out=pcb[bi][:], in_=pcf[nch][:]), reads=[b_pcf[nch]], writes=[b_pcb[bi]])
                        P.op("pe", lambda e, nch=nch, bi=bi, poc=poc, ncn=ncn: e.matmul(poc[0:65, :], lhsT=vca[:, nch, :], rhs=pcb[bi][:], start=(nch == 0), stop=(nch == ncn - 1)),
                             reads=[b_vca, b_pcb[bi]], writes=[pocb])
                        yield
                    P.op("act", lambda e, poc=poc: e.activation(out=osb[o0][:], in_=poc[0:65, :], func=AF.Copy), reads=[pocb], writes=[b_osb[o0]])
                    P.op("dve", lambda e: e.tensor_scalar(out=rec[o0][64:65, :], in0=osb[o0][64:65, :], scalar1=1e-30, scalar2=None, op0=ALU.add), reads=[b_osb[o0]], writes=[b_rec[o0]])
                    P.op("dve", lambda e: e.reciprocal(out=rec[o0][64:65, :], in_=rec[o0][64:65, :]), reads=[b_rec[o0]], writes=[b_rec[o0]])
                    yield
                    pbc, pbcb = ps_next(C)
                    P.op("pe", lambda e, pbc=pbc: e.matmul(pbc[:], lhsT=C.ones_f[64:65, :], rhs=rec[o0][64:65, :], start=True, stop=True), reads=[b_rec[o0], C.b_ones_f], writes=[pbcb])
                    for nch in range(ncn):
                        P.op("dve", lambda e, nch=nch, pbc=pbc: e.tensor_tensor(out=pcf[nch][:], in0=pcf[nch][:], in1=pbc[:], op=ALU.mult), reads=[b_pcf[nch], pbcb], writes=[b_pcf[nch]])
                    yield
                    pim, pimb = C.psum[7], C.psb[7]
                    k = 0
                    for nch in range(ncn):
                        for p4 in range(4):
                            P.op("pe", lambda e, nch=nch, p4=p4, pim=pim, k=k, ncn=ncn: e.matmul(pim[:, 0:128], lhsT=pcf[nch][:, p4 * 128:(p4 + 1) * 128], rhs=cover[:, nch, :],
                                                                                              start=(k == 0), stop=(k == ncn * 4 - 1)),
                                 reads=[b_pcf[nch], cb], writes=[pimb], inc=(k == ncn * 4 - 1))
                            k += 1
                    yield
                    P.op("dve", lambda e, pim=pim, qi=qi: e.tensor_tensor(out=impf[:], in0=pim[:, 0:128], in1=alw[qi][:], op=ALU.mult), reads=[pimb, b_al[qi]], writes=[b_imp])
                    P.op("dve", lambda e, qi=qi: e.tensor_tensor(out=impf[:], in0=impf[:], in1=adc[qi][:], op=ALU.add), reads=[b_imp, b_al[qi]], writes=[b_imp])
                    P.op("dve", lambda e: e.max(out=mx8[:], in_=impf[:]), reads=[b_imp], writes=[b_imp], strict=True)
                    P.op("dve", lambda e: e.match_replace(out=imp2[:], in_to_replace=mx8[:], in_values=impf[:], imm_value=-2.0), reads=[b_imp], writes=[b_imp], strict=True)
                    P.op("dve", lambda e: e.max(out=mx8b[:], in_=imp2[:]), reads=[b_imp], writes=[b_imp], strict=True)
                    P.op("dve", lambda e: e.tensor_reduce(out=thr[:], in_=mx8b[:], axis=AX.X, op=ALU.min), reads=[b_imp], writes=[b_imp], strict=True)
                    P.op("dve", lambda e: e.tensor_scalar(out=imp2[:], in0=impf[:], scalar1=thr[:, 0:1], scalar2=None, op0=ALU.is_ge), reads=[b_imp], writes=[b_imp], strict=True)
                    P.op("dve", lambda e: e.tensor_scalar(out=mbq[:], in0=imp2[:], scalar1=1.0, scalar2=240000.0, op0=ALU.subtract, op1=ALU.mult), reads=[b_imp], writes=[b_mbq], strict=True)
                    if dbg is not None and g == 0 and qb == NB - 1 and "dbg" in scr:
                        P.dma("sp", lambda e: e.dma_start(out=scr["dbg"][:, 0:128], in_=impf[:]), reads=[b_imp])
                        P.dma("sp", lambda e: e.dma_start(out=scr["dbg"][:, 128:256], in_=imp2[:]), reads=[b_imp])
                        P.dma("sp", lambda e: e.dma_start(out=scr["dbg"][:, 256:264], in_=mx8[:]), reads=[b_imp])
                        P.dma("sp", lambda e: e.dma_start(out=scr["dbg"][:, 264:272], in_=mx8b[:]), reads=[b_imp])
                        P.dma("sp", lambda e: e.dma_start(out=scr["dbg"][:, 272:273], in_=thr[:], allow_slow_non_contiguous=True), reads=[b_imp])
                    yield
                    pmt, pmtb = ps_next(C)
                    P.op("pe", lambda e, pmt=pmt: e.matmul(pmt[:, 0:128], lhsT=mbq[:], rhs=identb[:], start=True, stop=True), reads=[b_mbq, cb], writes=[pmtb])
                    for p4 in range(4):
                        P.op("act" if p4 % 2 else "dve", (lambda e, p4=p4, pmt=pmt: e.activation(out=mbT4s[qi][:, p4, :], in_=pmt[:, 0:128], func=AF.Copy)) if p4 % 2 else
                             (lambda e, p4=p4, pmt=pmt: e.tensor_copy(out=mbT4s[qi][:, p4, :], in_=pmt[:, 0:128])), reads=[pmtb], writes=[b_mbTs[qi]])
            def att_stage(qb, gen=None):
                    t0 = qb * 128
                    qi = qb % 2
                    o0 = 3 * qi
                    poc = None
                    qr = qblk[qi][:].rearrange("d p t -> d (p t)")
                    pos_, posb = C.psum[4], C.psb[4]
                    pq = []
                    for kb in range(qb + 1):
                        ps, pb = ps_next(C)
                        while ps is pos_ or ps is poc:
                            ps, pb = ps_next(C)
                        P.op("pe", lambda e, kb=kb, ps=ps, qr=qr: e.matmul(ps[:], lhsT=ksT[:, kb * 128:(kb + 1) * 128], rhs=qr, start=True, stop=False),
                             reads=[b_kv, b_qblk[qi]], writes=[pb], inc=False)
                        if kb == qb:
                            P.op("pe", lambda e, ps=ps: e.matmul(ps[:], lhsT=identb[:], rhs=negtri4[:], start=False, stop=False), reads=[cb], writes=[pb], inc=False)
                        P.op("pe", lambda e, kb=kb, ps=ps: e.matmul(ps[:], lhsT=emat[:, kb * 128:(kb + 1) * 128], rhs=mbT4s[qi][:].rearrange("p a b -> p (a b)"), start=False, stop=True),
                             reads=[cb, b_mbTs[qi]], writes=[pb])
                        pi = nptl[0] % 4; nptl[0] += 1
                        P.op("act", lambda e, ps=ps, pi=pi: e.activation(out=pt[pi][:], in_=ps[:], func=AF.Exp, scale=0.125), reads=[pb], writes=[b_pt[pi]])
                        def pv(kb=kb, pi=pi, pos_=pos_, qb=qb, posb=posb):
                            P.op("pe", lambda e: e.matmul(pos_[0:65, :], lhsT=vsa[:, kb, :], rhs=pt[pi][:], start=(kb == 0), stop=(kb == qb)),
                                 reads=[b_kv, b_pt[pi]], writes=[posb])
                        pq.append(pv)
                        if len(pq) > 2:
                            pq.pop(0)()
                        if gen is not None and kb % 2 == 1:
                            next(gen, None)
                    while pq:
                        pq.pop(0)()
                    pow_, powb = C.psum[5], C.psb[5]
                    while pow_ is pos_ or pow_ is poc:
                        pow_, powb = C.psum[5], C.psb[5]
                    kb0 = max(0, qb - 4)
                    pq = []
                    for kb in range(kb0, qb + 1):
                        ps, pb = ps_next(C)
                        while ps is pos_ or ps is poc or ps is pow_:
                            ps, pb = ps_next(C)
                        msk = negtri4 if kb == qb else (negle4 if kb == qb - 4 else None)
                        P.op("pe", lambda e, kb=kb, ps=ps, qr=qr, msk=msk: e.matmul(ps[:], lhsT=kwT[:, kb * 128:(kb + 1) * 128], rhs=qr, start=True, stop=(msk is None)),
                             reads=[b_kv, b_qblk[qi]], writes=[pb], inc=(msk is None))
                        if msk is not None:
                            P.op("pe", lambda e, ps=ps, msk=msk: e.matmul(ps[:], lhsT=identb[:], rhs=msk[:], start=False, stop=True), reads=[cb], writes=[pb])
                        pi = nptl[0] % 4; nptl[0] += 1
                        P.op("act", lambda e, ps=ps, pi=pi: e.activation(out=pt[pi][:], in_=ps[:], func=AF.Exp, scale=0.125), reads=[pb], writes=[b_pt[pi]])
                        def pv(kb=kb, pi=pi, pow_=pow_, qb=qb, kb0=kb0, powb=powb):
                            P.op("pe", lambda e: e.matmul(pow_[0:65, :], lhsT=vwa[:, kb, :], rhs=pt[pi][:], start=(kb == kb0), stop=(kb == qb)),
                                 reads=[b_kv, b_pt[pi]], writes=[powb])
                        pq.append(pv)
                        if len(pq) > 2:
                            pq.pop(0)()
                        if gen is not None and kb % 2 == 1:
                            next(gen, None)
                    while pq:
                        pq.pop(0)()
                    if gen is not None:
                        for _ in gen:
                            pass
                    for c, (po_, pob_) in enumerate(((None, None), (pos_, posb), (pow_, powb))):
                        ci = o0 if c == 0 else c
                        if c > 0:
                            P.op("act", lambda e, c=c, ci=ci, po_=po_: e.activation(out=osb[ci][:], in_=po_[0:65, :], func=AF.Copy), reads=[pob_], writes=[b_osb[ci]])
                            P.op("dve", lambda e, c=c, ci=ci: e.tensor_scalar(out=rec[ci][64:65, :], in0=osb[ci][64:65, :], scalar1=1e-30, scalar2=None, op0=ALU.add), reads=[b_osb[ci]], writes=[b_rec[ci]])
                            P.op("dve", lambda e, c=c, ci=ci: e.reciprocal(out=rec[ci][64:65, :], in_=rec[ci][64:65, :]), reads=[b_rec[ci]], writes=[b_rec[ci]])
                        P.op("dve", lambda e, c=c, ci=ci, qi=qi: e.tensor_tensor(out=rec[ci][64:65, :], in0=rec[ci][64:65, :], in1=grow[qi][64:65, c, :, :].rearrange("o h t -> o (h t)"), op=ALU.mult),
                             reads=[b_rec[ci], b_grow[qi]], writes=[b_rec[ci]])
                        if dbg is not None and c != dbg:
                            P.op("dve", lambda e, c=c, ci=ci: e.memset(rec[ci][64:65, :], 0.0), reads=[b_rec[ci]], writes=[b_rec[ci]])
                        pbc2, pbc2b = ps_next(C)
                        while pbc2 is pos_ or pbc2 is pow_:
                            pbc2, pbc2b = ps_next(C)
                        P.op("pe", lambda e, c=c, ci=ci, pbc2=pbc2: e.matmul(pbc2[0:64, :], lhsT=C.ones_f[64:65, 0:64], rhs=rec[ci][64:65, :], start=True, stop=True),
                             reads=[b_rec[ci], C.b_ones_f], writes=[pbc2b])
                        if c == 0:
                            P.op("dve", lambda e, pbc2=pbc2: e.tensor_tensor(out=oacc[:], in0=osb[o0][0:64, :], in1=pbc2[0:64, :], op=ALU.mult), reads=[b_osb[o0], pbc2b], writes=[b_oacc])
                        else:
                            P.op("dve", lambda e, c=c, ci=ci, pbc2=pbc2: e.tensor_tensor(out=otmp[:], in0=osb[ci][0:64, :], in1=pbc2[0:64, :], op=ALU.mult), reads=[b_osb[ci], pbc2b], writes=[b_oacc])
                            if c == 1:
                                P.op("dve", lambda e: e.tensor_tensor(out=oacc[:], in0=oacc[:], in1=otmp[:], op=ALU.add), reads=[b_oacc], writes=[b_oacc])
                            else:
                                oi = qb % 2
                                P.op("dve", lambda e, oi=oi: e.tensor_tensor(out=ost[oi][:], in0=oacc[:], in1=otmp[:], op=ALU.add), reads=[b_oacc], writes=[b_ost[oi]])
                                P.dma("sp", lambda e, g=g, t0=t0, oi=oi: e.dma_start(out=scr["oT"][g * 256:(g + 1) * 256, t0:t0 + 128].rearrange("(p d) t -> d p t", d=64),
                                                                                    in_=ost[oi][:].rearrange("d (p t) -> d p t", p=4)), reads=[b_ost[oi]])
            for _ in cmp_stage(0):
                pass
            for qb in range(NB):
                gen = cmp_stage(qb + 1) if qb + 1 < NB else None
                if gen is not None and not NSA_ILV:
                    for _ in gen:
                        pass
                att_stage(qb, gen)
        C.nrot = 8
        P.phase_end()


bf = ml_dtypes.bfloat16
def nsa_consts(S):
    NB = S // 128
    ncmp = (S - 32) // 16 + 1
    nnc = (ncmp + 127) // 128
    inv = (10000.0 ** (-np.arange(0, 64, 2, dtype=np.float32) / 64)).astype(np.float32)
    ang = np.arange(S, dtype=np.float32)[None, :] * inv[:, None]
    j = np.arange(128) % 32
    cosT = np.cos(ang)[j].astype(np.float32); sinT = np.sin(ang)[j].astype(np.float32)
    pm = np.zeros((128, 128), np.float32)
    for d in range(128):
        if (d % 64) < 32: pm[d + 32, d] = -1.0
        else: pm[d - 32, d] = 1.0
    n = np.arange(nnc * 128)[:, None]; t = np.arange(S)[None, :]
    cmask = np.where((16 * n + 31 <= t) & (n < ncmp), 0.0, -240000.0)
    jj = np.arange(128)[None, :]
    cover = ((n < ncmp) & (16 * n < 64 * jj + 64) & (16 * n + 32 > 64 * jj)).astype(np.float32)
    tq = (np.arange(NB)[:, None, None] * 128 + np.arange(128)[None, :, None]); cur = tq // 64
    jb = np.arange(128)[None, None, :]
    forced = (jb == 0) | (jb == cur) | (jb == cur - 1); allowed = jb <= cur
    allow = (allowed & ~forced).astype(np.float32)
    addc = np.where(forced, 1e9, np.where(allowed, 0.0, -1.0)).astype(np.float32)
    emat = (np.arange(128)[:, None] == (np.arange(S)[None, :] // 64)).astype(np.float32)
    s_ = np.arange(128)[:, None]; t_ = np.arange(128)[None, :]
    negtri = np.where(s_ > t_, -240000.0, 0.0); negle = np.where(s_ <= t_, -240000.0, 0.0)
    return {"cos": cosT, "sin": sinT, "pm": pm.astype(bf), "cmask": cmask.astype(bf), "cover": cover, "allow": allow, "addc": addc,
            "emat": emat.astype(bf), "negtri4": np.tile(negtri, (1, 4)).astype(bf), "negle4": np.tile(negle, (1, 4)).astype(bf),
            "identb": np.eye(128, dtype=np.float32).astype(bf)}


def gla_consts():
    s = np.arange(128)[:, None]; c = np.arange(128)[None, :]
    same = (s // 64) == (c // 64)
    m01 = np.ones((128, 2048), np.float32); m01[:, ::64] = 0
    return {"m01": m01.astype(bf), "umat": (same & (s > c)).astype(np.float32), "bcaus": (same & (s <= c)).astype(np.float32).astype(bf),
            "ident": np.eye(128, dtype=np.float32)}


def sb_consts():
    j = np.arange(128)[:, None]; s = np.arange(128)[None, :]
    return {
        "trii": np.where(j >= s, -8.0, 0.0).astype(bf),
        "nones": np.full((128, 128), -8.0, np.float32).astype(bf),
        "strict": np.where(j < s, 1.0, 0.0).astype(bf),
        "negtri_incl": np.where(j >= s, -240000.0, 0.0).astype(bf),
        "identb": np.eye(128, dtype=np.float32).astype(bf),
    }


SEQ = 8192
NCORE = 4


def all_consts():
    cs = {}
    for pre, d in (("n_", nsa_consts(SEQ)), ("g_", gla_consts()), ("s_", sb_consts())):
        for k, v in d.items():
            cs[pre + k] = v
    cs["f_tri"] = (np.tril(np.ones((128, 128), np.float32), -1) * -240000.0).astype(bf)
    cs["f_ident"] = np.eye(128, dtype=np.float32)
    cs["f_identb"] = np.eye(128, dtype=np.float32).astype(bf)
    return cs


def build_program(inputs, consts):
    S = SEQ
    nc = bass.Bass("TRN2", target_bir_lowering=False)
    W = {}
    for k, v in inputs.items():
        if k == "x":
            continue
        W[k] = nc.dram_tensor(k, list(v.shape), F32, kind="ExternalInput").ap()
    xT = nc.dram_tensor("xT", [D, S], F32, kind="ExternalInput").ap()
    yT = nc.dram_tensor("yT", [D, S], F32, kind="ExternalOutput").ap()
    cs = {k: nc.dram_tensor("c_" + k, list(v.shape), BF16 if v.dtype == bf else F32, kind="ExternalInput").ap() for k, v in consts.items()}
    sub = lambda pre: {k[len(pre):]: v for k, v in cs.items() if k.startswith(pre)}
    dt = lambda n, s, d: nc.dram_tensor("scr_" + n, s, d).ap()
    oT = dt("oT", [D, S], BF16)
    sg = dt("sg", [D, S], BF16)
    qT = dt("qT", [D, S], BF16)
    kT = dt("kT", [D, S], BF16)
    V = dt("V", [S, D], BF16)
    KV = dt("KV", [S, 1536], BF16)
    with ExitStack() as st:
        P = Prog(nc, st)
        C = Common(P, None)
        ng = W["norm_g"]
        ffn_phase(P, C, xT, yT, W["ffn1_w_gate"][0], W["ffn1_w_up"][0], W["ffn1_w_down"][0], ng[0, 0], S, "l0a")
        scr = {"qT": qT, "kT": kT, "V": V, "ls": dt("ls", [16, S], F32), "sg": sg, "crow": dt("crow", [3, 16, S], BF16), "oT": oT}
        fox_A(P, C, yT, S, W["fox_w_in"][0], ng[0, 1], W["fox_b_f"][0], W["fox_g_q"][0], W["fox_g_k"][0], scr)
        fox_B(P, C, S, scr, {"tri": cs["f_tri"], "ident": cs["f_ident"], "identb": cs["f_identb"]})
        mix_C(P, C, yT, S, W["fox_w_out"][0], scr, "sg")
        ffn_phase(P, C, yT, yT, W["ffn2_w_gate"][0], W["ffn2_w_up"][0], W["ffn2_w_down"][0], ng[0, 2], S, "l0b")
        ffn_phase(P, C, yT, yT, W["ffn1_w_gate"][1], W["ffn1_w_up"][1], W["ffn1_w_down"][1], ng[1, 0], S, "l1a")
        scr = {"qT": qT, "kT3": dt("kT3", [768, S], BF16), "vcT": dt("vcT", [256, S], BF16), "KV": KV, "gT": dt("gT", [48, S], F32), "oT": oT}
        ncs = sub("n_")
        nsa_A(P, C, yT, S, W["nsa_w_in"][0], ng[1, 1], W["nsa_g_q"][0], W["nsa_g_k"][0], scr, ncs)
        nsa_B(P, C, S, scr, ncs, W["nsa_cmp_pos_k"][0], W["nsa_cmp_pos_v"][0], W["nsa_w_cmp_k"][0], W["nsa_w_cmp_v"][0], W["nsa_g_k"][0])
        mix_C(P, C, yT, S, W["nsa_w_out"][0], scr, None)
        ffn_phase(P, C, yT, yT, W["ffn2_w_gate"][1], W["ffn2_w_up"][1], W["ffn2_w_down"][1], ng[1, 2], S, "l1b")
        ffn_phase(P, C, yT, yT, W["ffn1_w_gate"][2], W["ffn1_w_up"][2], W["ffn1_w_down"][2], ng[2, 0], S, "l2a")
        scr = {"qT": qT[0:512], "kT": kT[0:512], "KV": KV, "laT": dt("laT", [512, S], F32), "laK": dt("laK", [S, 512], F32), "sg": sg, "oT": oT}
        gcs = sub("g_")
        gla_A(P, C, yT, S, W["gla_w_in"][0], ng[2, 1], W["gla_w_gate_up"][0], W["gla_b_gate"][0], scr, gcs)
        gla_B(P, C, S, scr, gcs)
        gla_C(P, C, yT, S, W["gla_w_out"][0], W["gla_g_out"][0], scr)
        ffn_phase(P, C, yT, yT, W["ffn2_w_gate"][2], W["ffn2_w_up"][2], W["ffn2_w_down"][2], ng[2, 2], S, "l2b")
        ffn_phase(P, C, yT, yT, W["ffn1_w_gate"][3], W["ffn1_w_up"][3], W["ffn1_w_down"][3], ng[3, 0], S, "l3a")
        scr = {"qT": qT, "kT": kT, "V": V, "oT": oT}
        qkv_A(P, C, yT, S, W["sb_w_in"][0], ng[3, 1], scr)
        sb_B(P, C, S, scr, sub("s_"))
        mix_C(P, C, yT, S, W["sb_w_out"][0], scr, None)
        ffn_phase(P, C, yT, yT, W["ffn2_w_gate"][3], W["ffn2_w_up"][3], W["ffn2_w_down"][3], ng[3, 2], S, "l3b")
        P.finish()
        P.emit()
    return nc


def kernel(**inputs):
    inputs = {k: np.asarray(v) for k, v in inputs.items()}
    x = inputs["x"]
    consts = all_consts()
    nc = build_program(inputs, consts)
    base = {k: np.ascontiguousarray(v, dtype=np.float32) for k, v in inputs.items() if k != "x"}
    for k, v in consts.items():
        base["c_" + k] = v
    in_maps = []
    for b in range(NCORE):
        m = dict(base)
        m["xT"] = np.ascontiguousarray(x[b].T)
        in_maps.append(m)
    res = run_bass_kernel_spmd(nc, in_maps, core_ids=list(range(NCORE)))
    out = np.stack([np.ascontiguousarray(res.results[b]["yT"].T) for b in range(NCORE)], axis=0)
    return out.astype(np.float32)
```
